# Optimizing a Trainium2 kernel written in Bass

```python
import math
import jax, jax.numpy as jnp
from jax import lax
import numpy as np

D_MODEL = 1024
BATCH = 4
SEQ = 4096
DEPTH = 1

D_INNER = D_MODEL
M_WIDTH = D_INNER // 2
M_HEADS = 4
M_HEAD_DIM = M_WIDTH // M_HEADS
H_WIDTH = D_INNER - M_WIDTH
H_GROUPS = 8
HYENA_ORDER = 2
N_DIR = 2
CHUNK = 128
FILTER_EMB = 33
FILTER_HIDDEN = 64
N_SIN = 3
DECAY_TARGET = 1e-2
FAST_DECAY_PCT = 0.3
SLOW_DECAY_PCT = 1.5
D_FF = 4 * D_MODEL
N_GATE = 2 * N_DIR * M_HEADS
EPS = 1e-6

OFF_MQ = 0
OFF_MK = OFF_MQ + M_WIDTH
OFF_HV = OFF_MK + M_WIDTH
OFF_HX1 = OFF_HV + H_WIDTH
OFF_HX2 = OFF_HX1 + H_WIDTH
N_CONV = OFF_HX2 + H_WIDTH
OFF_MV = N_CONV
OFF_MO = OFF_MV + M_WIDTH
OFF_GATE = OFF_MO + M_WIDTH
N_IN = OFF_GATE + N_GATE

kernel_name = "hybrid_mlstm_hyena_sandwich_block"


def rms_norm(x, w):
    xf = x.astype(jnp.float32)
    y = xf * lax.rsqrt(jnp.mean(xf * xf, axis=-1, keepdims=True) + EPS)
    return (y * w.astype(jnp.float32)).astype(x.dtype)


def short_conv3(u, w, b):
    up = jnp.pad(u, ((0, 0), (1, 1), (0, 0)))
    return up[:, :-2] * w[0] + up[:, 1:-1] * w[1] + up[:, 2:] * w[2] + b


def mlstm_chunkwise(q, k, v, log_i, log_f):
    B, H, S, D = q.shape
    L = CHUNK
    NC = S // L
    qc = q.reshape(B, H, NC, L, D)
    kc = k.reshape(B, H, NC, L, D)
    vc = v.reshape(B, H, NC, L, D)
    li = log_i.reshape(B, H, NC, L)
    bcum = jnp.cumsum(log_f.reshape(B, H, NC, L), axis=-1)
    g = bcum[..., -1]
    a = g[..., None] - bcum + li

    def step(carry, xs):
        C, n, m = carry
        k_c, v_c, a_c, g_c = xs
        m_new = jnp.maximum(g_c + m, jnp.max(a_c, axis=-1))
        decay = jnp.exp(g_c + m - m_new)
        w = jnp.exp(a_c - m_new[..., None])
        C_new = decay[..., None, None] * C + jnp.einsum('bhl,bhld,bhle->bhde', w, k_c, v_c)
        n_new = decay[..., None] * n + jnp.einsum('bhl,bhld->bhd', w, k_c)
        return (C_new, n_new, m_new), (C, n, m)

    init = (jnp.zeros((B, H, D, D), jnp.float32),
            jnp.zeros((B, H, D), jnp.float32),
            jnp.zeros((B, H), jnp.float32))
    xs = (jnp.moveaxis(kc, 2, 0), jnp.moveaxis(vc, 2, 0),
          jnp.moveaxis(a, 2, 0), jnp.moveaxis(g, 2, 0))
    _, (C_s, n_s, m_s) = lax.scan(step, init, xs)
    C_s = jnp.moveaxis(C_s, 0, 2)
    n_s = jnp.moveaxis(n_s, 0, 2)
    m_s = jnp.moveaxis(m_s, 0, 2)

    causal = jnp.tril(jnp.ones((L, L), dtype=bool))
    dmat = bcum[..., :, None] - bcum[..., None, :] + li[..., None, :]
    dmat = jnp.where(causal, dmat, -jnp.inf)
    inter_log = bcum + m_s[..., None]
    m_t = jnp.maximum(inter_log, jnp.max(dmat, axis=-1))
    wts = jnp.exp(dmat - m_t[..., None])
    s_qk = jnp.einsum('bhctd,bhcsd->bhcts', qc, kc) * wts
    inter_scale = jnp.exp(inter_log - m_t)
    num = (jnp.einsum('bhcts,bhcse->bhcte', s_qk, vc)
           + inter_scale[..., None] * jnp.einsum('bhctd,bhcde->bhcte', qc, C_s))
    den = s_qk.sum(-1) + inter_scale * jnp.einsum('bhctd,bhcd->bhct', qc, n_s)
    h = num / jnp.maximum(jnp.abs(den), jnp.exp(-m_t))[..., None]
    return h.reshape(B, H, S, D)


def hyena_filters(L, w1, b1, w2, b2, w3, b3, w4, freq):
    bands = (FILTER_EMB - 1) // 2
    t = jnp.linspace(0.0, 1.0, L, dtype=jnp.float32)[:, None]
    t_resc = jnp.arange(L, dtype=jnp.float32)[:, None]
    ang = 2.0 * math.pi * t_resc / L
    f = jnp.linspace(1e-4, bands - 1, bands, dtype=jnp.float32)[None, :]
    z = jnp.concatenate([t, jnp.cos(f * ang), -jnp.sin(f * ang)], axis=-1)
    fr = freq.astype(jnp.float32)
    h = jnp.sin(fr[0] * (z @ w1.astype(jnp.float32) + b1.astype(jnp.float32)))
    h = jnp.sin(fr[1] * (h @ w2.astype(jnp.float32) + b2.astype(jnp.float32)))
    h = jnp.sin(fr[2] * (h @ w3.astype(jnp.float32) + b3.astype(jnp.float32)))
    h = (h @ w4.astype(jnp.float32)).reshape(L, N_DIR, HYENA_ORDER, H_WIDTH)
    max_decay = math.log(DECAY_TARGET) / FAST_DECAY_PCT
    min_decay = math.log(DECAY_TARGET) / SLOW_DECAY_PCT
    deltas = jnp.linspace(min_decay, max_decay, H_WIDTH, dtype=jnp.float32)
    window = jnp.exp(-t[:, :, None, None] * jnp.abs(deltas))
    return h * window


def two_sided_fft_conv(u, k_fwd, k_bwd, bias):
    S = u.shape[1]
    kern = jnp.concatenate([k_fwd, jnp.zeros_like(k_fwd[:1]), k_bwd[:0:-1]], axis=0)
    K = jnp.fft.rfft(kern, axis=0)
    U = jnp.fft.rfft(u, n=2 * S, axis=1)
    y = jnp.fft.irfft(U * K[None], n=2 * S, axis=1)[:, :S]
    return y + u * bias


def group_rms(u, n_groups, w):
    B, S, C = u.shape
    ug = u.reshape(B, S, n_groups, C // n_groups)
    ug = ug * lax.rsqrt(jnp.mean(ug * ug, axis=-1, keepdims=True) + EPS)
    return ug.reshape(B, S, C) * w.astype(jnp.float32)


def setup_inputs(seed: int = 0) -> dict:
    key = jax.random.key(seed)
    ks = jax.random.split(key, 32)
    f32 = jnp.float32
    nrm = lambda k, shape, scale: jax.random.normal(k, shape, f32) * scale
    gain = lambda k, n: 1.0 + 0.05 * jax.random.normal(k, (n,), f32)
    i_bias = nrm(ks[6], (M_HEADS,), 0.1)
    f_bias = jnp.linspace(3.0, 6.0, M_HEADS, dtype=f32) + nrm(ks[7], (M_HEADS,), 0.1)
    i_bias_b = nrm(ks[8], (M_HEADS,), 0.1)
    f_bias_b = jnp.linspace(3.0, 6.0, M_HEADS, dtype=f32) + nrm(ks[9], (M_HEADS,), 0.1)
    return {
        "x": jax.random.normal(ks[0], (BATCH, SEQ, D_MODEL), f32),
        "norm_mix_pre": gain(ks[1], D_MODEL),
        "norm_mix_post": gain(ks[2], D_MODEL),
        "norm_mlp_pre": gain(ks[3], D_MODEL),
        "norm_mlp_post": gain(ks[4], D_MODEL),
        "w_in": nrm(ks[5], (D_MODEL, N_IN), D_MODEL ** -0.5),
        "b_gates": jnp.concatenate([i_bias, f_bias, i_bias_b, f_bias_b]),
        "conv_w": nrm(ks[10], (3, N_CONV), 3 ** -0.5),
        "conv_b": nrm(ks[11], (N_CONV,), 0.02),
        "mlstm_norm_w": gain(ks[12], M_WIDTH),
        "hyena_norm_w": gain(ks[13], H_WIDTH),
        "filt_w1": nrm(ks[14], (FILTER_EMB, FILTER_HIDDEN), FILTER_EMB ** -0.5),
        "filt_b1": nrm(ks[15], (FILTER_HIDDEN,), 0.02),
        "filt_w2": nrm(ks[16], (FILTER_HIDDEN, FILTER_HIDDEN), FILTER_HIDDEN ** -0.5),
        "filt_b2": nrm(ks[17], (FILTER_HIDDEN,), 0.02),
        "filt_w3": nrm(ks[18], (FILTER_HIDDEN, FILTER_HIDDEN), FILTER_HIDDEN ** -0.5),
        "filt_b3": nrm(ks[19], (FILTER_HIDDEN,), 0.02),
        "filt_w4": nrm(ks[20], (FILTER_HIDDEN, N_DIR * HYENA_ORDER * H_WIDTH), FILTER_HIDDEN ** -0.5),
        "filt_freq": 1.0 + 0.05 * jax.random.normal(ks[21], (N_SIN, FILTER_HIDDEN), f32),
        "filt_bias": nrm(ks[22], (HYENA_ORDER, H_WIDTH), 0.1),
        "w_out": nrm(ks[23], (D_INNER, D_MODEL), D_INNER ** -0.5),
        "w_mlp_in": nrm(ks[24], (D_MODEL, D_FF), D_MODEL ** -0.5),
        "w_mlp_out": nrm(ks[25], (D_FF, D_MODEL), D_FF ** -0.5),
    }


def reference(x, norm_mix_pre, norm_mix_post, norm_mlp_pre, norm_mlp_post, w_in, b_gates,
              conv_w, conv_b, mlstm_norm_w, hyena_norm_w, filt_w1, filt_b1, filt_w2, filt_b2,
              filt_w3, filt_b3, filt_w4, filt_freq, filt_bias, w_out, w_mlp_in, w_mlp_out):
    B, S, _ = x.shape
    f32 = jnp.float32
    for _layer in range(DEPTH):
        hn = rms_norm(x, norm_mix_pre)
        proj = (hn @ w_in).astype(f32)
        cv = short_conv3(proj[..., :N_CONV], conv_w.astype(f32), conv_b.astype(f32))

        def heads(u):
            return u.reshape(B, S, M_HEADS, M_HEAD_DIM).transpose(0, 2, 1, 3)
        q = heads(jax.nn.silu(cv[..., OFF_MQ:OFF_MQ + M_WIDTH]))
        k = heads(jax.nn.silu(cv[..., OFF_MK:OFF_MK + M_WIDTH])) * (M_HEAD_DIM ** -0.5)
        v = heads(proj[..., OFF_MV:OFF_MV + M_WIDTH])
        o_gate = jax.nn.sigmoid(proj[..., OFF_MO:OFF_MO + M_WIDTH])
        gates = (proj[..., OFF_GATE:OFF_GATE + N_GATE] + b_gates.astype(f32)).transpose(0, 2, 1)
        gates = gates.reshape(B, 2 * N_DIR, M_HEADS, S)
        h_fwd = mlstm_chunkwise(q, k, v, gates[:, 0], jax.nn.log_sigmoid(gates[:, 1]))
        flip = lambda u: jnp.flip(u, axis=2)
        h_bwd = flip(mlstm_chunkwise(flip(q), flip(k), flip(v), flip(gates[:, 2]),
                                     flip(jax.nn.log_sigmoid(gates[:, 3]))))
        h_m = (h_fwd + h_bwd).transpose(0, 2, 1, 3).reshape(B, S, M_WIDTH) * o_gate
        y_m = group_rms(h_m, M_HEADS, mlstm_norm_w)

        filt = hyena_filters(S, filt_w1, filt_b1, filt_w2, filt_b2, filt_w3, filt_b3,
                             filt_w4, filt_freq)
        fb = filt_bias.astype(f32)
        z = cv[..., OFF_HV:OFF_HV + H_WIDTH]
        z = cv[..., OFF_HX1:OFF_HX1 + H_WIDTH] * two_sided_fft_conv(z, filt[:, 0, 0], filt[:, 1, 0], fb[0])
        z = cv[..., OFF_HX2:OFF_HX2 + H_WIDTH] * two_sided_fft_conv(z, filt[:, 0, 1], filt[:, 1, 1], fb[1])
        y_h = group_rms(z, H_GROUPS, hyena_norm_w)

        mix = jnp.concatenate([y_m, y_h], axis=-1).astype(x.dtype) @ w_out
        x = x + rms_norm(mix, norm_mix_post)

        hm = rms_norm(x, norm_mlp_pre)
        ff = jnp.square(jax.nn.relu(hm @ w_mlp_in)) @ w_mlp_out
        x = x + rms_norm(ff, norm_mlp_post)
    return x
```

```python
import contextlib
import math

import ml_dtypes
import numpy as np

import concourse.bass as bass
import concourse.mybir as mybir
from concourse.bass_utils import run_bass_kernel_spmd

F32 = mybir.dt.float32
BF16 = mybir.dt.bfloat16
AF = mybir.ActivationFunctionType
ALU = mybir.AluOpType
AX = mybir.AxisListType
BF = ml_dtypes.bfloat16

D = 1024
SEQ = 4096
NCH = 32
NH = 4
DFF = 4096
N_IN = 3600
OFF_MQ, OFF_MK, OFF_HV, OFF_HX1, OFF_HX2, OFF_MV, OFF_MO, OFF_GATE = 0, 512, 1024, 1536, 2048, 2560, 3072, 3584
EPS = 1e-6
NEG = -30000.0
NF, NF1, NF2, HK = 8192, 256, 32, 128
CB, NQ = 128, 32
TWO_PI = 2 * math.pi
MAGIC = 12582912.0
ENGS = ("sync", "scalar", "vector", "gpsimd", "tensor")
DMA_ENGS = ("sync", "gpsimd")
DEBUG_SCRATCH = False
STOP_AFTER = None
P2_ITERS = NCH_DEFAULT = 32
P2_FLUSH = 1000
P2_POST = True


class Tok:
    __slots__ = ("eng", "idx", "sem", "val", "gen")

    def __init__(self, gen, eng=None, idx=None, sem=None, val=None):
        self.gen, self.eng, self.idx, self.sem, self.val = gen, eng, idx, sem, val


def sem_names(n_dma_sems):
    return [f"c_{e}" for e in ENGS] + [f"d_{e}_{i}" for e in DMA_ENGS for i in range(n_dma_sems)]


class SemSet:
    BUDGET = 96

    def __init__(self, n_dma_sems=4):
        self.n_dma = n_dma_sems
        self.used = 0


class Sched:
    def __init__(self, nc, tag, semset):
        self.nc, self.tag, self.n_dma, self.semset = nc, tag, semset.n_dma, semset
        self.gen = 0
        self.max_seen = 0
        self._reset()

    def _reset(self):
        self.ops = {e: [] for e in ENGS}
        self.dma_i = {e: 0 for e in DMA_ENGS}

    def _live(self, deps):
        return [d for d in deps if d is not None and d.gen == self.gen]

    def op(self, eng, fn, deps=()):
        self.ops[eng].append([fn, self._live(deps), None])
        return Tok(self.gen, eng=eng, idx=len(self.ops[eng]) - 1)

    def dma(self, eng, fn, deps=()):
        i = self.dma_i[eng]
        self.dma_i[eng] += 1
        slot, rnd = i % self.n_dma, i // self.n_dma
        key = f"d_{eng}_{slot}"
        deps = self._live(deps)
        if rnd > 0:
            deps.append(Tok(self.gen, sem=key, val=16 * rnd))
        self.ops[eng].append([fn, deps, key])
        return Tok(self.gen, sem=key, val=16 * (rnd + 1))

    def flush(self):
        if not any(self.ops[e] for e in ENGS):
            return
        nc = self.nc
        waited = {e: set() for e in ENGS}
        for e in ENGS:
            for _, deps, _ in self.ops[e]:
                for d in deps:
                    if d.sem is None:
                        waited[d.eng].add(d.idx)
        value = {}
        for e in ENGS:
            n = 0
            for i in range(len(self.ops[e])):
                if i in waited[e]:
                    n += 1
                    value[(e, i)] = n
        names = sem_names(self.n_dma)
        self.semset.used += len(names)
        assert self.semset.used <= SemSet.BUDGET, f"semaphore ID budget exceeded ({self.semset.used}); IDs would be recycled stale"
        pre = f"{self.tag}g{self.gen}_"
        with contextlib.ExitStack() as st:
            sems = {k: st.enter_context(nc.semaphore(pre + k)) for k in names}
            block = st.enter_context(nc.Block())

            def make(eng):
                def body(e):
                    seen = {}
                    for i, (fn, deps, dkey) in enumerate(self.ops[eng]):
                        for d in deps:
                            k, v = (d.sem, d.val) if d.sem is not None else (f"c_{d.eng}", value[(d.eng, d.idx)])
                            if seen.get(k, 0) < v:
                                e.wait_ge(sems[k], v)
                                seen[k] = v
                        ins = fn(e)
                        if dkey is not None:
                            ins.then_inc(sems[dkey], 16)
                        elif (eng, i) in value:
                            ins.then_inc(sems[f"c_{eng}"], 1)
                    if eng in DMA_ENGS:
                        n = self.dma_i[eng]
                        for slot in range(min(n, self.n_dma)):
                            rounds = (n - slot + self.n_dma - 1) // self.n_dma
                            e.wait_ge(sems[f"d_{eng}_{slot}"], 16 * rounds)
                return body
            for eng in ENGS:
                if self.ops[eng]:
                    getattr(block, eng)(make(eng))
        mx = max([0] + list(value.values()) + [16 * ((self.dma_i[e] + self.n_dma - 1) // self.n_dma) for e in DMA_ENGS])
        self.max_seen = max(self.max_seen, mx)
        self.gen += 1
        self._reset()

    def finish(self):
        self.flush()
        print(f"[build] {self.tag}: {self.gen} block(s), max semaphore value {self.max_seen}", flush=True)


def _tables():
    m = np.arange(NF1)[:, None, None]
    r = np.arange(NF2)[None, :, None]
    k1 = np.arange(HK)[None, None, :]
    th = 2 * np.pi * ((NF2 * m + r) * (2 * k1 + 1) % (2 * NF)) / (2 * NF)
    FA = np.stack([np.cos(th), -np.sin(th)], axis=2)
    thT = np.transpose(th[:128], (2, 1, 0))
    FAi = np.stack([np.cos(thT), -np.sin(thT)], axis=2) * (2.0 / NF)
    rr = np.arange(NF2)[:, None]
    k2 = np.arange(NF2)[None, :]
    tb = 2 * np.pi * ((rr * k2) % NF2) / NF2
    Br, Bi = np.cos(tb), -np.sin(tb)
    bd = lambda a: np.kron(a, np.eye(4))
    FB = np.stack([bd(Br), bd(Bi), bd(-Bi)], axis=1)
    Cr, Ci = bd(Br.T), bd(-Bi.T)
    FBi = np.stack([np.concatenate([Cr, Ci], 1), np.concatenate([-Ci, Cr], 1)], axis=1)
    n = np.arange(NF)
    j = np.where(n < SEQ, n, NF - n).astype(np.float64)
    j[SEQ] = 0
    t = j / (SEQ - 1)
    ang = TWO_PI * j / SEQ
    f = np.linspace(1e-4, 15, 16)
    z = np.concatenate([t[:, None], np.cos(f[None] * ang[:, None]), -np.sin(f[None] * ang[:, None])], axis=1)
    sg = np.where(n < SEQ, 1.0, -1.0)
    sg[SEQ] = 0.0
    deltas = np.abs(np.linspace(math.log(1e-2) / 1.5, math.log(1e-2) / 0.3, 512))
    idx = np.arange(128)
    triF = (idx[:, None] <= idx[None, :]).astype(np.float32)
    triB = (idx[:, None] >= idx[None, :]).astype(np.float32)
    maskF = np.where(idx[:, None] <= idx[None, :], 0.0, NEG).astype(np.float32)
    maskB = np.where(idx[:, None] >= idx[None, :], 0.0, NEG).astype(np.float32)
    return dict(
        FA_lo=FA[:128].astype(BF), FA_hi=FA[128:].astype(BF), FAi=FAi.astype(BF), FB=FB.astype(BF), FBi=FBi.astype(BF),
        zT=np.ascontiguousarray(z.T).astype(np.float32), ntj=(-t.reshape(256, 32)).astype(np.float32),
        sg=sg.reshape(256, 32).astype(np.float32), deltas=deltas.astype(np.float32),
        cst=np.ascontiguousarray(np.stack([triF, triB, maskF, maskB], axis=1)),
        ident=np.eye(128).astype(BF), identf=np.eye(128, dtype=np.float32))


CONST_SPECS = dict(FA_lo=([128, 32, 2, 128], BF16), FA_hi=([128, 32, 2, 128], BF16), FAi=([128, 32, 2, 128], BF16),
                   FB=([128, 3, 128], BF16), FBi=([128, 2, 256], BF16), zT=([33, NF], F32), ntj=([256, 32], F32),
                   sg=([256, 32], F32), deltas=([512], F32), cst=([128, 4, 128], F32), ident=([128, 128], BF16),
                   identf=([128, 128], F32))
PARAM_SPECS = dict(xd=([SEQ, D], F32), w_in=([D, N_IN], F32), gains=([4, D], F32), bg=([16], F32), cw_qk=([1024, 4], F32),
                   cw_h=([4, 1536], F32), nw_m=([512], F32), nw_h=([512], F32), fw1=([33, 64], F32), fw2=([64, 64], F32),
                   fw3=([64, 64], F32), fw4=([64, 2048], F32), w4l0=([64, 2, 512], F32), fbs=([64, 3], F32),
                   ffq=([64, 3], F32), fb=([2, 512], F32), w_out=([D, D], F32), w1=([D, DFF], F32), w2=([DFF, D], F32))


def build_nc():
    nc = bass.Bass("TRN2", target_bir_lowering=False)
    I = {k: nc.dram_tensor(k, sh, dt, kind="ExternalInput").ap() for k, (sh, dt) in {**CONST_SPECS, **PARAM_SPECS}.items()}
    out_d = nc.dram_tensor("out_d", [16, 128, D], F32, kind="ExternalOutput").ap()
    qk_d = nc.dram_tensor("qk_d", [8, 128, SEQ], BF16, kind=("ExternalOutput" if DEBUG_SCRATCH else "Internal")).ap()
    v_d = nc.dram_tensor("v_d", [NCH, 128, 512], BF16, kind=("ExternalOutput" if DEBUG_SCRATCH else "Internal")).ap()
    og_d = nc.dram_tensor("og_d", [NCH, 128, 512], BF16, kind=("ExternalOutput" if DEBUG_SCRATCH else "Internal")).ap()
    gt_d = nc.dram_tensor("gt_d", [128, NCH, 16], F32, kind=("ExternalOutput" if DEBUG_SCRATCH else "Internal")).ap()
    cvh_d = nc.dram_tensor("cvh_d", [128, 32, 1536], BF16, kind=("ExternalOutput" if DEBUG_SCRATCH else "Internal")).ap()
    h3_d = nc.dram_tensor("h3_d", [64, NF], BF16, kind=("ExternalOutput" if DEBUG_SCRATCH else "Internal")).ap()
    ytok_d = nc.dram_tensor("ytok_d", [SEQ, D], BF16, kind=("ExternalOutput" if DEBUG_SCRATCH else "Internal")).ap()

    SEMS = SemSet()
    _build_phases(nc, I, out_d, SEMS, qk_d, v_d, og_d, gt_d, cvh_d, h3_d, ytok_d)
    print(f"[build] semaphore IDs used: {SEMS.used} / budget {SemSet.BUDGET}", flush=True)
    return nc


def _build_phases(nc, I, out_d, SEMS, qk_d, v_d, og_d, gt_d, cvh_d, h3_d, ytok_d):

    def ld(S, eng, out, in_, deps=()):
        return S.dma(eng, lambda e: e.dma_start(out=out, in_=in_), deps=list(deps))

    def cp(S, eng, out, in_, deps=()):
        if eng == "scalar":
            return S.op("scalar", lambda e: e.copy(out=out, in_=in_), deps=list(deps))
        return S.op(eng, lambda e: e.tensor_copy(out=out, in_=in_), deps=list(deps))

    with contextlib.ExitStack() as outer:
        hnT = outer.enter_context(nc.sbuf_tensor("p1_hnT", [128, 8, SEQ + 2], BF16))
        with contextlib.ExitStack() as st:
            sb = lambda name, shape, dt: st.enter_context(nc.sbuf_tensor("p1a_" + name, shape, dt))
            pt = lambda name, shape, dt: st.enter_context(nc.psum_tensor("p1a_" + name, shape, dt))
            S = Sched(nc, "p1a", SEMS)
            xt = [sb(f"xt{i}", [128, D], F32) for i in range(3)]
            xn = [sb(f"xn{i}", [128, D], BF16) for i in range(3)]
            sq = sb("sq", [128, D], F32)
            ss = sb("ss", [128, NCH], F32)
            gb = sb("gb", [128, D], F32)
            ident = sb("ident", [128, 128], BF16)
            wbuf = [sb(f"wbuf{i}", [128, 8, 512], BF16) for i in range(2)]
            wg = sb("wg", [128, 8, 16], BF16)
            stg = [sb(f"stg{i}", [128, 4, 512], F32) for i in range(2)]
            cw = sb("cw", [128, 8, 4], F32)
            bgb = sb("bgb", [128, 16], F32)
            row2 = [sb(f"row{i}", [128, SEQ + 2], F32) for i in range(2)]
            acc = sb("acc", [128, SEQ], F32)
            qko = [sb(f"qko{i}", [128, SEQ], BF16) for i in range(2)]
            vst = [sb(f"vst{i}", [128, 512], BF16) for i in range(2)]
            ost = [sb(f"ost{i}", [128, 512], BF16) for i in range(2)]
            gts = sb("gts", [128, NCH, 16], F32)
            pT = [pt(f"pT{i}", [128, 4, 128], BF16) for i in range(2)]
            pA = [pt(f"pA{i}", [128, 512], F32) for i in range(4)]

            t_g = ld(S, "sync", gb[:], I["gains"][0].partition_broadcast(128))
            t_id = ld(S, "sync", ident[:], I["ident"][:, :])
            t_cw = ld(S, "sync", cw[:], I["cw_qk"].rearrange("(c p) k -> p c k", p=128))
            t_bg = ld(S, "sync", bgb[:], I["bg"].partition_broadcast(128))
            t_hz = S.op("gpsimd", lambda e: e.memset(hnT[:, :, 0:1], 0.0))
            t_hz2 = S.op("gpsimd", lambda e: e.memset(hnT[:, :, SEQ + 1:SEQ + 2], 0.0))
            t_rz2 = [S.op("gpsimd", lambda e, i=i: e.memset(row2[i][:], 0.0)) for i in range(2)]

            stg_free = [None, None]
            si = [0]

            def load_group(dst, col0, ncols, guard):
                toks = []
                for half in range(2):
                    b = si[0] % 2
                    si[0] += 1
                    t_l = ld(S, "gpsimd" if b else "sync", stg[b][:, :, :ncols],
                             I["w_in"][half * 512:(half + 1) * 512, col0:col0 + ncols].rearrange("(k p) n -> p k n", p=128),
                             deps=[stg_free[b]])
                    t_c = cp(S, "scalar" if half else "vector", dst[:, half * 4:half * 4 + 4, :ncols], stg[b][:, :, :ncols],
                             deps=[t_l] + guard)
                    stg_free[b] = t_c
                    toks.append(t_c)
                return toks

            pa_free = [None] * 4
            pa_i = [0]
            w_users = [[], []]
            t_wv = load_group(wbuf[0], OFF_MV, 512, [])
            t_wo = load_group(wbuf[1], OFF_MO, 512, [])
            t_wgl = ld(S, "sync", stg[0][:, :, 0:16], I["w_in"][0:512, OFF_GATE:OFF_GATE + 16].rearrange("(k p) n -> p k n", p=128),
                       deps=[stg_free[0]])
            t_wgl2 = ld(S, "sync", stg[1][:, :, 0:16], I["w_in"][512:1024, OFF_GATE:OFF_GATE + 16].rearrange("(k p) n -> p k n", p=128),
                        deps=[stg_free[1]])
            t_wg = [cp(S, "vector", wg[:, 0:4, :], stg[0][:, :, 0:16], deps=[t_wgl]),
                    cp(S, "vector", wg[:, 4:8, :], stg[1][:, :, 0:16], deps=[t_wgl2])]
            stg_free[0], stg_free[1] = t_wg[0], t_wg[1]
            vo_free = [[None, None], [None, None]]
            g_evs = []

            def emit_vog(c, hn_c):
                for grp, (wb, t_w) in enumerate(((0, t_wv), (1, t_wo))):
                    pb = pa_i[0] % 4
                    pa_i[0] += 1
                    tt = None
                    for kc in range(8):
                        tt = S.op("tensor", lambda e, wb=wb, kc=kc, pb=pb: e.matmul(
                            pA[pb][:], lhsT=hnT[:, kc, 1 + c * 128:1 + (c + 1) * 128], rhs=wbuf[wb][:, kc, :],
                            start=(kc == 0), stop=(kc == 7)), deps=hn_c + t_w + [pa_free[pb]])
                    sbi = c % 2
                    if grp == 0:
                        ev = cp(S, "vector", vst[sbi][:], pA[pb][:], deps=[tt, vo_free[0][sbi]])
                        vo_free[0][sbi] = ld(S, "sync", v_d[c], vst[sbi][:], deps=[ev])
                    else:
                        ev = S.op("scalar", lambda e, sbi=sbi, pb=pb: e.activation(out=ost[sbi][:], in_=pA[pb][:], func=AF.Sigmoid),
                                  deps=[tt, vo_free[1][sbi]])
                        vo_free[1][sbi] = ld(S, "sync", og_d[c], ost[sbi][:], deps=[ev])
                    pa_free[pb] = ev
                    w_users[wb] = [tt]
                pb = pa_i[0] % 4
                pa_i[0] += 1
                tt = None
                for kc in range(8):
                    tt = S.op("tensor", lambda e, kc=kc, pb=pb: e.matmul(
                        pA[pb][:, 0:16], lhsT=hnT[:, kc, 1 + c * 128:1 + (c + 1) * 128], rhs=wg[:, kc, :],
                        start=(kc == 0), stop=(kc == 7)), deps=hn_c + t_wg + [pa_free[pb]])
                ev = S.op("vector", lambda e, pb=pb: e.tensor_add(out=gts[:, c, :], in0=pA[pb][:, 0:16], in1=bgb[:]), deps=[tt, t_bg])
                pa_free[pb] = ev
                g_evs.append(ev)

            NXB = 3
            x_free = [None] * NXB
            xn_free = [None] * NXB
            pT_free = [None, None]
            hn_tok = [None] * NCH
            tk_ld, tk_n = {}, {}
            pi = [0]

            def st_load(i):
                b = i % NXB
                tk_ld[i] = ld(S, "sync", xt[b][:], I["xd"][i * 128:(i + 1) * 128, :], deps=[x_free[b]])

            def st_norm(i):
                b = i % NXB
                t_sq = S.op("scalar", lambda e: e.activation(out=sq[:], in_=xt[b][:], func=AF.Square, accum_out=ss[:, i:i + 1]),
                            deps=[tk_ld[i]])
                t_ln = S.op("scalar", lambda e: e.activation(out=ss[:, i:i + 1], in_=ss[:, i:i + 1], func=AF.Ln, scale=1.0 / D, bias=EPS),
                            deps=[t_sq])
                t_ex = S.op("scalar", lambda e: e.activation(out=ss[:, i:i + 1], in_=ss[:, i:i + 1], func=AF.Exp, scale=-0.5), deps=[t_ln])
                t_n = S.op("vector", lambda e: e.scalar_tensor_tensor(out=xn[b][:], in0=xt[b][:], scalar=ss[:, i:i + 1], in1=gb[:],
                                                                      op0=ALU.mult, op1=ALU.mult), deps=[t_ex, t_g, xn_free[b]])
                x_free[b] = t_n
                tk_n[i] = t_n

            def st_tr(i):
                b = i % NXB
                cps = []
                tt = None
                for half in range(2):
                    pb = pi[0] % 2
                    pi[0] += 1
                    for j in range(4):
                        c = half * 4 + j
                        tt = S.op("tensor", lambda e, c=c, j=j, pb=pb: e.transpose(
                            out=pT[pb][:, j, :], in_=xn[b][:, c * 128:(c + 1) * 128], identity=ident[:]),
                            deps=[tk_n[i], t_id, pT_free[pb]])
                    t_cp = cp(S, "scalar" if half else "vector", hnT[:, half * 4:half * 4 + 4, 1 + i * 128:1 + (i + 1) * 128],
                              pT[pb][:], deps=[tt])
                    pT_free[pb] = t_cp
                    cps.append(t_cp)
                xn_free[b] = tt
                hn_tok[i] = cps

            for k in range(-2, NCH + 1):
                if 0 <= k + 2 < NCH:
                    st_load(k + 2)
                if 0 <= k + 1 < NCH:
                    st_norm(k + 1)
                if 0 <= k < NCH:
                    st_tr(k)
                if 0 <= k - 1 < NCH:
                    emit_vog(k - 1, hn_tok[k - 1] + [t_hz, t_hz2])
            all_hn = [t for c in hn_tok for t in c] + [t_hz, t_hz2]

            qko_free = [None, None]
            row_free = [t_rz2[0], t_rz2[1]]
            st8 = dict(acc_free=None, t_w=None)
            ev8 = {}

            def qk_mm(t):
                grp, ct = t // 4, t % 4
                wb = grp % 2
                if ct == 0:
                    st8["t_w"] = load_group(wbuf[wb], (OFF_MQ, OFF_MK)[grp], 512, w_users[wb])
                    w_users[wb] = []
                t_w = st8["t_w"]
                rb = t % 2
                row = row2[rb]
                evs = []
                tt = None
                for tb in range(8):
                    pb = pa_i[0] % 4
                    pa_i[0] += 1
                    for kc in range(8):
                        tt = S.op("tensor", lambda e, wb=wb, ct=ct, kc=kc, tb=tb, pb=pb: e.matmul(
                            pA[pb][:], lhsT=wbuf[wb][:, kc, ct * 128:(ct + 1) * 128],
                            rhs=hnT[:, kc, 1 + tb * 512:1 + (tb + 1) * 512], start=(kc == 0), stop=(kc == 7)),
                            deps=all_hn + t_w + [pa_free[pb]])
                    ev = cp(S, "scalar", row[:, 1 + tb * 512:1 + (tb + 1) * 512], pA[pb][:], deps=[tt, row_free[rb]])
                    pa_free[pb] = ev
                    evs.append(ev)
                w_users[wb].append(tt)
                ev8[t] = evs

            def qk_conv(t):
                grp = t // 4
                rb = t % 2
                row = row2[rb]
                c1 = S.op("vector", lambda e: e.tensor_scalar(
                    out=acc[:], in0=row[:, 1:SEQ + 1], scalar1=cw[:, t, 1:2], scalar2=cw[:, t, 3:4],
                    op0=ALU.mult, op1=ALU.add), deps=ev8[t] + [t_cw, st8["acc_free"]])
                c2 = S.op("vector", lambda e: e.scalar_tensor_tensor(
                    out=acc[:], in0=row[:, 0:SEQ], scalar=cw[:, t, 0:1], in1=acc[:], op0=ALU.mult, op1=ALU.add), deps=[c1])
                c3 = S.op("vector", lambda e: e.scalar_tensor_tensor(
                    out=acc[:], in0=row[:, 2:SEQ + 2], scalar=cw[:, t, 2:3], in1=acc[:], op0=ALU.mult, op1=ALU.add), deps=[c2])
                row_free[rb] = c3
                ob = t % 2
                if grp == 0:
                    t_s = S.op("scalar", lambda e: e.activation(out=qko[ob][:], in_=acc[:], func=AF.Silu), deps=[c3, qko_free[ob]])
                else:
                    t_s0 = S.op("scalar", lambda e: e.activation(out=acc[:], in_=acc[:], func=AF.Silu), deps=[c3])
                    t_s = S.op("vector", lambda e: e.tensor_scalar(out=qko[ob][:], in0=acc[:], scalar1=128 ** -0.5, scalar2=None,
                                                                   op0=ALU.mult), deps=[t_s0, qko_free[ob]])
                st8["acc_free"] = t_s
                qko_free[ob] = ld(S, "sync", qk_d[t], qko[ob][:], deps=[t_s])

            for t in range(9):
                if t < 8:
                    qk_mm(t)
                if t >= 1:
                    qk_conv(t - 1)

            ld(S, "sync", gt_d[:, :, :], gts[:], deps=g_evs)
            S.finish()
        if STOP_AFTER == 'P1a':
            return

        with contextlib.ExitStack() as st:
            sb = lambda name, shape, dt: st.enter_context(nc.sbuf_tensor("p1b_" + name, shape, dt))
            pt = lambda name, shape, dt: st.enter_context(nc.psum_tensor("p1b_" + name, shape, dt))
            S = Sched(nc, "p1b", SEMS)
            wh = sb("wh", [128, 8, 1536], BF16)
            stg = [sb(f"stgb{i}", [128, 4, 512], F32) for i in range(2)]
            cwb = sb("cwb", [128, 4, 1536], F32)
            pl = sb("pl", [128, 3, 1536], F32)
            tmpc = sb("tmpc", [128, 1536], F32)
            oacc = sb("oacc", [128, 1536], F32)
            cvb = [sb(f"cvb{i}", [128, 1536], BF16) for i in range(2)]
            pP = [[pt(f"pP{i}_{j}", [128, 512], F32) for j in range(3)] for i in range(2)]
            t_cwb = ld(S, "sync", cwb[:], I["cw_h"].partition_broadcast(128))
            stg_free = [None, None]
            t_wh = []
            k_ = 0
            for g3 in range(3):
                for half in range(2):
                    b = k_ % 2
                    k_ += 1
                    t_l = ld(S, "gpsimd" if b else "sync", stg[b][:],
                             I["w_in"][half * 512:(half + 1) * 512, OFF_HV + g3 * 512:OFF_HV + (g3 + 1) * 512].rearrange(
                                 "(k p) n -> p k n", p=128), deps=[stg_free[b]])
                    t_c = cp(S, "vector" if half else "scalar", wh[:, half * 4:half * 4 + 4, g3 * 512:(g3 + 1) * 512], stg[b][:],
                             deps=[t_l])
                    stg_free[b] = t_c
                    t_wh.append(t_c)
            pp_free = [[None] * 3, [None] * 3]
            pl_ready = {}
            pl_free = [None, None, None]
            cvb_free = [None, None]
            HV = (slice(0, 1024), slice(1024, 1536))
            HE = ("vector", "gpsimd")
            for r in range(-1, 34):
                if r <= 32:
                    pb = (r + 1) % 2
                    slot = (r + 1) % 3
                    evs = []
                    for g3 in range(3):
                        tt = None
                        for kc in range(8):
                            tt = S.op("tensor", lambda e, kc=kc, g3=g3, pb=pb, r=r: e.matmul(
                                pP[pb][g3][:], lhsT=hnT[:, kc, bass.ds(1 + r, 128, step=32)],
                                rhs=wh[:, kc, g3 * 512:(g3 + 1) * 512], start=(kc == 0), stop=(kc == 7)),
                                deps=t_wh + [pp_free[pb][g3]])
                        ev = cp(S, "scalar", pl[:, slot, g3 * 512:(g3 + 1) * 512], pP[pb][g3][:], deps=[tt, pl_free[slot]])
                        pp_free[pb][g3] = ev
                        evs.append(ev)
                    pl_ready[r] = evs
                ro = r - 1
                if 0 <= ro <= 31:
                    a, b_, c_ = (ro) % 3, (ro + 1) % 3, (ro + 2) % 3
                    ob = ro % 2
                    fin = []
                    for hv, he in zip(HV, HE):
                        dps = pl_ready[ro - 1] + pl_ready[ro] + pl_ready[ro + 1] + [t_cwb]
                        o1 = S.op(he, lambda e, hv=hv, b_=b_: e.tensor_tensor(out=oacc[:, hv], in0=pl[:, b_, hv], in1=cwb[:, 1, hv],
                                                                               op=ALU.mult), deps=dps)
                        o2 = S.op(he, lambda e, hv=hv: e.tensor_add(out=oacc[:, hv], in0=oacc[:, hv], in1=cwb[:, 3, hv]), deps=[o1])
                        o3 = S.op(he, lambda e, hv=hv, a=a: e.tensor_tensor(out=tmpc[:, hv], in0=pl[:, a, hv], in1=cwb[:, 0, hv],
                                                                             op=ALU.mult), deps=[o2])
                        o4 = S.op(he, lambda e, hv=hv: e.tensor_add(out=oacc[:, hv], in0=oacc[:, hv], in1=tmpc[:, hv]), deps=[o3])
                        o5 = S.op(he, lambda e, hv=hv, c_=c_: e.tensor_tensor(out=tmpc[:, hv], in0=pl[:, c_, hv], in1=cwb[:, 2, hv],
                                                                               op=ALU.mult), deps=[o4])
                        o6 = S.op(he, lambda e, hv=hv, ob=ob: e.tensor_add(out=cvb[ob][:, hv], in0=oacc[:, hv], in1=tmpc[:, hv]),
                                  deps=[o5, cvb_free[ob]])
                        fin.append(o6)
                    pl_free[a] = None
                    pl_free[a] = fin[0]
                    pl_free_extra = fin[1]
                    mk = S.op("vector", lambda e: e.tensor_copy(out=tmpc[0:1, 0:1], in_=tmpc[0:1, 0:1]), deps=fin)
                    pl_free[a] = mk
                    cvb_free[ob] = ld(S, "sync", cvh_d[:, ro, :], cvb[ob][:], deps=fin)
            S.finish()
    if STOP_AFTER == 'P1':
        return
    with contextlib.ExitStack() as st:
        sb = lambda name, shape, dt: st.enter_context(nc.sbuf_tensor("p2_" + name, shape, dt))
        pt = lambda name, shape, dt: st.enter_context(nc.psum_tensor("p2_" + name, shape, dt))
        S = Sched(nc, "p2", SEMS)
        cst = sb("cst", [128, 4, 128], F32)
        ones_f = sb("ones_f", [128, 128], F32)
        ident = sb("ident", [128, 128], BF16)
        identf = sb("identf", [128, 128], F32)
        nwb = sb("nwb", [128, 512], F32)
        gt = sb("gt", [128, NCH, 16], F32)
        NB = 3
        qkc = [[sb(f"qkc{d}{p}", [128, 8, 128], BF16) for p in range(NB)] for d in range(2)]
        v1c = [[sb(f"v1c{d}{p}", [128, NH, 129], BF16) for p in range(NB)] for d in range(2)]
        ogc = [sb(f"ogc{p}", [128, 512], BF16) for p in range(2)]
        lfB2 = [sb(f"lfB{q}", [128, 8, 128], F32) for q in range(2)]
        lf2 = [sb(f"lf{q}", [128, 8], F32) for q in range(2)]
        bcs2 = [sb(f"bcs{q}", [128, 8], F32) for q in range(2)]
        bias2 = [sb(f"bias{q}", [128, 8], F32) for q in range(2)]
        eb2 = [sb(f"eb{q}", [128, 8], F32) for q in range(2)]
        av2 = [sb(f"av{q}", [128, 8], F32) for q in range(2)]
        eg2 = [sb(f"eg{q}", [128, 8], F32) for q in range(2)]
        WT2 = [sb(f"WT{q}", [128, 8, 128], F32) for q in range(2)]
        SW2 = [sb(f"SW{q}", [128, 8, 128], BF16) for q in range(2)]
        kT_tok2 = [[sb(f"kT_tok{d}{q}", [128, NH, 128], BF16) for q in range(2)] for d in range(2)]
        Va2 = [sb(f"Va{q}", [128, 8, 129], BF16) for q in range(2)]
        Cst = sb("Cst", [128, 8, 129], F32)
        Cbf = sb("Cbf", [128, 8, 129], BF16)
        hacc = sb("hacc", [128, NCH, 512], F32)
        inter = [sb(f"inter{d}", [128, 129], F32) for d in range(2)]
        tot = [sb(f"tot{d}", [128, 129], F32) for d in range(2)]
        rden = [sb(f"rden{d}", [128, 1], F32) for d in range(2)]
        hsum = sb("hsum", [128, 512], F32)
        sqs = sb("sqs", [128, 512], F32)
        gst = sb("gst", [128, 4], F32)
        yt = [sb(f"yt{p}", [128, 512], BF16) for p in range(2)]
        bankA = [pt(f"bankA{d}", [128, NH, 128], F32) for d in range(2)]
        bankB = [pt(f"bankB{d}", [128, 512], F32) for d in range(2)]
        bankC = [pt(f"bankC{d}", [128, NH, 128], BF16) for d in range(2)]
        bankD = [pt(f"bankD{d}", [128, 129], F32) for d in range(2)]
        dg = sb("dg", [128, NH, 128], F32)
        t_cst = ld(S, "sync", cst[:], I["cst"][:, :, :])
        t_id = ld(S, "sync", ident[:], I["ident"][:, :])
        t_idf = ld(S, "sync", identf[:], I["identf"][:, :])
        t_nw = ld(S, "sync", nwb[:], I["nw_m"].partition_broadcast(128))
        t_gt = ld(S, "sync", gt[:], gt_d[:, :, :])
        t_ones = S.op("gpsimd", lambda e: e.memset(ones_f[:], 1.0))
        t_cz = S.op("gpsimd", lambda e: e.memset(Cst[:], 0.0))
        t_cbz = S.op("gpsimd", lambda e: e.memset(Cbf[:], 0.0))
        t_v1 = [[S.op("gpsimd", lambda e, d=d, p=p: e.memset(v1c[d][p][:], 1.0)) for p in range(NB)] for d in range(2)]
        qk_view = qk_d.rearrange("t p s -> p t s")

        buf_free = [[None] * NB, [None] * NB]
        loaded = {}

        def issue_loads(i):
            p = i % NB
            for d, c in ((0, i), (1, NCH - 1 - i)):
                t1 = ld(S, "sync", qkc[d][p][:], qk_view[:, :, c * 128:(c + 1) * 128], deps=[buf_free[d][p]])
                t2 = ld(S, "sync", v1c[d][p][:, :, 0:128], v_d[c].rearrange("p (h e) -> p h e", h=NH),
                        deps=[buf_free[d][p], t_v1[d][p]])
                loaded[(i, d)] = [t1, t2]

        state = dict(lastp=None, post_i=0, og_free=[None, None], yt_free=[None, None])
        done = {}
        ctx = {}

        def front(i, c, d):
            q = i % 2
            p = i % NB
            lf, bcs, bias, eb, av, eg = lf2[q], bcs2[q], bias2[q], eb2[q], av2[q], eg2[q]
            lfB, WT, SW, Va, kT_tok = lfB2[q], WT2[q], SW2[q], Va2[q], kT_tok2[d][q]
            Q, V = qkc[d][p], v1c[d][p]
            tl = loaded[(i, d)]
            old = done.get((i - 2, d), [])
            pc = ctx.get((i - 1, d), {})
            pW = pS = bankA[d]
            pTk = bankC[d]
            J = slice(d * 4, d * 4 + 4)
            gi = slice(d * 8, d * 8 + 4)
            gf = slice(d * 8 + 4, d * 8 + 8)
            a1 = S.op("scalar", lambda e: e.activation(out=lf[:, J], in_=gt[:, c, gf], func=AF.Exp, scale=-1.0), deps=[t_gt] + old)
            a2 = S.op("scalar", lambda e: e.activation(out=lf[:, J], in_=lf[:, J], func=AF.Ln, bias=1.0), deps=[a1])
            a3 = S.op("vector", lambda e: e.tensor_scalar(out=lf[:, J], in0=lf[:, J], scalar1=-1.0, scalar2=None, op0=ALU.mult),
                      deps=[a2])
            tb = [S.op("vector", lambda e: e.tensor_tensor(
                out=lfB[:, J, :], in0=cst[:, d, :].unsqueeze(1).to_broadcast([128, NH, 128]),
                in1=lf[:, J].unsqueeze(2).to_broadcast([128, NH, 128]), op=ALU.mult), deps=[a3, t_cst] + old)]
            yield
            tw = None
            for h in range(NH):
                S.op("tensor", lambda e, h=h: e.matmul(pW[:, h, :], lhsT=ones_f[:], rhs=lfB[:, d * 4 + h, :], start=True, stop=False),
                     deps=tb + [t_ones] + pc.get("tsw", []))
                tw = S.op("tensor", lambda e, h=h: e.matmul(pW[:, h, :], lhsT=identf[:], rhs=cst[:, 2 + d, :], start=False, stop=True),
                          deps=[t_idf])
            tk = None
            for h in range(NH):
                tk = S.op("tensor", lambda e, h=h: e.transpose(out=pTk[:, h, :], in_=Q[:, 4 + h, :], identity=ident[:]),
                          deps=tl + [t_id] + pc.get("tkc", []))
            yield
            b0 = S.op("vector", lambda e: e.tensor_tensor(out=dg[:], in0=pW[:], in1=identf[:].unsqueeze(1).to_broadcast([128, NH, 128]),
                                                          op=ALU.mult), deps=[tw, t_idf] + old + pc.get("b1", []))
            b1 = S.op("vector", lambda e: e.reduce_sum(out=bcs[:, J], in_=dg[:], axis=AX.X), deps=[b0])
            b2 = S.op("vector", lambda e: e.tensor_sub(out=bias[:, J], in0=gt[:, c, gi], in1=bcs[:, J]), deps=[b1])
            b3 = S.op("scalar", lambda e: e.activation(out=eb[:, J], in_=bcs[:, J], func=AF.Exp), deps=[b1] + old)
            tkc = S.op("scalar", lambda e: e.copy(out=kT_tok[:], in_=pTk[:]), deps=[tk] + old)
            gcol = 127 if d == 0 else 0
            tW = []
            for h in range(NH):
                j = d * 4 + h
                tW.append(S.op("scalar", lambda e, h=h, j=j: e.activation(
                    out=WT[:, j, :], in_=pW[:, h, :], func=AF.Exp, bias=bias[:, j:j + 1]), deps=[tw, b2] + old))
            t_eg = S.op("scalar", lambda e: e.activation(out=eg[:, J], in_=pW[:, :, gcol], func=AF.Exp), deps=[tw] + old)
            t_ebi = S.op("scalar", lambda e: e.activation(out=av[:, J], in_=bias[:, J], func=AF.Exp), deps=[b2] + old)
            t_av = S.op("vector", lambda e: e.tensor_mul(out=av[:, J], in0=av[:, J], in1=eg[:, J]), deps=[t_eg, t_ebi])
            yield
            ts_ = None
            for h in range(NH):
                ts_ = S.op("tensor", lambda e, h=h: e.matmul(pS[:, h, :], lhsT=Q[:, 4 + h, :], rhs=Q[:, h, :], start=True, stop=True),
                           deps=tl + [t_eg, b0])
            tsw = [S.op("vector", lambda e, h=h: e.tensor_tensor(out=SW[:, d * 4 + h, :], in0=pS[:, h, :], in1=WT[:, d * 4 + h, :],
                                                                 op=ALU.mult), deps=[ts_, tW[h]] + old) for h in range(NH)]
            u1s = [S.op("scalar", lambda e, h=h: e.activation(out=Va[:, d * 4 + h, :], in_=V[:, h, :], func=AF.Copy,
                                                              scale=av[:, d * 4 + h:d * 4 + h + 1]), deps=[t_av] + old + tl) for h in range(NH)]
            yield
            ctx[(i, d)] = dict(tsw=tsw, b1=[b1], tkc=[tkc], u1s=u1s, b3=b3, t_av=t_av, tl=tl)

        def heads(i, c, d):
            q = i % 2
            p = i % NB
            eb, eg = eb2[q], eg2[q]
            SW, Va, kT_tok = SW2[q], Va2[q], kT_tok2[d][q]
            Q, V = qkc[d][p], v1c[d][p]
            cx = ctx[(i, d)]
            tsw, u1s, tkc, b3, t_av, tl = cx["tsw"], cx["u1s"], cx["tkc"][0], cx["b3"], cx["t_av"], cx["tl"]
            prev_l = done.get((i - 1, d), [])
            pN = bankB[d][:, 0:258].rearrange("p (a b) -> p a b", a=2)
            pC = bankD[d][:]
            first = (d == 0 and c < NCH // 2) or (d == 1 and c >= NCH // 2)
            last_v = None
            tm_last = None
            u4 = None
            for h in range(NH):
                j = d * 4 + h
                S.op("tensor", lambda e, h=h, j=j: e.matmul(pN[:, 0, :], lhsT=SW[:, j, :], rhs=V[:, h, :], start=True, stop=True),
                     deps=[tsw[h], last_v] + tl + prev_l)
                tm = S.op("tensor", lambda e, h=h, j=j: e.matmul(pN[:, 1, :], lhsT=Q[:, h, :], rhs=Cbf[:, j, :], start=True, stop=True),
                          deps=[t_cbz, last_v] + prev_l)
                u2 = S.op("tensor", lambda e, h=h, j=j: e.matmul(pC, lhsT=kT_tok[:, h, :], rhs=Va[:, j, :], start=True, stop=True),
                          deps=[u1s[h], tkc, last_v] + prev_l)
                i1 = S.op("scalar", lambda e, j=j: e.activation(out=inter[d][:], in_=pN[:, 1, :], func=AF.Copy, scale=eb[:, j:j + 1]),
                          deps=[tm, b3, last_v] + prev_l)
                i2 = S.op("vector", lambda e: e.tensor_add(out=tot[d][:], in0=pN[:, 0, :], in1=inter[d][:]), deps=[i1, tm] + prev_l)
                u3 = S.op("vector", lambda e, j=j: e.scalar_tensor_tensor(out=Cst[:, j, :], in0=Cst[:, j, :], scalar=eg[:, j:j + 1],
                                                                          in1=pC, op0=ALU.mult, op1=ALU.add), deps=[u2, t_av, t_cz, tm])
                u4 = S.op("scalar", lambda e, j=j: e.copy(out=Cbf[:, j, :], in_=Cst[:, j, :]), deps=[u3, tm])
                i3a = S.op("vector", lambda e: e.scalar_tensor_tensor(out=rden[d][:], in0=tot[d][:, 128:129], scalar=-1.0,
                                                                      in1=tot[d][:, 128:129], op0=ALU.mult, op1=ALU.max), deps=[i2])
                i3 = S.op("vector", lambda e: e.tensor_scalar_max(out=rden[d][:], in0=rden[d][:], scalar1=1.0), deps=[i3a])
                i4 = S.op("vector", lambda e: e.reciprocal(out=rden[d][:], in_=rden[d][:]), deps=[i3])
                hs = slice(h * 128, (h + 1) * 128)
                if first:
                    i5 = S.op("vector", lambda e, hs=hs: e.tensor_scalar(out=hacc[:, c, hs], in0=tot[d][:, 0:128], scalar1=rden[d][:, 0:1],
                                                                         scalar2=None, op0=ALU.mult), deps=[i4])
                else:
                    i5 = S.op("vector", lambda e, hs=hs: e.scalar_tensor_tensor(out=hacc[:, c, hs], in0=tot[d][:, 0:128], scalar=rden[d][:, 0:1],
                                                                                in1=hacc[:, c, hs], op0=ALU.mult, op1=ALU.add), deps=[i4])
                last_v = i5
                tm_last = u2
                yield
            done[(i, d)] = [last_v, u4]
            buf_free[d][p] = tm_last
            if not first and P2_POST:
                post(c, last_v)

        def post(c, done):
            k = state["post_i"]
            state["post_i"] += 1
            p = k % 2
            t_og = ld(S, "sync", ogc[p][:], og_d[c], deps=[state["og_free"][p]])
            p1 = S.op("vector", lambda e: e.tensor_tensor(out=hsum[:], in0=hacc[:, c, :], in1=ogc[p][:], op=ALU.mult),
                      deps=[done, state["lastp"], t_og])
            state["og_free"][p] = p1
            pl_ = None
            for h in range(NH):
                hs = slice(h * 128, (h + 1) * 128)
                pl_ = S.op("scalar", lambda e, h=h, hs=hs: e.activation(out=sqs[:, hs], in_=hsum[:, hs], func=AF.Square,
                                                                        accum_out=gst[:, h:h + 1]), deps=[p1])
            p3 = S.op("scalar", lambda e: e.activation(out=gst[:], in_=gst[:], func=AF.Ln, scale=1.0 / 128, bias=EPS), deps=[pl_])
            p4 = S.op("scalar", lambda e: e.activation(out=gst[:], in_=gst[:], func=AF.Exp, scale=-0.5), deps=[p3])
            pq = None
            for h in range(NH):
                hs = slice(h * 128, (h + 1) * 128)
                pq = S.op("vector", lambda e, h=h, hs=hs: e.scalar_tensor_tensor(
                    out=yt[p][:, hs], in0=hsum[:, hs], scalar=gst[:, h:h + 1], in1=nwb[:, hs], op0=ALU.mult, op1=ALU.mult),
                    deps=[p4, t_nw, state["yt_free"][p]])
            state["lastp"] = pq
            state["yt_free"][p] = ld(S, "sync", ytok_d[c * 128:(c + 1) * 128, 0:512], yt[p][:], deps=[pq])

        def run_interleaved(gens):
            gens = list(gens)
            while gens:
                for g_ in list(gens):
                    try:
                        next(g_)
                    except StopIteration:
                        gens.remove(g_)

        issue_loads(0)
        issue_loads(1)
        run_interleaved([front(0, 0, 0), front(0, NCH - 1, 1)])
        for i in range(NCH):
            if i + 2 < NCH:
                issue_loads(i + 2)
            gens = [heads(i, i, 0), heads(i, NCH - 1 - i, 1)]
            if i + 1 < NCH:
                gens += [front(i + 1, i + 1, 0), front(i + 1, NCH - 2 - i, 1)]
            run_interleaved(gens)
        S.finish()
    if STOP_AFTER == 'P2':
        return
    with contextlib.ExitStack() as st:
        sb = lambda name, shape, dt: st.enter_context(nc.sbuf_tensor("p3_" + name, shape, dt))
        pt = lambda name, shape, dt: st.enter_context(nc.psum_tensor("p3_" + name, shape, dt))
        S = Sched(nc, "p3", SEMS)
        zT = sb("zT", [33, NF], F32)
        w1s = sb("w1", [33, 64], F32)
        w23 = [sb("w2", [64, 64], BF16), sb("w3", [64, 64], BF16)]
        wst = sb("wst", [64, 2, 64], F32)
        sc = sb("sc", [64, 3], F32)
        bi = sb("bi", [64, 3], F32)
        bsT = sb("bsT", [64, 3], F32)
        hA = sb("hA", [64, NF], BF16)
        hB = sb("hB", [64, NF], BF16)
        NPH = 4
        tmpf = [sb(f"tmpf{i}", [64, 512], F32) for i in range(NPH)]
        tmpu = [sb(f"tmpu{i}", [64, 512], F32) for i in range(NPH)]
        pH = [pt(f"pH{i}", [64, 512], F32) for i in range(NPH)]
        t_z = ld(S, "sync", zT[:], I["zT"][:, :])
        t_w1 = ld(S, "sync", w1s[:], I["fw1"][:, :])
        t_bs = ld(S, "sync", bsT[:], I["fbs"][:, :])
        t_fq = ld(S, "sync", sc[:], I["ffq"][:, :])
        t_l2 = ld(S, "sync", wst[:, 0, :], I["fw2"][:, :])
        t_l3 = ld(S, "sync", wst[:, 1, :], I["fw3"][:, :])
        t_c2 = cp(S, "vector", w23[0][:], wst[:, 0, :], deps=[t_l2])
        t_c3 = cp(S, "vector", w23[1][:], wst[:, 1, :], deps=[t_l3])
        t_bi = S.op("vector", lambda e: e.tensor_tensor(out=bi[:], in0=bsT[:], in1=sc[:], op=ALU.mult), deps=[t_bs, t_fq])
        frH = [None] * NPH
        toks_prev = [t_z, t_w1, t_bi, t_fq, t_c2, t_c3]
        src = None
        for layer in range(3):
            dst = hA if layer % 2 == 0 else hB
            toks = []
            for ch in range(16):
                b = ch % NPH
                cs = slice(ch * 512, (ch + 1) * 512)
                if layer == 0:
                    tt = S.op("tensor", lambda e, b=b, cs=cs: e.matmul(pH[b][:], lhsT=w1s[:], rhs=zT[:, cs], start=True, stop=True),
                              deps=toks_prev + [frH[b]])
                else:
                    tt = S.op("tensor", lambda e, b=b, cs=cs, w=w23[layer - 1], src=src: e.matmul(
                        pH[b][:], lhsT=w[:], rhs=src[:, cs], start=True, stop=True), deps=toks_prev + [frH[b]])
                r1 = S.op("vector", lambda e, b=b, layer=layer: e.tensor_scalar(
                    out=tmpf[b][:], in0=pH[b][:], scalar1=sc[:, layer:layer + 1], scalar2=bi[:, layer:layer + 1],
                    op0=ALU.mult, op1=ALU.add), deps=[tt] + (toks[-NPH:len(toks) - NPH + 1] if len(toks) >= NPH else []))
                frH[b] = r1
                r2a = S.op("vector", lambda e, b=b: e.tensor_scalar(out=tmpu[b][:], in0=tmpf[b][:], scalar1=1.0 / TWO_PI, scalar2=MAGIC,
                                                                    op0=ALU.mult, op1=ALU.add), deps=[r1] + (toks[-NPH:len(toks) - NPH + 1] if len(toks) >= NPH else []))
                r2b = S.op("vector", lambda e, b=b: e.tensor_scalar(out=tmpu[b][:], in0=tmpu[b][:], scalar1=-MAGIC, scalar2=-TWO_PI,
                                                                    op0=ALU.add, op1=ALU.mult), deps=[r2a])
                r2 = S.op("vector", lambda e, b=b: e.tensor_add(out=tmpf[b][:], in0=tmpf[b][:], in1=tmpu[b][:]), deps=[r2b])
                r3 = S.op("scalar", lambda e, b=b, cs=cs, dst=dst: e.activation(out=dst[:, cs], in_=tmpf[b][:], func=AF.Sin,
                                                                                 scale=1.0 - 2e-6), deps=[r2] + toks_prev)
                toks.append(r3)
            toks_prev = toks
            src = dst
        ld(S, "sync", h3_d[:, :], src[:], deps=toks_prev)
        S.finish()
    if STOP_AFTER == 'P3':
        return

    with contextlib.ExitStack() as st:
        sb = lambda name, shape, dt: st.enter_context(nc.sbuf_tensor("p4_" + name, shape, dt))
        pt = lambda name, shape, dt: st.enter_context(nc.psum_tensor("p4_" + name, shape, dt))
        S = Sched(nc, "p4", SEMS)
        FAl = sb("FAl", [128, 32, 2, 128], BF16)
        FAh = sb("FAh", [128, 32, 2, 128], BF16)
        FAi = sb("FAi", [128, 32, 2, 128], BF16)
        FB = sb("FB", [128, 3, 128], BF16)
        FBi = sb("FBi", [128, 2, 256], BF16)
        ident = sb("ident", [128, 128], BF16)
        h3T = sb("h3T", [64, NF], BF16)
        w4b = sb("w4b", [64, 2048], BF16)
        w4l0b = sb("w4l0b", [64, 2, 512], BF16)
        wst = sb("wst", [64, 1024], F32)
        ntj = sb("ntj", [128, 2, 32], F32)
        sgs = sb("sg", [128, 2, 32], F32)
        dlb = sb("dlb", [128, 512], F32)
        fbr = sb("fbr", [1, 2, 512], F32)
        nwh = sb("nwh", [128, 512], F32)
        xin = [sb(f"xin{i}", [128, 32, CB], BF16) for i in range(3)]
        kern = [sb(f"kern{i}", [128, 32, CB], BF16) for i in range(2)]
        A_sb = sb("A_sb", [128, 2, NQ, 32, 4], BF16)
        At = sb("At", [128, 2, NQ, 128], BF16)
        Kf1 = sb("Kf", [128, 2, NQ, 128], BF16)
        Kf = [Kf1, Kf1]
        Xs2 = [sb(f"Xs{i}", [128, 2, 512], F32) for i in range(2)]
        t1_2 = [sb(f"t1_{i}", [128, 2, 512], F32) for i in range(2)]
        t2_2 = [sb(f"t2_{i}", [128, 2, 512], F32) for i in range(2)]
        win16 = [sb(f"win{i}", [128, 4, CB], F32) for i in range(2)]
        l0t = sb("l0t", [1, CB], F32)
        zc = sb("zc", [128, 4, CB], F32)
        sqc = sb("sqc", [128, 4, CB], F32)
        gss = sb("gss", [128, 4, 2], F32)
        pool32 = [pt(f"pb{i}", [128, 512], F32) for i in range(6)]
        pTn = [pt(f"pT{i}", [128, 4, 128], BF16) for i in range(2)]
        free32 = [None] * 6
        free16 = [None] * 2
        rr = dict(b32=0, b16=0, xs=0, win=0)

        def get32():
            i = rr["b32"] % 6
            rr["b32"] += 1
            return i

        def get16():
            k = rr["b16"] % 2
            rr["b16"] += 1
            return k, pTn[k][:]
        Gt = A_sb[:].rearrange("p i q r c -> p (i q r c)").rearrange("p (i r c) -> p i r c", i=2, r=32)

        cl = [ld(S, "sync", FAl[:], I["FA_lo"][:, :, :, :]), ld(S, "gpsimd", FAh[:], I["FA_hi"][:, :, :, :]),
              ld(S, "sync", FAi[:], I["FAi"][:, :, :, :]), ld(S, "sync", FB[:], I["FB"][:, :, :]),
              ld(S, "sync", FBi[:], I["FBi"][:, :, :]), ld(S, "sync", ident[:], I["ident"][:, :]),
              ld(S, "gpsimd", h3T[:], h3_d[:, :])]
        t_ntj = ld(S, "sync", ntj[:], I["ntj"].rearrange("(h p) r -> p h r", p=128))
        t_sg = ld(S, "sync", sgs[:], I["sg"].rearrange("(h p) r -> p h r", p=128))
        t_dl = ld(S, "sync", dlb[:], I["deltas"].partition_broadcast(128))
        t_fb = ld(S, "sync", fbr[:], I["fb"].rearrange("(a o) c -> a o c", a=1))
        t_nw = ld(S, "sync", nwh[:], I["nw_h"].partition_broadcast(128))
        t_l4a = ld(S, "sync", wst[:], I["fw4"][:, 0:1024])
        t_w4a = cp(S, "vector", w4b[:, 0:1024], wst[:], deps=[t_l4a])
        t_l4b = ld(S, "sync", wst[:], I["fw4"][:, 1024:2048], deps=[t_w4a])
        t_w4 = cp(S, "vector", w4b[:, 1024:2048], wst[:], deps=[t_l4b])
        t_l40 = ld(S, "sync", wst[:], I["w4l0"].rearrange("k o c -> k (o c)"), deps=[t_w4])
        t_w40 = cp(S, "vector", w4l0b[:].rearrange("k o c -> k (o c)"), wst[:], deps=[t_l40])
        t_neg = S.op("vector", lambda e: e.tensor_scalar(out=w4b[:, 1024:2048], in0=w4b[:, 1024:2048], scalar1=-1.0, scalar2=None,
                                                         op0=ALU.mult), deps=[t_w4])
        cdeps = cl + [t_w4a, t_w4, t_w40, t_neg]
        xs_free = [[], []]
        win_free = [None] * 2
        ytok_v = ytok_d.rearrange("(m r) c -> m r c", r=32)
        cvh_v = cvh_d

        def fwd_transform(srcs, deps, on_group):
            evA = []
            for rp in range(16):
                i = get32()
                pa = pool32[i][:].rearrange("p (a b c) -> p a b c", a=2, b=2)
                tt = None
                for r2 in range(2):
                    r = rp * 2 + r2
                    for ri in range(2):
                        for si_, (FAt, dat) in enumerate(srcs):
                            tt = S.op("tensor", lambda e, pa=pa, r=r, r2=r2, ri=ri, FAt=FAt, dat=dat, si_=si_: e.matmul(
                                pa[:, r2, ri, :], lhsT=FAt[:, r, ri, :], rhs=dat[:, r, :], start=(si_ == 0),
                                stop=(si_ == len(srcs) - 1)), deps=deps + cdeps + [free32[i]])
                ev = None
                for r2 in range(2):
                    ev = cp(S, "vector" if (rp + r2) % 2 == 0 else "scalar", A_sb[:, :, :, rp * 2 + r2, :],
                            pa[:, r2, :, :].rearrange("p i (q c) -> p i q c", c=4), deps=[tt, ev] + deps)
                    evA.append(ev)
                free32[i] = ev
            evT = []
            n_ = 0
            for ri in range(2):
                for qg in range(NQ // 4):
                    k, ptile = get16()
                    tt = None
                    for j in range(4):
                        q = qg * 4 + j
                        tt = S.op("tensor", lambda e, ptile=ptile, ri=ri, q=q, j=j: e.transpose(
                            out=ptile[:, j, :], in_=A_sb[:, ri, q, :, :].rearrange("p r c -> p (r c)"),
                            identity=ident[:]), deps=evA + [free16[k]])
                    ev = cp(S, "vector" if n_ % 2 == 0 else "scalar", At[:, ri, qg * 4:qg * 4 + 4, :], ptile, deps=[tt] + deps)
                    n_ += 1
                    free16[k] = ev
                    evT.append(ev)
            outs = []
            for qg in range(NQ // 4):
                qs = slice(qg * 4, qg * 4 + 4)
                rhs_r = At[:, 0, qs, :].rearrange("p q k -> p (q k)")
                rhs_i = At[:, 1, qs, :].rearrange("p q k -> p (q k)")
                i0, i1 = get32(), get32()
                b0, b1 = pool32[i0], pool32[i1]
                S.op("tensor", lambda e, rhs_r=rhs_r, b0=b0: e.matmul(b0[:], lhsT=FB[:, 0, :], rhs=rhs_r, start=True, stop=False),
                     deps=[evT[qg], evT[NQ // 4 + qg], free32[i0]])
                tr = S.op("tensor", lambda e, rhs_i=rhs_i, b0=b0: e.matmul(b0[:], lhsT=FB[:, 2, :], rhs=rhs_i, start=False, stop=True))
                S.op("tensor", lambda e, rhs_r=rhs_r, b1=b1: e.matmul(b1[:], lhsT=FB[:, 1, :], rhs=rhs_r, start=True, stop=False),
                     deps=[free32[i1]])
                ti = S.op("tensor", lambda e, rhs_i=rhs_i, b1=b1: e.matmul(b1[:], lhsT=FB[:, 0, :], rhs=rhs_i, start=False, stop=True))
                o_, f0, f1 = on_group(qg, qs, b0, b1, tr, ti)
                free32[i0], free32[i1] = f0, f1
                outs += o_
            return outs

        def make_kernel(blk, order, deps):
            tk_all = []
            for half in range(2):
                col0 = half * 1024 + order * 512 + blk * CB
                for r4 in range(8):
                    i = get32()
                    pk4 = pool32[i][:].rearrange("p (j c) -> p j c", j=4)
                    wi = rr["win"] % 2
                    rr["win"] += 1
                    tt = None
                    tws = []
                    for jj in range(4):
                        r = r4 * 4 + jj
                        tt = S.op("tensor", lambda e, pk4=pk4, jj=jj, half=half, r=r, col0=col0: e.matmul(
                            pk4[:, jj, :], lhsT=h3T[:, bass.ds(half * 4096 + r, 128, step=32)], rhs=w4b[:, col0:col0 + CB],
                            start=True, stop=True), deps=cdeps + deps + [free32[i]])
                        tws.append(S.op("scalar", lambda e, half=half, r=r, wi=wi, jj=jj: e.activation(
                            out=win16[wi][:, jj, :], in_=dlb[:, blk * CB:(blk + 1) * CB], func=AF.Exp, scale=ntj[:, half, r:r + 1]),
                            deps=[t_dl, t_ntj, win_free[wi]]))
                    tk = S.op("vector", lambda e, pk4=pk4, half=half, r4=r4, wi=wi: e.tensor_tensor(
                        out=kern[half][:, r4 * 4:r4 * 4 + 4, :], in0=pk4, in1=win16[wi][:], op=ALU.mult), deps=[tt] + tws + deps)
                    free32[i] = tk
                    win_free[wi] = tk
                    tk_all.append(tk)
            tz0 = S.op("vector", lambda e: e.memset(kern[1][0:1, 0, :], 0.0), deps=tk_all)
            tk_all.append(tz0)
            i = get32()
            p0 = pool32[i][0:1, 0:CB]
            t0 = S.op("tensor", lambda e: e.matmul(p0, lhsT=h3T[:, 0:1], rhs=w4l0b[:, order, blk * CB:(blk + 1) * CB],
                                                   start=True, stop=True), deps=cdeps + [free32[i]])
            t0a = S.op("vector", lambda e: e.tensor_add(out=l0t[:], in0=p0, in1=fbr[0:1, order, blk * CB:(blk + 1) * CB]),
                       deps=[t0, t_fb] + tk_all[-1:])
            free32[i] = t0a
            t0b = S.op("vector", lambda e: e.tensor_copy(out=kern[0][0:1, 0, :], in_=l0t[:]), deps=[t0a] + tk_all)

            def on_group(qg, qs, b0, b1, tr, ti):
                a = cp(S, "scalar", Kf[order][:, 0, qs, :].rearrange("p q k -> p (q k)"), b0[:], deps=[tr] + deps)
                b_ = cp(S, "vector", Kf[order][:, 1, qs, :].rearrange("p q k -> p (q k)"), b1[:], deps=[ti] + deps)
                return [a, b_], a, b_
            return fwd_transform([(FAl, kern[0]), (FAh, kern[1])], tk_all + [t0b], on_group)

        def conv(x_tile, Kft, deps, on_out):
            def on_group(qg, qs, b0, b1, tr, ti):
                xi_ = rr["xs"] % 2
                rr["xs"] += 1
                Xs, t1, t2 = Xs2[xi_], t1_2[xi_], t2_2[xi_]
                xr = cp(S, "scalar", Xs[:, 0, :], b0[:], deps=[tr] + xs_free[xi_])
                xi = cp(S, "scalar", Xs[:, 1, :], b1[:], deps=[ti])
                kr = Kft[:, 0, qs, :].rearrange("p q k -> p (q k)")
                ki = Kft[:, 1, qs, :].rearrange("p q k -> p (q k)")
                yr = At[:, 0, qs, :].rearrange("p q k -> p (q k)")
                yi = At[:, 1, qs, :].rearrange("p q k -> p (q k)")
                a1 = S.op("vector", lambda e: e.tensor_mul(out=t1[:, 0, :], in0=Xs[:, 0, :], in1=kr), deps=[xr] + deps)
                a2 = S.op("vector", lambda e: e.tensor_mul(out=t1[:, 1, :], in0=Xs[:, 1, :], in1=ki), deps=[xi])
                a3 = S.op("vector", lambda e: e.tensor_sub(out=yr, in0=t1[:, 0, :], in1=t1[:, 1, :]), deps=[a1, a2, tr, ti])
                b1_ = S.op("gpsimd", lambda e: e.tensor_mul(out=t2[:, 0, :], in0=Xs[:, 0, :], in1=ki), deps=[xr] + deps)
                b2_ = S.op("gpsimd", lambda e: e.tensor_mul(out=t2[:, 1, :], in0=Xs[:, 1, :], in1=kr), deps=[xi])
                b3_ = S.op("gpsimd", lambda e: e.tensor_add(out=yi, in0=t2[:, 0, :], in1=t2[:, 1, :]), deps=[b1_, b2_, tr, ti])
                xs_free[xi_] = [a3, b3_]
                return [a3, b3_], xr, xi
            evY = fwd_transform([(FAl, x_tile)], deps, on_group)
            evG = []
            for qp in range(NQ // 2):
                i = get32()
                pg = pool32[i][:].rearrange("p (j n) -> p j n", j=2)
                tt = None
                for j in range(2):
                    q = qp * 2 + j
                    S.op("tensor", lambda e, pg=pg, q=q, j=j: e.matmul(pg[:, j, :], lhsT=At[:, 0, q, :], rhs=FBi[:, 0, :], start=True, stop=False),
                         deps=evY[2 * (q // 4):2 * (q // 4) + 2] + [free32[i]])
                    tt = S.op("tensor", lambda e, pg=pg, q=q, j=j: e.matmul(pg[:, j, :], lhsT=At[:, 1, q, :], rhs=FBi[:, 1, :], start=False, stop=True))
                srcv = pg.rearrange("p j (i r c) -> p j i r c", i=2, r=32)
                dstv = Gt[:, :, :, qp * 8:qp * 8 + 8].rearrange("p i r (j c) -> p j i r c", j=2)
                ev = cp(S, "vector" if qp % 2 == 0 else "scalar", dstv, srcv, deps=[tt] + deps)
                free32[i] = ev
                evG.append(ev)
            outs = []
            for rg in range(8):
                i = get32()
                po = pool32[i][:].rearrange("p (j n) -> p j n", j=4)
                tt = None
                for j in range(4):
                    r = rg * 4 + j
                    S.op("tensor", lambda e, po=po, r=r, j=j: e.matmul(po[:, j, :], lhsT=FAi[:, r, 0, :], rhs=Gt[:, 0, r, :], start=True, stop=False),
                         deps=evG + [free32[i]])
                    tt = S.op("tensor", lambda e, po=po, r=r, j=j: e.matmul(po[:, j, :], lhsT=FAi[:, r, 1, :], rhs=Gt[:, 1, r, :], start=False, stop=True))
                ev, ftok = on_out(rg, tt, po)
                free32[i] = ftok
                outs += ev
            return outs

        blk_done = []
        for blk in range(4):
            tl = [ld(S, "sync", xin[g][:], cvh_v[:, :, g * 512 + blk * CB:g * 512 + (blk + 1) * CB], deps=blk_done) for g in range(3)]
            tK0 = make_kernel(blk, 0, blk_done)
            z1 = xin[0]
            yh = xin[1]

            def out1(rg, tt, po):
                o = S.op("vector", lambda e: e.tensor_tensor(out=z1[:, rg * 4:rg * 4 + 4, :], in0=po, in1=xin[1][:, rg * 4:rg * 4 + 4, :],
                                                             op=ALU.mult), deps=[tt] + tK0 + tl)
                return [o], o
            tz1 = conv(xin[0], Kf[0], tl + tK0, out1)
            tK1 = make_kernel(blk, 1, tz1)

            def out2(rg, tt, po):
                o1 = S.op("vector", lambda e: e.tensor_tensor(out=zc[:], in0=po, in1=xin[2][:, rg * 4:rg * 4 + 4, :], op=ALU.mult),
                          deps=[tt] + out2.prev)
                o2 = S.op("scalar", lambda e: e.activation(out=sqc[:], in_=zc[:], func=AF.Square), deps=[o1])
                o3 = S.op("vector", lambda e: e.reduce_sum(out=gss[:], in_=sqc[:].rearrange("p r (g c) -> p r g c", g=2), axis=AX.X), deps=[o2])
                o4 = S.op("scalar", lambda e: e.activation(out=gss[:], in_=gss[:], func=AF.Ln, scale=1.0 / 64, bias=EPS), deps=[o3])
                o5 = S.op("scalar", lambda e: e.activation(out=gss[:], in_=gss[:], func=AF.Exp, scale=-0.5), deps=[o4])
                o6 = S.op("vector", lambda e: e.tensor_tensor(
                    out=zc[:].rearrange("p r (g c) -> p r g c", g=2), in0=zc[:].rearrange("p r (g c) -> p r g c", g=2),
                    in1=gss[:].unsqueeze(3).to_broadcast([128, 4, 2, 64]), op=ALU.mult), deps=[o5])
                o7 = S.op("vector", lambda e, blk=blk: e.tensor_tensor(
                    out=yh[:, rg * 4:rg * 4 + 4, :], in0=zc[:], in1=nwh[:, blk * CB:(blk + 1) * CB].unsqueeze(1).to_broadcast([128, 4, CB]),
                    op=ALU.mult), deps=[o6, t_nw])
                out2.prev = [o7]
                return [o7], o1
            out2.prev = []
            ty = conv(z1, Kf[1], tz1 + tK1, out2)
            t_st = ld(S, "sync", ytok_v[:, :, 512 + blk * CB:512 + (blk + 1) * CB], yh[:], deps=ty)
            blk_done = [t_st] + ty
        S.finish()
    if STOP_AFTER == 'P4':
        return

    with contextlib.ExitStack() as st:
        sb = lambda name, shape, dt: st.enter_context(nc.sbuf_tensor("p5_" + name, shape, dt))
        pt = lambda name, shape, dt: st.enter_context(nc.psum_tensor("p5_" + name, shape, dt))
        S = Sched(nc, "p5", SEMS)
        wo_b = sb("wo_b", [128, 8, D], BF16)
        w1_b = sb("w1_b", [128, 8, DFF], BF16)
        w2_b = sb("w2_b", [128, 32, D], BF16)
        stg = [sb(f"stg{i}", [128, 1024], F32) for i in range(2)]
        gb = sb("gb", [128, 3, D], F32)
        ident = sb("ident", [128, 128], BF16)
        ytile = sb("ytile", [128, D], BF16)
        yT = sb("yT", [128, 8, 128], BF16)
        xt = sb("xt", [128, D], F32)
        x1 = sb("x1", [128, D], F32)
        tmp = sb("tmp", [128, D], F32)
        sq = sb("sq", [128, D], F32)
        hmb = sb("hmb", [128, D], BF16)
        hmT = sb("hmT", [128, 8, 128], BF16)
        aT = sb("aT", [128, 32, 128], BF16)
        st_ = sb("stat", [128, 8], F32)
        pmm = [pt(f"pmm{i}", [128, 512], F32) for i in range(4)]
        ptr = [pt(f"ptr{i}", [128, 4, 128], BF16) for i in range(2)]
        t_g = ld(S, "sync", gb[:], I["gains"][1:4].partition_broadcast(128))
        t_id = ld(S, "sync", ident[:], I["ident"][:, :])
        y_view = ytok_d.rearrange("(m r) c -> r m c", r=32)
        x_view = I["xd"].rearrange("(m r) d -> r m d", r=32)
        t_x = ld(S, "sync", xt[:], x_view[0])
        t_yl = ld(S, "sync", ytile[:], y_view[0])
        stg_free = [None, None]
        si = [0]

        w_ready = {"wo": [[]], "w1": [[], [], [], []], "w2": [[]]}

        def emit_unit(name, blk_i, dst, src, kcs, c0):
            for kc in kcs:
                b = si[0] % 2
                si[0] += 1
                t_l = ld(S, "gpsimd" if b else "sync", stg[b][:], src[kc * 128:(kc + 1) * 128, c0:c0 + 1024], deps=[stg_free[b]])
                t_c = cp(S, "vector" if b else "scalar", dst[:, kc, c0:c0 + 1024], stg[b][:], deps=[t_l])
                stg_free[b] = t_c
                w_ready[name][blk_i].append(t_c)
        units = [lambda: emit_unit("wo", 0, wo_b, I["w_out"], range(8), 0)]
        units += [lambda cb=cb: emit_unit("w1", cb, w1_b, I["w1"], range(8), cb * 1024) for cb in range(4)]
        units += [lambda q=q: emit_unit("w2", 0, w2_b, I["w2"], range(q * 8, q * 8 + 8), 0) for q in range(4)]
        pending = list(units)

        def emit_units(n):
            for _ in range(n):
                if pending:
                    pending.pop(0)()
        emit_units(2)

        pmo = [pt(f"pmo{i}", [128, 512], F32) for i in range(2)]
        tmpF = stg[0]
        x1b = [x1, stg[1]]

        def rms_scale(src_ap, col, deps, junk):
            t1_ = S.op("scalar", lambda e: e.activation(out=junk, in_=src_ap, func=AF.Square, accum_out=st_[:, col:col + 1]), deps=deps)
            t2_ = S.op("scalar", lambda e: e.activation(out=st_[:, col:col + 1], in_=st_[:, col:col + 1], func=AF.Ln, scale=1.0 / D, bias=EPS),
                       deps=[t1_])
            return S.op("scalar", lambda e: e.activation(out=st_[:, col:col + 1], in_=st_[:, col:col + 1], func=AF.Exp, scale=-0.5), deps=[t2_])

        pm_free = [None] * 4
        pmo_free = [None, None]
        ptr_free = [None, None]
        T = dict(t_x=t_x, t_yl=t_yl, y_done=None, last=None, mo_done=None)
        fr1, fr2 = {}, {}

        def front1(r):
            tcy = []
            tt = None
            for half in range(2):
                for j in range(4):
                    c = half * 4 + j
                    tt = S.op("tensor", lambda e, c=c, j=j, half=half: e.transpose(
                        out=ptr[half][:, j, :], in_=ytile[:, c * 128:(c + 1) * 128], identity=ident[:]), deps=[T["t_yl"], t_id, ptr_free[half]])
                tc_ = cp(S, "scalar", yT[:, half * 4:half * 4 + 4, :], ptr[half][:], deps=[tt])
                ptr_free[half] = tc_
                tcy.append(tc_)
            T["y_done"] = tt
            ev = []
            for nb in range(2):
                tt = None
                for kc in range(8):
                    tt = S.op("tensor", lambda e, nb=nb, kc=kc: e.matmul(
                        pmo[nb][:], lhsT=yT[:, kc, :], rhs=wo_b[:, kc, nb * 512:(nb + 1) * 512], start=(kc == 0), stop=(kc == 7)),
                        deps=tcy + [pmo_free[nb]] + w_ready["wo"][0])
                ev.append(cp(S, "vector", tmpF[:, nb * 512:(nb + 1) * 512], pmo[nb][:], deps=[tt, stg_free[0]] + fr2.get(r - 1, [])))
                pmo_free[nb] = ev[-1]
            fr1[r] = ev

        def front2(r):
            p = r % 2
            X1 = x1b[p]
            c0 = 4 * p
            t_r = rms_scale(tmpF[:], c0, fr1[r], hmb[:])
            t_a = S.op("vector", lambda e: e.scalar_tensor_tensor(out=tmpF[:], in0=tmpF[:], scalar=st_[:, c0:c0 + 1], in1=gb[:, 0, :],
                                                                  op0=ALU.mult, op1=ALU.mult), deps=[t_r, t_g])
            t_x1 = S.op("vector", lambda e: e.tensor_add(out=X1[:], in0=tmpF[:], in1=xt[:]),
                        deps=[t_a, T["t_x"], stg_free[1]] + fr2.get(("fin", r - 2), []))
            if r + 1 < 16:
                T["t_x"] = ld(S, "sync", xt[:], x_view[r + 1], deps=[t_x1])
                T["t_yl"] = ld(S, "sync", ytile[:], y_view[r + 1], deps=[T["y_done"]])
            t_r2 = rms_scale(X1[:], c0 + 1, [t_x1], hmb[:])
            t_h = S.op("vector", lambda e: e.scalar_tensor_tensor(out=hmb[:], in0=X1[:], scalar=st_[:, c0 + 1:c0 + 2], in1=gb[:, 1, :],
                                                                  op0=ALU.mult, op1=ALU.mult), deps=[t_r2])
            tcp = []
            for half in range(2):
                tt = None
                for j in range(4):
                    c = half * 4 + j
                    tt = S.op("tensor", lambda e, c=c, j=j, half=half: e.transpose(
                        out=ptr[half][:, j, :], in_=hmb[:, c * 128:(c + 1) * 128], identity=ident[:]), deps=[t_h, ptr_free[half]])
                tc_ = cp(S, "scalar", hmT[:, half * 4:half * 4 + 4, :], ptr[half][:], deps=[tt])
                ptr_free[half] = tc_
                tcp.append(tc_)
            fr2[r] = [t_x1]
            stg_free[0] = t_x1
            T[("tcp", r)] = tcp

        def mlp_in(r):
            tcp = T[("tcp", r)]
            t_act = []
            for fg in range(8):
                if r == 0 and fg % 2 == 0:
                    emit_units(1 if fg < 6 else 2)
                pb = 2 + (fg % 2)
                tt = None
                for fj in range(4):
                    f = fg * 4 + fj
                    for kc in range(8):
                        tt = S.op("tensor", lambda e, f=f, fj=fj, kc=kc, pb=pb: e.matmul(
                            pmm[pb][:, fj * 128:(fj + 1) * 128], lhsT=w1_b[:, kc, f * 128:(f + 1) * 128], rhs=hmT[:, kc, :],
                            start=(kc == 0), stop=(kc == 7)), deps=tcp + [pm_free[pb]] + w_ready["w1"][f // 8])
                hs = slice((fg % 2) * 512, (fg % 2 + 1) * 512)
                t_rl = S.op("scalar", lambda e, pb=pb, hs=hs: e.activation(out=sq[:, hs], in_=pmm[pb][:], func=AF.Relu),
                            deps=[tt, T["last"]] + t_act[-2:])
                pm_free[pb] = t_rl
                t_sq = S.op("vector", lambda e, fg=fg, hs=hs: e.tensor_tensor(out=aT[:, fg * 4:fg * 4 + 4, :], in0=sq[:, hs], in1=sq[:, hs],
                                                                               op=ALU.mult), deps=[t_rl])
                t_act.append(t_sq)
            if r == 0:
                emit_units(len(pending))
            T[("act", r)] = t_act

        def mlp_out_mm(r):
            t_act = T[("act", r)]
            tts = []
            for nb in range(2):
                tt = None
                for kc in range(32):
                    tt = S.op("tensor", lambda e, nb=nb, kc=kc: e.matmul(
                        pmm[nb][:], lhsT=aT[:, kc, :], rhs=w2_b[:, kc, nb * 512:(nb + 1) * 512], start=(kc == 0), stop=(kc == 31)),
                        deps=t_act + [pm_free[nb]] + w_ready["w2"][0])
                tts.append(tt)
            T[("mo", r)] = tts

        def final(r):
            p = r % 2
            X1 = x1b[p]
            c0 = 4 * p
            ev = []
            for nb in range(2):
                e_ = cp(S, "vector", tmp[:, nb * 512:(nb + 1) * 512], pmm[nb][:], deps=[T[("mo", r)][nb], T["last"]])
                pm_free[nb] = e_
                ev.append(e_)
            t_r3 = rms_scale(tmp[:], c0 + 2, ev + T[("act", r)], sq[:])
            t_b = S.op("vector", lambda e: e.scalar_tensor_tensor(out=tmp[:], in0=tmp[:], scalar=st_[:, c0 + 2:c0 + 3], in1=gb[:, 2, :],
                                                                  op0=ALU.mult, op1=ALU.mult), deps=[t_r3])
            t_o = S.op("vector", lambda e: e.tensor_add(out=sq[:], in0=tmp[:], in1=X1[:]), deps=[t_b])
            T["last"] = ld(S, "sync", out_d[r], sq[:], deps=[t_o])
            fr2[("fin", r)] = [t_o]

        front1(0)
        front2(0)
        for r in range(16):
            mlp_in(r)
            if r + 1 < 16:
                front1(r + 1)
            mlp_out_mm(r)
            if r + 1 < 16:
                front2(r + 1)
            final(r)
        S.finish()


def _prep_core(inp, b, h, consts):
    rev = (h == 1)
    x = np.asarray(inp["x"][b], dtype=np.float32)
    w_in = np.asarray(inp["w_in"], dtype=np.float32)
    bg = np.asarray(inp["b_gates"], dtype=np.float32)
    conv_w = np.asarray(inp["conv_w"], dtype=np.float32)
    w4 = np.asarray(inp["filt_w4"], dtype=np.float32).reshape(64, 2, 2, 512)
    w4l0 = np.ascontiguousarray(w4[:, 0])
    if rev:
        x = x[::-1]
        w_in = w_in.copy()
        w_in[:, OFF_GATE:OFF_GATE + 8], w_in[:, OFF_GATE + 8:OFF_GATE + 16] = (
            np.asarray(inp["w_in"])[:, OFF_GATE + 8:OFF_GATE + 16], np.asarray(inp["w_in"])[:, OFF_GATE:OFF_GATE + 8])
        bg = np.concatenate([bg[8:], bg[:8]])
        conv_w = conv_w[::-1]
        w4 = w4[:, ::-1]
    conv_b = np.asarray(inp["conv_b"], dtype=np.float32)
    cw_all = np.concatenate([conv_w, conv_b[None]], axis=0)
    d = dict(consts)
    d.update(
        xd=np.ascontiguousarray(x), w_in=np.ascontiguousarray(w_in),
        gains=np.ascontiguousarray(np.stack([inp["norm_mix_pre"], inp["norm_mix_post"], inp["norm_mlp_pre"],
                                             inp["norm_mlp_post"]]).astype(np.float32)),
        bg=np.ascontiguousarray(bg), cw_qk=np.ascontiguousarray(cw_all[:, 0:1024].T), cw_h=np.ascontiguousarray(cw_all[:, 1024:2560]),
        nw_m=np.asarray(inp["mlstm_norm_w"], np.float32), nw_h=np.asarray(inp["hyena_norm_w"], np.float32),
        fw1=np.asarray(inp["filt_w1"], np.float32), fw2=np.asarray(inp["filt_w2"], np.float32),
        fw3=np.asarray(inp["filt_w3"], np.float32), fw4=np.ascontiguousarray(w4.reshape(64, 2048)), w4l0=w4l0,
        fbs=np.ascontiguousarray(np.stack([inp["filt_b1"], inp["filt_b2"], inp["filt_b3"]]).astype(np.float32).T),
        ffq=np.ascontiguousarray(np.asarray(inp["filt_freq"], np.float32).T), fb=np.asarray(inp["filt_bias"], np.float32),
        w_out=np.asarray(inp["w_out"], np.float32), w1=np.asarray(inp["w_mlp_in"], np.float32),
        w2=np.asarray(inp["w_mlp_out"], np.float32))
    return d


_CACHE = {}


def kernel(**inputs):
    if "nc" not in _CACHE:
        _CACHE["consts"] = _tables()
        _CACHE["nc"] = build_nc()
    consts, nc = _CACHE["consts"], _CACHE["nc"]
    maps = [_prep_core(inputs, c // 2, c % 2, consts) for c in range(8)]
    res = run_bass_kernel_spmd(nc, maps, core_ids=list(range(8)))
    out = np.empty((4, SEQ, D), np.float32)
    for c in range(8):
        b, h = c // 2, c % 2
        o = np.asarray(res.results[c]["out_d"], dtype=np.float32)
        tdev = (32 * np.arange(128)[None, :] + np.arange(16)[:, None]).reshape(-1)
        torig = tdev if h == 0 else (SEQ - 1 - tdev)
        out[b, torig] = o.reshape(16 * 128, D)
    return out
```

```python
import contextlib
import math

import ml_dtypes
import numpy as np

import concourse.bass as bass
import concourse.mybir as mybir
from concourse.bass_utils import run_bass_kernel_spmd

F32 = mybir.dt.float32
BF16 = mybir.dt.bfloat16
AF = mybir.ActivationFunctionType
ALU = mybir.AluOpType
AX = mybir.AxisListType
BF = ml_dtypes.bfloat16

D = 1024
SEQ = 4096
NCH = 32
NH = 4
DFF = 4096
N_IN = 3600
OFF_MQ, OFF_MK, OFF_HV, OFF_HX1, OFF_HX2, OFF_MV, OFF_MO, OFF_GATE = 0, 512, 1024, 1536, 2048, 2560, 3072, 3584
EPS = 1e-6
NEG = -30000.0
NF, NF1, NF2, HK = 8192, 256, 32, 128
CB, NQ = 128, 32
TWO_PI = 2 * math.pi
MAGIC = 12582912.0
ENGS = ("sync", "scalar", "vector", "gpsimd", "tensor")
DMA_ENGS = ("sync", "gpsimd")
DEBUG_SCRATCH = False
STOP_AFTER = None
P2_ITERS = NCH_DEFAULT = 32
P2_FLUSH = 1000
P2_POST = True


class Tok:
    __slots__ = ("eng", "idx", "sem", "val", "gen")

    def __init__(self, gen, eng=None, idx=None, sem=None, val=None):
        self.gen, self.eng, self.idx, self.sem, self.val = gen, eng, idx, sem, val


def sem_names(n_dma_sems):
    return [f"c_{e}" for e in ENGS] + [f"d_{e}_{i}" for e in DMA_ENGS for i in range(n_dma_sems)]


class SemSet:
    BUDGET = 96

    def __init__(self, n_dma_sems=4):
        self.n_dma = n_dma_sems
        self.used = 0


class Sched:
    def __init__(self, nc, tag, semset):
        self.nc, self.tag, self.n_dma, self.semset = nc, tag, semset.n_dma, semset
        self.gen = 0
        self.max_seen = 0
        self._reset()

    def _reset(self):
        self.ops = {e: [] for e in ENGS}
        self.dma_i = {e: 0 for e in DMA_ENGS}

    def _live(self, deps):
        return [d for d in deps if d is not None and d.gen == self.gen]

    def op(self, eng, fn, deps=()):
        self.ops[eng].append([fn, self._live(deps), None])
        return Tok(self.gen, eng=eng, idx=len(self.ops[eng]) - 1)

    def dma(self, eng, fn, deps=()):
        i = self.dma_i[eng]
        self.dma_i[eng] += 1
        slot, rnd = i % self.n_dma, i // self.n_dma
        key = f"d_{eng}_{slot}"
        deps = self._live(deps)
        if rnd > 0:
            deps.append(Tok(self.gen, sem=key, val=16 * rnd))
        self.ops[eng].append([fn, deps, key])
        return Tok(self.gen, sem=key, val=16 * (rnd + 1))

    def flush(self):
        if not any(self.ops[e] for e in ENGS):
            return
        nc = self.nc
        waited = {e: set() for e in ENGS}
        for e in ENGS:
            for _, deps, _ in self.ops[e]:
                for d in deps:
                    if d.sem is None:
                        waited[d.eng].add(d.idx)
        value = {}
        for e in ENGS:
            n = 0
            for i in range(len(self.ops[e])):
                if i in waited[e]:
                    n += 1
                    value[(e, i)] = n
        names = sem_names(self.n_dma)
        self.semset.used += len(names)
        assert self.semset.used <= SemSet.BUDGET, f"semaphore ID budget exceeded ({self.semset.used}); IDs would be recycled stale"
        pre = f"{self.tag}g{self.gen}_"
        with contextlib.ExitStack() as st:
            sems = {k: st.enter_context(nc.semaphore(pre + k)) for k in names}
            block = st.enter_context(nc.Block())

            def make(eng):
                def body(e):
                    seen = {}
                    for i, (fn, deps, dkey) in enumerate(self.ops[eng]):
                        for d in deps:
                            k, v = (d.sem, d.val) if d.sem is not None else (f"c_{d.eng}", value[(d.eng, d.idx)])
                            if seen.get(k, 0) < v:
                                e.wait_ge(sems[k], v)
                                seen[k] = v
                        ins = fn(e)
                        if dkey is not None:
                            ins.then_inc(sems[dkey], 16)
                        elif (eng, i) in value:
                            ins.then_inc(sems[f"c_{eng}"], 1)
                    if eng in DMA_ENGS:
                        n = self.dma_i[eng]
                        for slot in range(min(n, self.n_dma)):
                            rounds = (n - slot + self.n_dma - 1) // self.n_dma
                            e.wait_ge(sems[f"d_{eng}_{slot}"], 16 * rounds)
                return body
            for eng in ENGS:
                if self.ops[eng]:
                    getattr(block, eng)(make(eng))
        mx = max([0] + list(value.values()) + [16 * ((self.dma_i[e] + self.n_dma - 1) // self.n_dma) for e in DMA_ENGS])
        self.max_seen = max(self.max_seen, mx)
        self.gen += 1
        self._reset()

    def finish(self):
        self.flush()
        print(f"[build] {self.tag}: {self.gen} block(s), max semaphore value {self.max_seen}", flush=True)


def _tables():
    m = np.arange(NF1)[:, None, None]
    r = np.arange(NF2)[None, :, None]
    k1 = np.arange(HK)[None, None, :]
    th = 2 * np.pi * ((NF2 * m + r) * (2 * k1 + 1) % (2 * NF)) / (2 * NF)
    FA = np.stack([np.cos(th), -np.sin(th)], axis=2)
    thT = np.transpose(th[:128], (2, 1, 0))
    FAi = np.stack([np.cos(thT), -np.sin(thT)], axis=2) * (2.0 / NF)
    rr = np.arange(NF2)[:, None]
    k2 = np.arange(NF2)[None, :]
    tb = 2 * np.pi * ((rr * k2) % NF2) / NF2
    Br, Bi = np.cos(tb), -np.sin(tb)
    bd = lambda a: np.kron(a, np.eye(4))
    FB = np.stack([bd(Br), bd(Bi), bd(-Bi)], axis=1)
    Cr, Ci = bd(Br.T), bd(-Bi.T)
    FBi = np.stack([np.concatenate([Cr, Ci], 1), np.concatenate([-Ci, Cr], 1)], axis=1)
    n = np.arange(NF)
    j = np.where(n < SEQ, n, NF - n).astype(np.float64)
    j[SEQ] = 0
    t = j / (SEQ - 1)
    ang = TWO_PI * j / SEQ
    f = np.linspace(1e-4, 15, 16)
    z = np.concatenate([t[:, None], np.cos(f[None] * ang[:, None]), -np.sin(f[None] * ang[:, None])], axis=1)
    sg = np.where(n < SEQ, 1.0, -1.0)
    sg[SEQ] = 0.0
    deltas = np.abs(np.linspace(math.log(1e-2) / 1.5, math.log(1e-2) / 0.3, 512))
    idx = np.arange(128)
    triF = (idx[:, None] <= idx[None, :]).astype(np.float32)
    triB = (idx[:, None] >= idx[None, :]).astype(np.float32)
    maskF = np.where(idx[:, None] <= idx[None, :], 0.0, NEG).astype(np.float32)
    maskB = np.where(idx[:, None] >= idx[None, :], 0.0, NEG).astype(np.float32)
    return dict(
        FA_lo=FA[:128].astype(BF), FA_hi=FA[128:].astype(BF), FAi=FAi.astype(BF), FB=FB.astype(BF), FBi=FBi.astype(BF),
        zT=np.ascontiguousarray(z.T).astype(np.float32), ntj=(-t.reshape(256, 32)).astype(np.float32),
        sg=sg.reshape(256, 32).astype(np.float32), deltas=deltas.astype(np.float32),
        cst=np.ascontiguousarray(np.stack([triF, triB, maskF, maskB], axis=1)),
        ident=np.eye(128).astype(BF), identf=np.eye(128, dtype=np.float32))


CONST_SPECS = dict(FA_lo=([128, 32, 2, 128], BF16), FA_hi=([128, 32, 2, 128], BF16), FAi=([128, 32, 2, 128], BF16),
                   FB=([128, 3, 128], BF16), FBi=([128, 2, 256], BF16), zT=([33, NF], F32), ntj=([256, 32], F32),
                   sg=([256, 32], F32), deltas=([512], F32), cst=([128, 4, 128], F32), ident=([128, 128], BF16),
                   identf=([128, 128], F32))
PARAM_SPECS = dict(xd=([SEQ, D], F32), w_in=([D, N_IN], F32), gains=([4, D], F32), bg=([16], F32), cw_qk=([1024, 4], F32),
                   cw_h=([4, 1536], F32), nw_m=([512], F32), nw_h=([512], F32), fw1=([33, 64], F32), fw2=([64, 64], F32),
                   fw3=([64, 64], F32), fw4=([64, 2048], F32), w4l0=([64, 2, 512], F32), fbs=([64, 3], F32),
                   ffq=([64, 3], F32), fb=([2, 512], F32), w_out=([D, D], F32), w1=([D, DFF], F32), w2=([DFF, D], F32))


def build_nc():
    nc = bass.Bass("TRN2", target_bir_lowering=False)
    I = {k: nc.dram_tensor(k, sh, dt, kind="ExternalInput").ap() for k, (sh, dt) in {**CONST_SPECS, **PARAM_SPECS}.items()}
    out_d = nc.dram_tensor("out_d", [16, 128, D], F32, kind="ExternalOutput").ap()
    qk_d = nc.dram_tensor("qk_d", [8, 128, SEQ], BF16, kind=("ExternalOutput" if DEBUG_SCRATCH else "Internal")).ap()
    v_d = nc.dram_tensor("v_d", [NCH, 128, 512], BF16, kind=("ExternalOutput" if DEBUG_SCRATCH else "Internal")).ap()
    og_d = nc.dram_tensor("og_d", [NCH, 128, 512], BF16, kind=("ExternalOutput" if DEBUG_SCRATCH else "Internal")).ap()
    gt_d = nc.dram_tensor("gt_d", [128, NCH, 16], F32, kind=("ExternalOutput" if DEBUG_SCRATCH else "Internal")).ap()
    cvh_d = nc.dram_tensor("cvh_d", [128, 32, 1536], BF16, kind=("ExternalOutput" if DEBUG_SCRATCH else "Internal")).ap()
    h3_d = nc.dram_tensor("h3_d", [64, NF], BF16, kind=("ExternalOutput" if DEBUG_SCRATCH else "Internal")).ap()
    ytok_d = nc.dram_tensor("ytok_d", [SEQ, D], BF16, kind=("ExternalOutput" if DEBUG_SCRATCH else "Internal")).ap()

    SEMS = SemSet()
    _build_phases(nc, I, out_d, SEMS, qk_d, v_d, og_d, gt_d, cvh_d, h3_d, ytok_d)
    print(f"[build] semaphore IDs used: {SEMS.used} / budget {SemSet.BUDGET}", flush=True)
    return nc


def _build_phases(nc, I, out_d, SEMS, qk_d, v_d, og_d, gt_d, cvh_d, h3_d, ytok_d):

    def ld(S, eng, out, in_, deps=()):
        return S.dma(eng, lambda e: e.dma_start(out=out, in_=in_), deps=list(deps))

    def cp(S, eng, out, in_, deps=()):
        if eng == "scalar":
            return S.op("scalar", lambda e: e.copy(out=out, in_=in_), deps=list(deps))
        return S.op(eng, lambda e: e.tensor_copy(out=out, in_=in_), deps=list(deps))

    with contextlib.ExitStack() as outer:
        hnT = outer.enter_context(nc.sbuf_tensor("p1_hnT", [128, 8, SEQ + 2], BF16))
        with contextlib.ExitStack() as st:
            sb = lambda name, shape, dt: st.enter_context(nc.sbuf_tensor("p1a_" + name, shape, dt))
            pt = lambda name, shape, dt: st.enter_context(nc.psum_tensor("p1a_" + name, shape, dt))
            S = Sched(nc, "p1a", SEMS)
            xt = [sb(f"xt{i}", [128, D], F32) for i in range(3)]
            xn = [sb(f"xn{i}", [128, D], BF16) for i in range(3)]
            sq = sb("sq", [128, D], F32)
            ss = sb("ss", [128, NCH], F32)
            gb = sb("gb", [128, D], F32)
            ident = sb("ident", [128, 128], BF16)
            wbuf = [sb(f"wbuf{i}", [128, 8, 512], BF16) for i in range(2)]
            wg = sb("wg", [128, 8, 16], BF16)
            stg = [sb(f"stg{i}", [128, 4, 512], F32) for i in range(2)]
            cw = sb("cw", [128, 8, 4], F32)
            bgb = sb("bgb", [128, 16], F32)
            row2 = [sb(f"row{i}", [128, SEQ + 2], F32) for i in range(2)]
            acc = sb("acc", [128, SEQ], F32)
            qko = [sb(f"qko{i}", [128, SEQ], BF16) for i in range(2)]
            vst = [sb(f"vst{i}", [128, 512], BF16) for i in range(2)]
            ost = [sb(f"ost{i}", [128, 512], BF16) for i in range(2)]
            gts = sb("gts", [128, NCH, 16], F32)
            pT = [pt(f"pT{i}", [128, 4, 128], BF16) for i in range(2)]
            pA = [pt(f"pA{i}", [128, 512], F32) for i in range(4)]

            t_g = ld(S, "sync", gb[:], I["gains"][0].partition_broadcast(128))
            t_id = ld(S, "sync", ident[:], I["ident"][:, :])
            t_cw = ld(S, "sync", cw[:], I["cw_qk"].rearrange("(c p) k -> p c k", p=128))
            t_bg = ld(S, "sync", bgb[:], I["bg"].partition_broadcast(128))
            t_hz = S.op("gpsimd", lambda e: e.memset(hnT[:, :, 0:1], 0.0))
            t_hz2 = S.op("gpsimd", lambda e: e.memset(hnT[:, :, SEQ + 1:SEQ + 2], 0.0))
            t_rz2 = [S.op("gpsimd", lambda e, i=i: e.memset(row2[i][:], 0.0)) for i in range(2)]

            stg_free = [None, None]
            si = [0]

            def load_group(dst, col0, ncols, guard):
                toks = []
                for half in range(2):
                    b = si[0] % 2
                    si[0] += 1
                    t_l = ld(S, "gpsimd" if b else "sync", stg[b][:, :, :ncols],
                             I["w_in"][half * 512:(half + 1) * 512, col0:col0 + ncols].rearrange("(k p) n -> p k n", p=128),
                             deps=[stg_free[b]])
                    t_c = cp(S, "scalar" if half else "vector", dst[:, half * 4:half * 4 + 4, :ncols], stg[b][:, :, :ncols],
                             deps=[t_l] + guard)
                    stg_free[b] = t_c
                    toks.append(t_c)
                return toks

            pa_free = [None] * 4
            pa_i = [0]
            w_users = [[], []]
            t_wv = load_group(wbuf[0], OFF_MV, 512, [])
            t_wo = load_group(wbuf[1], OFF_MO, 512, [])
            t_wgl = ld(S, "sync", stg[0][:, :, 0:16], I["w_in"][0:512, OFF_GATE:OFF_GATE + 16].rearrange("(k p) n -> p k n", p=128),
                       deps=[stg_free[0]])
            t_wgl2 = ld(S, "sync", stg[1][:, :, 0:16], I["w_in"][512:1024, OFF_GATE:OFF_GATE + 16].rearrange("(k p) n -> p k n", p=128),
                        deps=[stg_free[1]])
            t_wg = [cp(S, "vector", wg[:, 0:4, :], stg[0][:, :, 0:16], deps=[t_wgl]),
                    cp(S, "vector", wg[:, 4:8, :], stg[1][:, :, 0:16], deps=[t_wgl2])]
            stg_free[0], stg_free[1] = t_wg[0], t_wg[1]
            vo_free = [[None, None], [None, None]]
            g_evs = []

            def emit_vog(c, hn_c):
                for grp, (wb, t_w) in enumerate(((0, t_wv), (1, t_wo))):
                    pb = pa_i[0] % 4
                    pa_i[0] += 1
                    tt = None
                    for kc in range(8):
                        tt = S.op("tensor", lambda e, wb=wb, kc=kc, pb=pb: e.matmul(
                            pA[pb][:], lhsT=hnT[:, kc, 1 + c * 128:1 + (c + 1) * 128], rhs=wbuf[wb][:, kc, :],
                            start=(kc == 0), stop=(kc == 7)), deps=hn_c + t_w + [pa_free[pb]])
                    sbi = c % 2
                    if grp == 0:
                        ev = cp(S, "vector", vst[sbi][:], pA[pb][:], deps=[tt, vo_free[0][sbi]])
                        vo_free[0][sbi] = ld(S, "sync", v_d[c], vst[sbi][:], deps=[ev])
                    else:
                        ev = S.op("scalar", lambda e, sbi=sbi, pb=pb: e.activation(out=ost[sbi][:], in_=pA[pb][:], func=AF.Sigmoid),
                                  deps=[tt, vo_free[1][sbi]])
                        vo_free[1][sbi] = ld(S, "sync", og_d[c], ost[sbi][:], deps=[ev])
                    pa_free[pb] = ev
                    w_users[wb] = [tt]
                pb = pa_i[0] % 4
                pa_i[0] += 1
                tt = None
                for kc in range(8):
                    tt = S.op("tensor", lambda e, kc=kc, pb=pb: e.matmul(
                        pA[pb][:, 0:16], lhsT=hnT[:, kc, 1 + c * 128:1 + (c + 1) * 128], rhs=wg[:, kc, :],
                        start=(kc == 0), stop=(kc == 7)), deps=hn_c + t_wg + [pa_free[pb]])
                ev = S.op("vector", lambda e, pb=pb: e.tensor_add(out=gts[:, c, :], in0=pA[pb][:, 0:16], in1=bgb[:]), deps=[tt, t_bg])
                pa_free[pb] = ev
                g_evs.append(ev)

            NXB = 3
            x_free = [None] * NXB
            xn_free = [None] * NXB
            pT_free = [None, None]
            hn_tok = [None] * NCH
            tk_ld, tk_n = {}, {}
            pi = [0]

            def st_load(i):
                b = i % NXB
                tk_ld[i] = ld(S, "sync", xt[b][:], I["xd"][i * 128:(i + 1) * 128, :], deps=[x_free[b]])

            def st_norm(i):
                b = i % NXB
                t_sq = S.op("scalar", lambda e: e.activation(out=sq[:], in_=xt[b][:], func=AF.Square, accum_out=ss[:, i:i + 1]),
                            deps=[tk_ld[i]])
                t_ln = S.op("scalar", lambda e: e.activation(out=ss[:, i:i + 1], in_=ss[:, i:i + 1], func=AF.Ln, scale=1.0 / D, bias=EPS),
                            deps=[t_sq])
                t_ex = S.op("scalar", lambda e: e.activation(out=ss[:, i:i + 1], in_=ss[:, i:i + 1], func=AF.Exp, scale=-0.5), deps=[t_ln])
                t_n = S.op("vector", lambda e: e.scalar_tensor_tensor(out=xn[b][:], in0=xt[b][:], scalar=ss[:, i:i + 1], in1=gb[:],
                                                                      op0=ALU.mult, op1=ALU.mult), deps=[t_ex, t_g, xn_free[b]])
                x_free[b] = t_n
                tk_n[i] = t_n

            def st_tr(i):
                b = i % NXB
                cps = []
                tt = None
                for half in range(2):
                    pb = pi[0] % 2
                    pi[0] += 1
                    for j in range(4):
                        c = half * 4 + j
                        tt = S.op("tensor", lambda e, c=c, j=j, pb=pb: e.transpose(
                            out=pT[pb][:, j, :], in_=xn[b][:, c * 128:(c + 1) * 128], identity=ident[:]),
                            deps=[tk_n[i], t_id, pT_free[pb]])
                    t_cp = cp(S, "scalar" if half else "vector", hnT[:, half * 4:half * 4 + 4, 1 + i * 128:1 + (i + 1) * 128],
                              pT[pb][:], deps=[tt])
                    pT_free[pb] = t_cp
                    cps.append(t_cp)
                xn_free[b] = tt
                hn_tok[i] = cps

            for k in range(-2, NCH + 1):
                if 0 <= k + 2 < NCH:
                    st_load(k + 2)
                if 0 <= k + 1 < NCH:
                    st_norm(k + 1)
                if 0 <= k < NCH:
                    st_tr(k)
                if 0 <= k - 1 < NCH:
                    emit_vog(k - 1, hn_tok[k - 1] + [t_hz, t_hz2])
            all_hn = [t for c in hn_tok for t in c] + [t_hz, t_hz2]

            qko_free = [None, None]
            row_free = [t_rz2[0], t_rz2[1]]
            st8 = dict(acc_free=None, t_w=None)
            ev8 = {}

            def qk_mm(t):
                grp, ct = t // 4, t % 4
                wb = grp % 2
                if ct == 0:
                    st8["t_w"] = load_group(wbuf[wb], (OFF_MQ, OFF_MK)[grp], 512, w_users[wb])
                    w_users[wb] = []
                t_w = st8["t_w"]
                rb = t % 2
                row = row2[rb]
                evs = []
                tt = None
                for tb in range(8):
                    pb = pa_i[0] % 4
                    pa_i[0] += 1
                    for kc in range(8):
                        tt = S.op("tensor", lambda e, wb=wb, ct=ct, kc=kc, tb=tb, pb=pb: e.matmul(
                            pA[pb][:], lhsT=wbuf[wb][:, kc, ct * 128:(ct + 1) * 128],
                            rhs=hnT[:, kc, 1 + tb * 512:1 + (tb + 1) * 512], start=(kc == 0), stop=(kc == 7)),
                            deps=all_hn + t_w + [pa_free[pb]])
                    ev = cp(S, "scalar", row[:, 1 + tb * 512:1 + (tb + 1) * 512], pA[pb][:], deps=[tt, row_free[rb]])
                    pa_free[pb] = ev
                    evs.append(ev)
                w_users[wb].append(tt)
                ev8[t] = evs

            def qk_conv(t):
                grp = t // 4
                rb = t % 2
                row = row2[rb]
                c1 = S.op("vector", lambda e: e.tensor_scalar(
                    out=acc[:], in0=row[:, 1:SEQ + 1], scalar1=cw[:, t, 1:2], scalar2=cw[:, t, 3:4],
                    op0=ALU.mult, op1=ALU.add), deps=ev8[t] + [t_cw, st8["acc_free"]])
                c2 = S.op("vector", lambda e: e.scalar_tensor_tensor(
                    out=acc[:], in0=row[:, 0:SEQ], scalar=cw[:, t, 0:1], in1=acc[:], op0=ALU.mult, op1=ALU.add), deps=[c1])
                c3 = S.op("vector", lambda e: e.scalar_tensor_tensor(
                    out=acc[:], in0=row[:, 2:SEQ + 2], scalar=cw[:, t, 2:3], in1=acc[:], op0=ALU.mult, op1=ALU.add), deps=[c2])
                row_free[rb] = c3
                ob = t % 2
                if grp == 0:
                    t_s = S.op("scalar", lambda e: e.activation(out=qko[ob][:], in_=acc[:], func=AF.Silu), deps=[c3, qko_free[ob]])
                else:
                    t_s0 = S.op("scalar", lambda e: e.activation(out=acc[:], in_=acc[:], func=AF.Silu), deps=[c3])
                    t_s = S.op("vector", lambda e: e.tensor_scalar(out=qko[ob][:], in0=acc[:], scalar1=128 ** -0.5, scalar2=None,
                                                                   op0=ALU.mult), deps=[t_s0, qko_free[ob]])
                st8["acc_free"] = t_s
                qko_free[ob] = ld(S, "sync", qk_d[t], qko[ob][:], deps=[t_s])

            for t in range(9):
                if t < 8:
                    qk_mm(t)
                if t >= 1:
                    qk_conv(t - 1)

            ld(S, "sync", gt_d[:, :, :], gts[:], deps=g_evs)
            S.finish()
        if STOP_AFTER == 'P1a':
            return

        with contextlib.ExitStack() as st:
            sb = lambda name, shape, dt: st.enter_context(nc.sbuf_tensor("p1b_" + name, shape, dt))
            pt = lambda name, shape, dt: st.enter_context(nc.psum_tensor("p1b_" + name, shape, dt))
            S = Sched(nc, "p1b", SEMS)
            wh = sb("wh", [128, 8, 1536], BF16)
            stg = [sb(f"stgb{i}", [128, 4, 512], F32) for i in range(2)]
            cwb = sb("cwb", [128, 4, 1536], F32)
            pl = sb("pl", [128, 3, 1536], F32)
            tmpc = sb("tmpc", [128, 1536], F32)
            oacc = sb("oacc", [128, 1536], F32)
            cvb = [sb(f"cvb{i}", [128, 1536], BF16) for i in range(2)]
            pP = [[pt(f"pP{i}_{j}", [128, 512], F32) for j in range(3)] for i in range(2)]
            t_cwb = ld(S, "sync", cwb[:], I["cw_h"].partition_broadcast(128))
            stg_free = [None, None]
            t_wh = []
            k_ = 0
            for g3 in range(3):
                for half in range(2):
                    b = k_ % 2
                    k_ += 1
                    t_l = ld(S, "gpsimd" if b else "sync", stg[b][:],
                             I["w_in"][half * 512:(half + 1) * 512, OFF_HV + g3 * 512:OFF_HV + (g3 + 1) * 512].rearrange(
                                 "(k p) n -> p k n", p=128), deps=[stg_free[b]])
                    t_c = cp(S, "vector" if half else "scalar", wh[:, half * 4:half * 4 + 4, g3 * 512:(g3 + 1) * 512], stg[b][:],
                             deps=[t_l])
                    stg_free[b] = t_c
                    t_wh.append(t_c)
            pp_free = [[None] * 3, [None] * 3]
            pl_ready = {}
            pl_free = [None, None, None]
            cvb_free = [None, None]
            OWN_R = 16
            HV_FULL = (slice(0, 1024), slice(1024, 1536))
            HV_TRIM = (slice(0, 640), slice(640, 1024))
            HE = ("vector", "gpsimd")
            for r in range(-1, 34):
                if r <= 32:
                    pb = (r + 1) % 2
                    slot = (r + 1) % 3
                    evs = []
                    for g3 in range(3 if r <= OWN_R else 2):
                        tt = None
                        for kc in range(8):
                            tt = S.op("tensor", lambda e, kc=kc, g3=g3, pb=pb, r=r: e.matmul(
                                pP[pb][g3][:], lhsT=hnT[:, kc, bass.ds(1 + r, 128, step=32)],
                                rhs=wh[:, kc, g3 * 512:(g3 + 1) * 512], start=(kc == 0), stop=(kc == 7)),
                                deps=t_wh + [pp_free[pb][g3]])
                        ev = cp(S, "scalar", pl[:, slot, g3 * 512:(g3 + 1) * 512], pP[pb][g3][:], deps=[tt, pl_free[slot]])
                        pp_free[pb][g3] = ev
                        evs.append(ev)
                    pl_ready[r] = evs
                ro = r - 1
                if 0 <= ro <= 31:
                    a, b_, c_ = (ro) % 3, (ro + 1) % 3, (ro + 2) % 3
                    ob = ro % 2
                    fin = []
                    ncol = 1536 if ro < OWN_R else 1024
                    for hv, he in zip(HV_FULL if ro < OWN_R else HV_TRIM, HE):
                        dps = pl_ready[ro - 1] + pl_ready[ro] + pl_ready[ro + 1] + [t_cwb]
                        o1 = S.op(he, lambda e, hv=hv, b_=b_: e.tensor_tensor(out=oacc[:, hv], in0=pl[:, b_, hv], in1=cwb[:, 1, hv],
                                                                               op=ALU.mult), deps=dps)
                        o2 = S.op(he, lambda e, hv=hv: e.tensor_add(out=oacc[:, hv], in0=oacc[:, hv], in1=cwb[:, 3, hv]), deps=[o1])
                        o3 = S.op(he, lambda e, hv=hv, a=a: e.tensor_tensor(out=tmpc[:, hv], in0=pl[:, a, hv], in1=cwb[:, 0, hv],
                                                                             op=ALU.mult), deps=[o2])
                        o4 = S.op(he, lambda e, hv=hv: e.tensor_add(out=oacc[:, hv], in0=oacc[:, hv], in1=tmpc[:, hv]), deps=[o3])
                        o5 = S.op(he, lambda e, hv=hv, c_=c_: e.tensor_tensor(out=tmpc[:, hv], in0=pl[:, c_, hv], in1=cwb[:, 2, hv],
                                                                               op=ALU.mult), deps=[o4])
                        o6 = S.op(he, lambda e, hv=hv, ob=ob: e.tensor_add(out=cvb[ob][:, hv], in0=oacc[:, hv], in1=tmpc[:, hv]),
                                  deps=[o5, cvb_free[ob]])
                        fin.append(o6)
                    pl_free[a] = None
                    pl_free[a] = fin[0]
                    pl_free_extra = fin[1]
                    mk = S.op("vector", lambda e: e.tensor_copy(out=tmpc[0:1, 0:1], in_=tmpc[0:1, 0:1]), deps=fin)
                    pl_free[a] = mk
                    cvb_free[ob] = ld(S, "sync", cvh_d[:, ro, 0:ncol], cvb[ob][:, 0:ncol], deps=fin)
            S.finish()
    if STOP_AFTER == 'P1':
        return
    with contextlib.ExitStack() as st:
        sb = lambda name, shape, dt: st.enter_context(nc.sbuf_tensor("p2_" + name, shape, dt))
        pt = lambda name, shape, dt: st.enter_context(nc.psum_tensor("p2_" + name, shape, dt))
        S = Sched(nc, "p2", SEMS)
        cst = sb("cst", [128, 4, 128], F32)
        ones_f = sb("ones_f", [128, 128], F32)
        ident = sb("ident", [128, 128], BF16)
        identf = sb("identf", [128, 128], F32)
        nwb = sb("nwb", [128, 512], F32)
        gt = sb("gt", [128, NCH, 16], F32)
        NB = 3
        qkc = [[sb(f"qkc{d}{p}", [128, 8, 128], BF16) for p in range(NB)] for d in range(2)]
        v1c = [[sb(f"v1c{d}{p}", [128, NH, 129], BF16) for p in range(NB)] for d in range(2)]
        ogc = [sb(f"ogc{p}", [128, 512], BF16) for p in range(2)]
        lfB2 = [sb(f"lfB{q}", [128, 8, 128], F32) for q in range(2)]
        lf2 = [sb(f"lf{q}", [128, 8], F32) for q in range(2)]
        bcs2 = [sb(f"bcs{q}", [128, 8], F32) for q in range(2)]
        bias2 = [sb(f"bias{q}", [128, 8], F32) for q in range(2)]
        eb2 = [sb(f"eb{q}", [128, 8], F32) for q in range(2)]
        av2 = [sb(f"av{q}", [128, 8], F32) for q in range(2)]
        eg2 = [sb(f"eg{q}", [128, 8], F32) for q in range(2)]
        WT2 = [sb(f"WT{q}", [128, 8, 128], F32) for q in range(2)]
        SW2 = [sb(f"SW{q}", [128, 8, 128], BF16) for q in range(2)]
        kT_tok2 = [[sb(f"kT_tok{d}{q}", [128, NH, 128], BF16) for q in range(2)] for d in range(2)]
        Va2 = [sb(f"Va{q}", [128, 8, 129], BF16) for q in range(2)]
        Cst = sb("Cst", [128, 8, 129], F32)
        Cbf = sb("Cbf", [128, 8, 129], BF16)
        hacc = sb("hacc", [128, NCH, 512], F32)
        inter = [sb(f"inter{d}", [128, 129], F32) for d in range(2)]
        tot = [sb(f"tot{d}", [128, 129], F32) for d in range(2)]
        rden = [sb(f"rden{d}", [128, 1], F32) for d in range(2)]
        hsum = sb("hsum", [128, 512], F32)
        sqs = sb("sqs", [128, 512], F32)
        gst = sb("gst", [128, 4], F32)
        yt = [sb(f"yt{p}", [128, 512], BF16) for p in range(2)]
        bankA = [pt(f"bankA{d}", [128, NH, 128], F32) for d in range(2)]
        bankB = [pt(f"bankB{d}", [128, 512], F32) for d in range(2)]
        bankC = [pt(f"bankC{d}", [128, NH, 128], BF16) for d in range(2)]
        bankD = [pt(f"bankD{d}", [128, 129], F32) for d in range(2)]
        dg = sb("dg", [128, NH, 128], F32)
        t_cst = ld(S, "sync", cst[:], I["cst"][:, :, :])
        t_id = ld(S, "sync", ident[:], I["ident"][:, :])
        t_idf = ld(S, "sync", identf[:], I["identf"][:, :])
        t_nw = ld(S, "sync", nwb[:], I["nw_m"].partition_broadcast(128))
        t_gt = ld(S, "sync", gt[:], gt_d[:, :, :])
        t_ones = S.op("gpsimd", lambda e: e.memset(ones_f[:], 1.0))
        t_cz = S.op("gpsimd", lambda e: e.memset(Cst[:], 0.0))
        t_cbz = S.op("gpsimd", lambda e: e.memset(Cbf[:], 0.0))
        t_v1 = [[S.op("gpsimd", lambda e, d=d, p=p: e.memset(v1c[d][p][:], 1.0)) for p in range(NB)] for d in range(2)]
        qk_view = qk_d.rearrange("t p s -> p t s")

        buf_free = [[None] * NB, [None] * NB]
        loaded = {}

        def issue_loads(i):
            p = i % NB
            for d, c in ((0, i), (1, NCH - 1 - i)):
                t1 = ld(S, "sync", qkc[d][p][:], qk_view[:, :, c * 128:(c + 1) * 128], deps=[buf_free[d][p]])
                t2 = ld(S, "sync", v1c[d][p][:, :, 0:128], v_d[c].rearrange("p (h e) -> p h e", h=NH),
                        deps=[buf_free[d][p], t_v1[d][p]])
                loaded[(i, d)] = [t1, t2]

        state = dict(lastp=None, post_i=0, og_free=[None, None], yt_free=[None, None])
        done = {}
        ctx = {}

        def front(i, c, d):
            q = i % 2
            p = i % NB
            lf, bcs, bias, eb, av, eg = lf2[q], bcs2[q], bias2[q], eb2[q], av2[q], eg2[q]
            lfB, WT, SW, Va, kT_tok = lfB2[q], WT2[q], SW2[q], Va2[q], kT_tok2[d][q]
            Q, V = qkc[d][p], v1c[d][p]
            tl = loaded[(i, d)]
            old = done.get((i - 2, d), [])
            pc = ctx.get((i - 1, d), {})
            pW = pS = bankA[d]
            pTk = bankC[d]
            J = slice(d * 4, d * 4 + 4)
            gi = slice(d * 8, d * 8 + 4)
            gf = slice(d * 8 + 4, d * 8 + 8)
            a1 = S.op("scalar", lambda e: e.activation(out=lf[:, J], in_=gt[:, c, gf], func=AF.Exp, scale=-1.0), deps=[t_gt] + old)
            a2 = S.op("scalar", lambda e: e.activation(out=lf[:, J], in_=lf[:, J], func=AF.Ln, bias=1.0), deps=[a1])
            a3 = S.op("vector", lambda e: e.tensor_scalar(out=lf[:, J], in0=lf[:, J], scalar1=-1.0, scalar2=None, op0=ALU.mult),
                      deps=[a2])
            tb = [S.op("vector", lambda e: e.tensor_tensor(
                out=lfB[:, J, :], in0=cst[:, d, :].unsqueeze(1).to_broadcast([128, NH, 128]),
                in1=lf[:, J].unsqueeze(2).to_broadcast([128, NH, 128]), op=ALU.mult), deps=[a3, t_cst] + old)]
            yield
            tw = None
            for h in range(NH):
                S.op("tensor", lambda e, h=h: e.matmul(pW[:, h, :], lhsT=ones_f[:], rhs=lfB[:, d * 4 + h, :], start=True, stop=False),
                     deps=tb + [t_ones] + pc.get("tsw", []))
                tw = S.op("tensor", lambda e, h=h: e.matmul(pW[:, h, :], lhsT=identf[:], rhs=cst[:, 2 + d, :], start=False, stop=True),
                          deps=[t_idf])
            tk = None
            for h in range(NH):
                tk = S.op("tensor", lambda e, h=h: e.transpose(out=pTk[:, h, :], in_=Q[:, 4 + h, :], identity=ident[:]),
                          deps=tl + [t_id] + pc.get("tkc", []))
            yield
            b0 = S.op("vector", lambda e: e.tensor_tensor(out=dg[:], in0=pW[:], in1=identf[:].unsqueeze(1).to_broadcast([128, NH, 128]),
                                                          op=ALU.mult), deps=[tw, t_idf] + old + pc.get("b1", []))
            b1 = S.op("vector", lambda e: e.reduce_sum(out=bcs[:, J], in_=dg[:], axis=AX.X), deps=[b0])
            b2 = S.op("vector", lambda e: e.tensor_sub(out=bias[:, J], in0=gt[:, c, gi], in1=bcs[:, J]), deps=[b1])
            b3 = S.op("scalar", lambda e: e.activation(out=eb[:, J], in_=bcs[:, J], func=AF.Exp), deps=[b1] + old)
            tkc = S.op("scalar", lambda e: e.copy(out=kT_tok[:], in_=pTk[:]), deps=[tk] + old)
            gcol = 127 if d == 0 else 0
            tW = []
            for h in range(NH):
                j = d * 4 + h
                tW.append(S.op("scalar", lambda e, h=h, j=j: e.activation(
                    out=WT[:, j, :], in_=pW[:, h, :], func=AF.Exp, bias=bias[:, j:j + 1]), deps=[tw, b2] + old))
            t_eg = S.op("scalar", lambda e: e.activation(out=eg[:, J], in_=pW[:, :, gcol], func=AF.Exp), deps=[tw] + old)
            t_ebi = S.op("scalar", lambda e: e.activation(out=av[:, J], in_=bias[:, J], func=AF.Exp), deps=[b2] + old)
            t_av = S.op("vector", lambda e: e.tensor_mul(out=av[:, J], in0=av[:, J], in1=eg[:, J]), deps=[t_eg, t_ebi])
            yield
            ts_ = None
            for h in range(NH):
                ts_ = S.op("tensor", lambda e, h=h: e.matmul(pS[:, h, :], lhsT=Q[:, 4 + h, :], rhs=Q[:, h, :], start=True, stop=True),
                           deps=tl + [t_eg, b0])
            tsw = [S.op("vector", lambda e, h=h: e.tensor_tensor(out=SW[:, d * 4 + h, :], in0=pS[:, h, :], in1=WT[:, d * 4 + h, :],
                                                                 op=ALU.mult), deps=[ts_, tW[h]] + old) for h in range(NH)]
            u1s = [S.op("scalar", lambda e, h=h: e.activation(out=Va[:, d * 4 + h, :], in_=V[:, h, :], func=AF.Copy,
                                                              scale=av[:, d * 4 + h:d * 4 + h + 1]), deps=[t_av] + old + tl) for h in range(NH)]
            yield
            ctx[(i, d)] = dict(tsw=tsw, b1=[b1], tkc=[tkc], u1s=u1s, b3=b3, t_av=t_av, tl=tl)

        def heads(i, c, d):
            q = i % 2
            p = i % NB
            eb, eg = eb2[q], eg2[q]
            SW, Va, kT_tok = SW2[q], Va2[q], kT_tok2[d][q]
            Q, V = qkc[d][p], v1c[d][p]
            cx = ctx[(i, d)]
            tsw, u1s, tkc, b3, t_av, tl = cx["tsw"], cx["u1s"], cx["tkc"][0], cx["b3"], cx["t_av"], cx["tl"]
            prev_l = done.get((i - 1, d), [])
            pN = bankB[d][:, 0:258].rearrange("p (a b) -> p a b", a=2)
            pC = bankD[d][:]
            first = (d == 0 and c < NCH // 2) or (d == 1 and c >= NCH // 2)
            last_v = None
            tm_last = None
            u4 = None
            for h in range(NH):
                j = d * 4 + h
                S.op("tensor", lambda e, h=h, j=j: e.matmul(pN[:, 0, :], lhsT=SW[:, j, :], rhs=V[:, h, :], start=True, stop=True),
                     deps=[tsw[h], last_v] + tl + prev_l)
                tm = S.op("tensor", lambda e, h=h, j=j: e.matmul(pN[:, 1, :], lhsT=Q[:, h, :], rhs=Cbf[:, j, :], start=True, stop=True),
                          deps=[t_cbz, last_v] + prev_l)
                u2 = S.op("tensor", lambda e, h=h, j=j: e.matmul(pC, lhsT=kT_tok[:, h, :], rhs=Va[:, j, :], start=True, stop=True),
                          deps=[u1s[h], tkc, last_v] + prev_l)
                i1 = S.op("scalar", lambda e, j=j: e.activation(out=inter[d][:], in_=pN[:, 1, :], func=AF.Copy, scale=eb[:, j:j + 1]),
                          deps=[tm, b3, last_v] + prev_l)
                i2 = S.op("vector", lambda e: e.tensor_add(out=tot[d][:], in0=pN[:, 0, :], in1=inter[d][:]), deps=[i1, tm] + prev_l)
                u3 = S.op("vector", lambda e, j=j: e.scalar_tensor_tensor(out=Cst[:, j, :], in0=Cst[:, j, :], scalar=eg[:, j:j + 1],
                                                                          in1=pC, op0=ALU.mult, op1=ALU.add), deps=[u2, t_av, t_cz, tm])
                u4 = S.op("scalar", lambda e, j=j: e.copy(out=Cbf[:, j, :], in_=Cst[:, j, :]), deps=[u3, tm])
                i3a = S.op("vector", lambda e: e.scalar_tensor_tensor(out=rden[d][:], in0=tot[d][:, 128:129], scalar=-1.0,
                                                                      in1=tot[d][:, 128:129], op0=ALU.mult, op1=ALU.max), deps=[i2])
                i3 = S.op("vector", lambda e: e.tensor_scalar_max(out=rden[d][:], in0=rden[d][:], scalar1=1.0), deps=[i3a])
                i4 = S.op("vector", lambda e: e.reciprocal(out=rden[d][:], in_=rden[d][:]), deps=[i3])
                hs = slice(h * 128, (h + 1) * 128)
                if first:
                    i5 = S.op("vector", lambda e, hs=hs: e.tensor_scalar(out=hacc[:, c, hs], in0=tot[d][:, 0:128], scalar1=rden[d][:, 0:1],
                                                                         scalar2=None, op0=ALU.mult), deps=[i4])
                else:
                    i5 = S.op("vector", lambda e, hs=hs: e.scalar_tensor_tensor(out=hacc[:, c, hs], in0=tot[d][:, 0:128], scalar=rden[d][:, 0:1],
                                                                                in1=hacc[:, c, hs], op0=ALU.mult, op1=ALU.add), deps=[i4])
                last_v = i5
                tm_last = u2
                yield
            done[(i, d)] = [last_v, u4]
            buf_free[d][p] = tm_last
            if not first and P2_POST:
                post(c, last_v)

        def post(c, done):
            k = state["post_i"]
            state["post_i"] += 1
            p = k % 2
            t_og = ld(S, "sync", ogc[p][:], og_d[c], deps=[state["og_free"][p]])
            p1 = S.op("vector", lambda e: e.tensor_tensor(out=hsum[:], in0=hacc[:, c, :], in1=ogc[p][:], op=ALU.mult),
                      deps=[done, state["lastp"], t_og])
            state["og_free"][p] = p1
            pl_ = None
            for h in range(NH):
                hs = slice(h * 128, (h + 1) * 128)
                pl_ = S.op("scalar", lambda e, h=h, hs=hs: e.activation(out=sqs[:, hs], in_=hsum[:, hs], func=AF.Square,
                                                                        accum_out=gst[:, h:h + 1]), deps=[p1])
            p3 = S.op("scalar", lambda e: e.activation(out=gst[:], in_=gst[:], func=AF.Ln, scale=1.0 / 128, bias=EPS), deps=[pl_])
            p4 = S.op("scalar", lambda e: e.activation(out=gst[:], in_=gst[:], func=AF.Exp, scale=-0.5), deps=[p3])
            pq = None
            for h in range(NH):
                hs = slice(h * 128, (h + 1) * 128)
                pq = S.op("vector", lambda e, h=h, hs=hs: e.scalar_tensor_tensor(
                    out=yt[p][:, hs], in0=hsum[:, hs], scalar=gst[:, h:h + 1], in1=nwb[:, hs], op0=ALU.mult, op1=ALU.mult),
                    deps=[p4, t_nw, state["yt_free"][p]])
            state["lastp"] = pq
            state["yt_free"][p] = ld(S, "sync", ytok_d[c * 128:(c + 1) * 128, 0:512], yt[p][:], deps=[pq])

        def run_interleaved(gens):
            gens = list(gens)
            while gens:
                for g_ in list(gens):
                    try:
                        next(g_)
                    except StopIteration:
                        gens.remove(g_)

        issue_loads(0)
        issue_loads(1)
        run_interleaved([front(0, 0, 0), front(0, NCH - 1, 1)])
        for i in range(NCH):
            if i + 2 < NCH:
                issue_loads(i + 2)
            gens = [heads(i, i, 0), heads(i, NCH - 1 - i, 1)]
            if i + 1 < NCH:
                gens += [front(i + 1, i + 1, 0), front(i + 1, NCH - 2 - i, 1)]
            run_interleaved(gens)
        S.finish()
    if STOP_AFTER == 'P2':
        return
    with contextlib.ExitStack() as st:
        sb = lambda name, shape, dt: st.enter_context(nc.sbuf_tensor("p3_" + name, shape, dt))
        pt = lambda name, shape, dt: st.enter_context(nc.psum_tensor("p3_" + name, shape, dt))
        S = Sched(nc, "p3", SEMS)
        zT = sb("zT", [33, NF], F32)
        w1s = sb("w1", [33, 64], F32)
        w23 = [sb("w2", [64, 64], BF16), sb("w3", [64, 64], BF16)]
        wst = sb("wst", [64, 2, 64], F32)
        sc = sb("sc", [64, 3], F32)
        bi = sb("bi", [64, 3], F32)
        bsT = sb("bsT", [64, 3], F32)
        hA = sb("hA", [64, NF], BF16)
        hB = sb("hB", [64, NF], BF16)
        NPH = 4
        tmpf = [sb(f"tmpf{i}", [64, 512], F32) for i in range(NPH)]
        tmpu = [sb(f"tmpu{i}", [64, 512], F32) for i in range(NPH)]
        pH = [pt(f"pH{i}", [64, 512], F32) for i in range(NPH)]
        t_z = ld(S, "sync", zT[:], I["zT"][:, :])
        t_w1 = ld(S, "sync", w1s[:], I["fw1"][:, :])
        t_bs = ld(S, "sync", bsT[:], I["fbs"][:, :])
        t_fq = ld(S, "sync", sc[:], I["ffq"][:, :])
        t_l2 = ld(S, "sync", wst[:, 0, :], I["fw2"][:, :])
        t_l3 = ld(S, "sync", wst[:, 1, :], I["fw3"][:, :])
        t_c2 = cp(S, "vector", w23[0][:], wst[:, 0, :], deps=[t_l2])
        t_c3 = cp(S, "vector", w23[1][:], wst[:, 1, :], deps=[t_l3])
        t_bi = S.op("vector", lambda e: e.tensor_tensor(out=bi[:], in0=bsT[:], in1=sc[:], op=ALU.mult), deps=[t_bs, t_fq])
        frH = [None] * NPH
        toks_prev = [t_z, t_w1, t_bi, t_fq, t_c2, t_c3]
        src = None
        for layer in range(3):
            dst = hA if layer % 2 == 0 else hB
            toks = []
            for ch in range(16):
                b = ch % NPH
                cs = slice(ch * 512, (ch + 1) * 512)
                if layer == 0:
                    tt = S.op("tensor", lambda e, b=b, cs=cs: e.matmul(pH[b][:], lhsT=w1s[:], rhs=zT[:, cs], start=True, stop=True),
                              deps=toks_prev + [frH[b]])
                else:
                    tt = S.op("tensor", lambda e, b=b, cs=cs, w=w23[layer - 1], src=src: e.matmul(
                        pH[b][:], lhsT=w[:], rhs=src[:, cs], start=True, stop=True), deps=toks_prev + [frH[b]])
                r1 = S.op("vector", lambda e, b=b, layer=layer: e.tensor_scalar(
                    out=tmpf[b][:], in0=pH[b][:], scalar1=sc[:, layer:layer + 1], scalar2=bi[:, layer:layer + 1],
                    op0=ALU.mult, op1=ALU.add), deps=[tt] + (toks[-NPH:len(toks) - NPH + 1] if len(toks) >= NPH else []))
                frH[b] = r1
                r2a = S.op("vector", lambda e, b=b: e.tensor_scalar(out=tmpu[b][:], in0=tmpf[b][:], scalar1=1.0 / TWO_PI, scalar2=MAGIC,
                                                                    op0=ALU.mult, op1=ALU.add), deps=[r1] + (toks[-NPH:len(toks) - NPH + 1] if len(toks) >= NPH else []))
                r2b = S.op("vector", lambda e, b=b: e.tensor_scalar(out=tmpu[b][:], in0=tmpu[b][:], scalar1=-MAGIC, scalar2=-TWO_PI,
                                                                    op0=ALU.add, op1=ALU.mult), deps=[r2a])
                r2 = S.op("vector", lambda e, b=b: e.tensor_add(out=tmpf[b][:], in0=tmpf[b][:], in1=tmpu[b][:]), deps=[r2b])
                r3 = S.op("scalar", lambda e, b=b, cs=cs, dst=dst: e.activation(out=dst[:, cs], in_=tmpf[b][:], func=AF.Sin,
                                                                                 scale=1.0 - 2e-6), deps=[r2] + toks_prev)
                toks.append(r3)
            toks_prev = toks
            src = dst
        ld(S, "sync", h3_d[:, :], src[:], deps=toks_prev)
        S.finish()
    if STOP_AFTER == 'P3':
        return

    with contextlib.ExitStack() as st:
        sb = lambda name, shape, dt: st.enter_context(nc.sbuf_tensor("p4_" + name, shape, dt))
        pt = lambda name, shape, dt: st.enter_context(nc.psum_tensor("p4_" + name, shape, dt))
        S = Sched(nc, "p4", SEMS)
        FAl = sb("FAl", [128, 32, 2, 128], BF16)
        FAh = sb("FAh", [128, 32, 2, 128], BF16)
        FAi = sb("FAi", [128, 32, 2, 128], BF16)
        FB = sb("FB", [128, 3, 128], BF16)
        FBi = sb("FBi", [128, 2, 256], BF16)
        ident = sb("ident", [128, 128], BF16)
        h3T = sb("h3T", [64, NF], BF16)
        w4b = sb("w4b", [64, 2048], BF16)
        w4l0b = sb("w4l0b", [64, 2, 512], BF16)
        wst = sb("wst", [64, 1024], F32)
        ntj = sb("ntj", [128, 2, 32], F32)
        sgs = sb("sg", [128, 2, 32], F32)
        dlb = sb("dlb", [128, 512], F32)
        fbr = sb("fbr", [1, 2, 512], F32)
        nwh = sb("nwh", [128, 512], F32)
        xin = [sb(f"xin{i}", [128, 32, CB], BF16) for i in range(3)]
        kern = [sb(f"kern{i}", [128, 32, CB], BF16) for i in range(2)]
        A_sb = sb("A_sb", [128, 2, NQ, 32, 4], BF16)
        At = sb("At", [128, 2, NQ, 128], BF16)
        Kf1 = sb("Kf", [128, 2, NQ, 128], BF16)
        Kf = [Kf1, Kf1]
        Xs2 = [sb(f"Xs{i}", [128, 2, 512], F32) for i in range(2)]
        t1_2 = [sb(f"t1_{i}", [128, 2, 512], F32) for i in range(2)]
        t2_2 = [sb(f"t2_{i}", [128, 2, 512], F32) for i in range(2)]
        win16 = [sb(f"win{i}", [128, 4, CB], F32) for i in range(2)]
        l0t = sb("l0t", [1, CB], F32)
        zc = sb("zc", [128, 4, CB], F32)
        sqc = sb("sqc", [128, 4, CB], F32)
        gss = sb("gss", [128, 4, 2], F32)
        pool32 = [pt(f"pb{i}", [128, 512], F32) for i in range(6)]
        pTn = [pt(f"pT{i}", [128, 4, 128], BF16) for i in range(2)]
        free32 = [None] * 6
        free16 = [None] * 2
        rr = dict(b32=0, b16=0, xs=0, win=0)

        def get32():
            i = rr["b32"] % 6
            rr["b32"] += 1
            return i

        def get16():
            k = rr["b16"] % 2
            rr["b16"] += 1
            return k, pTn[k][:]
        Gt = A_sb[:].rearrange("p i q r c -> p (i q r c)").rearrange("p (i r c) -> p i r c", i=2, r=32)

        cl = [ld(S, "sync", FAl[:], I["FA_lo"][:, :, :, :]), ld(S, "gpsimd", FAh[:], I["FA_hi"][:, :, :, :]),
              ld(S, "sync", FAi[:], I["FAi"][:, :, :, :]), ld(S, "sync", FB[:], I["FB"][:, :, :]),
              ld(S, "sync", FBi[:], I["FBi"][:, :, :]), ld(S, "sync", ident[:], I["ident"][:, :]),
              ld(S, "gpsimd", h3T[:], h3_d[:, :])]
        t_ntj = ld(S, "sync", ntj[:], I["ntj"].rearrange("(h p) r -> p h r", p=128))
        t_sg = ld(S, "sync", sgs[:], I["sg"].rearrange("(h p) r -> p h r", p=128))
        t_dl = ld(S, "sync", dlb[:], I["deltas"].partition_broadcast(128))
        t_fb = ld(S, "sync", fbr[:], I["fb"].rearrange("(a o) c -> a o c", a=1))
        t_nw = ld(S, "sync", nwh[:], I["nw_h"].partition_broadcast(128))
        t_l4a = ld(S, "sync", wst[:], I["fw4"][:, 0:1024])
        t_w4a = cp(S, "vector", w4b[:, 0:1024], wst[:], deps=[t_l4a])
        t_l4b = ld(S, "sync", wst[:], I["fw4"][:, 1024:2048], deps=[t_w4a])
        t_w4 = cp(S, "vector", w4b[:, 1024:2048], wst[:], deps=[t_l4b])
        t_l40 = ld(S, "sync", wst[:], I["w4l0"].rearrange("k o c -> k (o c)"), deps=[t_w4])
        t_w40 = cp(S, "vector", w4l0b[:].rearrange("k o c -> k (o c)"), wst[:], deps=[t_l40])
        t_neg = S.op("vector", lambda e: e.tensor_scalar(out=w4b[:, 1024:2048], in0=w4b[:, 1024:2048], scalar1=-1.0, scalar2=None,
                                                         op0=ALU.mult), deps=[t_w4])
        cdeps = cl + [t_w4a, t_w4, t_w40, t_neg]
        xs_free = [[], []]
        win_free = [None] * 2
        ytok_v = ytok_d.rearrange("(m r) c -> m r c", r=32)
        cvh_v = cvh_d

        def fwd_transform(srcs, deps, on_group):
            evA = []
            for rp in range(16):
                i = get32()
                pa = pool32[i][:].rearrange("p (a b c) -> p a b c", a=2, b=2)
                tt = None
                for r2 in range(2):
                    r = rp * 2 + r2
                    for ri in range(2):
                        for si_, (FAt, dat) in enumerate(srcs):
                            tt = S.op("tensor", lambda e, pa=pa, r=r, r2=r2, ri=ri, FAt=FAt, dat=dat, si_=si_: e.matmul(
                                pa[:, r2, ri, :], lhsT=FAt[:, r, ri, :], rhs=dat[:, r, :], start=(si_ == 0),
                                stop=(si_ == len(srcs) - 1)), deps=deps + cdeps + [free32[i]])
                ev = None
                for r2 in range(2):
                    ev = cp(S, "vector" if (rp + r2) % 2 == 0 else "scalar", A_sb[:, :, :, rp * 2 + r2, :],
                            pa[:, r2, :, :].rearrange("p i (q c) -> p i q c", c=4), deps=[tt, ev] + deps)
                    evA.append(ev)
                free32[i] = ev
            evT = []
            n_ = 0
            for ri in range(2):
                for qg in range(NQ // 4):
                    k, ptile = get16()
                    tt = None
                    for j in range(4):
                        q = qg * 4 + j
                        tt = S.op("tensor", lambda e, ptile=ptile, ri=ri, q=q, j=j: e.transpose(
                            out=ptile[:, j, :], in_=A_sb[:, ri, q, :, :].rearrange("p r c -> p (r c)"),
                            identity=ident[:]), deps=evA + [free16[k]])
                    ev = cp(S, "vector" if n_ % 2 == 0 else "scalar", At[:, ri, qg * 4:qg * 4 + 4, :], ptile, deps=[tt] + deps)
                    n_ += 1
                    free16[k] = ev
                    evT.append(ev)
            outs = []
            for qg in range(NQ // 4):
                qs = slice(qg * 4, qg * 4 + 4)
                rhs_r = At[:, 0, qs, :].rearrange("p q k -> p (q k)")
                rhs_i = At[:, 1, qs, :].rearrange("p q k -> p (q k)")
                i0, i1 = get32(), get32()
                b0, b1 = pool32[i0], pool32[i1]
                S.op("tensor", lambda e, rhs_r=rhs_r, b0=b0: e.matmul(b0[:], lhsT=FB[:, 0, :], rhs=rhs_r, start=True, stop=False),
                     deps=[evT[qg], evT[NQ // 4 + qg], free32[i0]])
                tr = S.op("tensor", lambda e, rhs_i=rhs_i, b0=b0: e.matmul(b0[:], lhsT=FB[:, 2, :], rhs=rhs_i, start=False, stop=True))
                S.op("tensor", lambda e, rhs_r=rhs_r, b1=b1: e.matmul(b1[:], lhsT=FB[:, 1, :], rhs=rhs_r, start=True, stop=False),
                     deps=[free32[i1]])
                ti = S.op("tensor", lambda e, rhs_i=rhs_i, b1=b1: e.matmul(b1[:], lhsT=FB[:, 0, :], rhs=rhs_i, start=False, stop=True))
                o_, f0, f1 = on_group(qg, qs, b0, b1, tr, ti)
                free32[i0], free32[i1] = f0, f1
                outs += o_
            return outs

        def make_kernel(blk, order, deps):
            tk_all = []
            for half in range(2):
                col0 = half * 1024 + order * 512 + blk * CB
                for r4 in range(8):
                    i = get32()
                    pk4 = pool32[i][:].rearrange("p (j c) -> p j c", j=4)
                    wi = rr["win"] % 2
                    rr["win"] += 1
                    tt = None
                    tws = []
                    for jj in range(4):
                        r = r4 * 4 + jj
                        tt = S.op("tensor", lambda e, pk4=pk4, jj=jj, half=half, r=r, col0=col0: e.matmul(
                            pk4[:, jj, :], lhsT=h3T[:, bass.ds(half * 4096 + r, 128, step=32)], rhs=w4b[:, col0:col0 + CB],
                            start=True, stop=True), deps=cdeps + deps + [free32[i]])
                        tws.append(S.op("scalar", lambda e, half=half, r=r, wi=wi, jj=jj: e.activation(
                            out=win16[wi][:, jj, :], in_=dlb[:, blk * CB:(blk + 1) * CB], func=AF.Exp, scale=ntj[:, half, r:r + 1]),
                            deps=[t_dl, t_ntj, win_free[wi]]))
                    tk = S.op("vector", lambda e, pk4=pk4, half=half, r4=r4, wi=wi: e.tensor_tensor(
                        out=kern[half][:, r4 * 4:r4 * 4 + 4, :], in0=pk4, in1=win16[wi][:], op=ALU.mult), deps=[tt] + tws + deps)
                    free32[i] = tk
                    win_free[wi] = tk
                    tk_all.append(tk)
            tz0 = S.op("vector", lambda e: e.memset(kern[1][0:1, 0, :], 0.0), deps=tk_all)
            tk_all.append(tz0)
            i = get32()
            p0 = pool32[i][0:1, 0:CB]
            t0 = S.op("tensor", lambda e: e.matmul(p0, lhsT=h3T[:, 0:1], rhs=w4l0b[:, order, blk * CB:(blk + 1) * CB],
                                                   start=True, stop=True), deps=cdeps + [free32[i]])
            t0a = S.op("vector", lambda e: e.tensor_add(out=l0t[:], in0=p0, in1=fbr[0:1, order, blk * CB:(blk + 1) * CB]),
                       deps=[t0, t_fb] + tk_all[-1:])
            free32[i] = t0a
            t0b = S.op("vector", lambda e: e.tensor_copy(out=kern[0][0:1, 0, :], in_=l0t[:]), deps=[t0a] + tk_all)

            def on_group(qg, qs, b0, b1, tr, ti):
                a = cp(S, "scalar", Kf[order][:, 0, qs, :].rearrange("p q k -> p (q k)"), b0[:], deps=[tr] + deps)
                b_ = cp(S, "vector", Kf[order][:, 1, qs, :].rearrange("p q k -> p (q k)"), b1[:], deps=[ti] + deps)
                return [a, b_], a, b_
            return fwd_transform([(FAl, kern[0]), (FAh, kern[1])], tk_all + [t0b], on_group)

        def conv(x_tile, Kft, deps, on_out, n_rg=8):
            def on_group(qg, qs, b0, b1, tr, ti):
                xi_ = rr["xs"] % 2
                rr["xs"] += 1
                Xs, t1, t2 = Xs2[xi_], t1_2[xi_], t2_2[xi_]
                xr = cp(S, "scalar", Xs[:, 0, :], b0[:], deps=[tr] + xs_free[xi_])
                xi = cp(S, "scalar", Xs[:, 1, :], b1[:], deps=[ti])
                kr = Kft[:, 0, qs, :].rearrange("p q k -> p (q k)")
                ki = Kft[:, 1, qs, :].rearrange("p q k -> p (q k)")
                yr = At[:, 0, qs, :].rearrange("p q k -> p (q k)")
                yi = At[:, 1, qs, :].rearrange("p q k -> p (q k)")
                a1 = S.op("vector", lambda e: e.tensor_mul(out=t1[:, 0, :], in0=Xs[:, 0, :], in1=kr), deps=[xr] + deps)
                a2 = S.op("vector", lambda e: e.tensor_mul(out=t1[:, 1, :], in0=Xs[:, 1, :], in1=ki), deps=[xi])
                a3 = S.op("vector", lambda e: e.tensor_sub(out=yr, in0=t1[:, 0, :], in1=t1[:, 1, :]), deps=[a1, a2, tr, ti])
                b1_ = S.op("gpsimd", lambda e: e.tensor_mul(out=t2[:, 0, :], in0=Xs[:, 0, :], in1=ki), deps=[xr] + deps)
                b2_ = S.op("gpsimd", lambda e: e.tensor_mul(out=t2[:, 1, :], in0=Xs[:, 1, :], in1=kr), deps=[xi])
                b3_ = S.op("gpsimd", lambda e: e.tensor_add(out=yi, in0=t2[:, 0, :], in1=t2[:, 1, :]), deps=[b1_, b2_, tr, ti])
                xs_free[xi_] = [a3, b3_]
                return [a3, b3_], xr, xi
            evY = fwd_transform([(FAl, x_tile)], deps, on_group)
            evG = []
            for qp in range(NQ // 2):
                i = get32()
                pg = pool32[i][:].rearrange("p (j n) -> p j n", j=2)
                tt = None
                for j in range(2):
                    q = qp * 2 + j
                    S.op("tensor", lambda e, pg=pg, q=q, j=j: e.matmul(pg[:, j, :], lhsT=At[:, 0, q, :], rhs=FBi[:, 0, :], start=True, stop=False),
                         deps=evY[2 * (q // 4):2 * (q // 4) + 2] + [free32[i]])
                    tt = S.op("tensor", lambda e, pg=pg, q=q, j=j: e.matmul(pg[:, j, :], lhsT=At[:, 1, q, :], rhs=FBi[:, 1, :], start=False, stop=True))
                srcv = pg.rearrange("p j (i r c) -> p j i r c", i=2, r=32)
                dstv = Gt[:, :, :, qp * 8:qp * 8 + 8].rearrange("p i r (j c) -> p j i r c", j=2)
                ev = cp(S, "vector" if qp % 2 == 0 else "scalar", dstv, srcv, deps=[tt] + deps)
                free32[i] = ev
                evG.append(ev)
            outs = []
            for rg in range(n_rg):
                i = get32()
                po = pool32[i][:].rearrange("p (j n) -> p j n", j=4)
                tt = None
                for j in range(4):
                    r = rg * 4 + j
                    S.op("tensor", lambda e, po=po, r=r, j=j: e.matmul(po[:, j, :], lhsT=FAi[:, r, 0, :], rhs=Gt[:, 0, r, :], start=True, stop=False),
                         deps=evG + [free32[i]])
                    tt = S.op("tensor", lambda e, po=po, r=r, j=j: e.matmul(po[:, j, :], lhsT=FAi[:, r, 1, :], rhs=Gt[:, 1, r, :], start=False, stop=True))
                ev, ftok = on_out(rg, tt, po)
                free32[i] = ftok
                outs += ev
            return outs

        blk_done = []
        for blk in range(4):
            tl = [ld(S, "sync", xin[g][:], cvh_v[:, :, g * 512 + blk * CB:g * 512 + (blk + 1) * CB], deps=blk_done) for g in range(2)]
            tl.append(ld(S, "sync", xin[2][:, 0:16, :], cvh_v[:, 0:16, 1024 + blk * CB:1024 + (blk + 1) * CB], deps=blk_done))
            tK0 = make_kernel(blk, 0, blk_done)
            z1 = xin[0]
            yh = xin[1]

            def out1(rg, tt, po):
                o = S.op("vector", lambda e: e.tensor_tensor(out=z1[:, rg * 4:rg * 4 + 4, :], in0=po, in1=xin[1][:, rg * 4:rg * 4 + 4, :],
                                                             op=ALU.mult), deps=[tt] + tK0 + tl)
                return [o], o
            tz1 = conv(xin[0], Kf[0], tl + tK0, out1)
            tK1 = make_kernel(blk, 1, tz1)

            def out2(rg, tt, po):
                o1 = S.op("vector", lambda e: e.tensor_tensor(out=zc[:], in0=po, in1=xin[2][:, rg * 4:rg * 4 + 4, :], op=ALU.mult),
                          deps=[tt] + out2.prev)
                o2 = S.op("scalar", lambda e: e.activation(out=sqc[:], in_=zc[:], func=AF.Square), deps=[o1])
                o3 = S.op("vector", lambda e: e.reduce_sum(out=gss[:], in_=sqc[:].rearrange("p r (g c) -> p r g c", g=2), axis=AX.X), deps=[o2])
                o4 = S.op("scalar", lambda e: e.activation(out=gss[:], in_=gss[:], func=AF.Ln, scale=1.0 / 64, bias=EPS), deps=[o3])
                o5 = S.op("scalar", lambda e: e.activation(out=gss[:], in_=gss[:], func=AF.Exp, scale=-0.5), deps=[o4])
                o6 = S.op("vector", lambda e: e.tensor_tensor(
                    out=zc[:].rearrange("p r (g c) -> p r g c", g=2), in0=zc[:].rearrange("p r (g c) -> p r g c", g=2),
                    in1=gss[:].unsqueeze(3).to_broadcast([128, 4, 2, 64]), op=ALU.mult), deps=[o5])
                o7 = S.op("vector", lambda e, blk=blk: e.tensor_tensor(
                    out=yh[:, rg * 4:rg * 4 + 4, :], in0=zc[:], in1=nwh[:, blk * CB:(blk + 1) * CB].unsqueeze(1).to_broadcast([128, 4, CB]),
                    op=ALU.mult), deps=[o6, t_nw])
                out2.prev = [o7]
                return [o7], o1
            out2.prev = []
            ty = conv(z1, Kf[1], tz1 + tK1, out2, n_rg=4)
            t_st = ld(S, "sync", ytok_v[:, 0:16, 512 + blk * CB:512 + (blk + 1) * CB], yh[:, 0:16, :], deps=ty)
            blk_done = [t_st] + ty
        S.finish()
    if STOP_AFTER == 'P4':
        return

    with contextlib.ExitStack() as st:
        sb = lambda name, shape, dt: st.enter_context(nc.sbuf_tensor("p5_" + name, shape, dt))
        pt = lambda name, shape, dt: st.enter_context(nc.psum_tensor("p5_" + name, shape, dt))
        S = Sched(nc, "p5", SEMS)
        wo_b = sb("wo_b", [128, 8, D], BF16)
        w1_b = sb("w1_b", [128, 8, DFF], BF16)
        w2_b = sb("w2_b", [128, 32, D], BF16)
        stg = [sb(f"stg{i}", [128, 1024], F32) for i in range(2)]
        gb = sb("gb", [128, 3, D], F32)
        ident = sb("ident", [128, 128], BF16)
        ytile = sb("ytile", [128, D], BF16)
        yT = sb("yT", [128, 8, 128], BF16)
        xt = sb("xt", [128, D], F32)
        x1 = sb("x1", [128, D], F32)
        tmp = sb("tmp", [128, D], F32)
        sq = sb("sq", [128, D], F32)
        hmb = sb("hmb", [128, D], BF16)
        hmT = sb("hmT", [128, 8, 128], BF16)
        aT = sb("aT", [128, 32, 128], BF16)
        st_ = sb("stat", [128, 8], F32)
        pmm = [pt(f"pmm{i}", [128, 512], F32) for i in range(4)]
        ptr = [pt(f"ptr{i}", [128, 4, 128], BF16) for i in range(2)]
        t_g = ld(S, "sync", gb[:], I["gains"][1:4].partition_broadcast(128))
        t_id = ld(S, "sync", ident[:], I["ident"][:, :])
        y_view = ytok_d.rearrange("(m r) c -> r m c", r=32)
        x_view = I["xd"].rearrange("(m r) d -> r m d", r=32)
        t_x = ld(S, "sync", xt[:], x_view[0])
        t_yl = ld(S, "sync", ytile[:], y_view[0])
        stg_free = [None, None]
        si = [0]

        w_ready = {"wo": [[]], "w1": [[], [], [], []], "w2": [[]]}

        def emit_unit(name, blk_i, dst, src, kcs, c0):
            for kc in kcs:
                b = si[0] % 2
                si[0] += 1
                t_l = ld(S, "gpsimd" if b else "sync", stg[b][:], src[kc * 128:(kc + 1) * 128, c0:c0 + 1024], deps=[stg_free[b]])
                t_c = cp(S, "vector" if b else "scalar", dst[:, kc, c0:c0 + 1024], stg[b][:], deps=[t_l])
                stg_free[b] = t_c
                w_ready[name][blk_i].append(t_c)
        units = [lambda: emit_unit("wo", 0, wo_b, I["w_out"], range(8), 0)]
        units += [lambda cb=cb: emit_unit("w1", cb, w1_b, I["w1"], range(8), cb * 1024) for cb in range(4)]
        units += [lambda q=q: emit_unit("w2", 0, w2_b, I["w2"], range(q * 8, q * 8 + 8), 0) for q in range(4)]
        pending = list(units)

        def emit_units(n):
            for _ in range(n):
                if pending:
                    pending.pop(0)()
        emit_units(2)

        pmo = [pt(f"pmo{i}", [128, 512], F32) for i in range(2)]
        tmpF = stg[0]
        x1b = [x1, stg[1]]

        def rms_scale(src_ap, col, deps, junk):
            t1_ = S.op("scalar", lambda e: e.activation(out=junk, in_=src_ap, func=AF.Square, accum_out=st_[:, col:col + 1]), deps=deps)
            t2_ = S.op("scalar", lambda e: e.activation(out=st_[:, col:col + 1], in_=st_[:, col:col + 1], func=AF.Ln, scale=1.0 / D, bias=EPS),
                       deps=[t1_])
            return S.op("scalar", lambda e: e.activation(out=st_[:, col:col + 1], in_=st_[:, col:col + 1], func=AF.Exp, scale=-0.5), deps=[t2_])

        pm_free = [None] * 4
        pmo_free = [None, None]
        ptr_free = [None, None]
        T = dict(t_x=t_x, t_yl=t_yl, y_done=None, last=None, mo_done=None)
        fr1, fr2 = {}, {}

        def front1(r):
            tcy = []
            tt = None
            for half in range(2):
                for j in range(4):
                    c = half * 4 + j
                    tt = S.op("tensor", lambda e, c=c, j=j, half=half: e.transpose(
                        out=ptr[half][:, j, :], in_=ytile[:, c * 128:(c + 1) * 128], identity=ident[:]), deps=[T["t_yl"], t_id, ptr_free[half]])
                tc_ = cp(S, "scalar", yT[:, half * 4:half * 4 + 4, :], ptr[half][:], deps=[tt])
                ptr_free[half] = tc_
                tcy.append(tc_)
            T["y_done"] = tt
            ev = []
            for nb in range(2):
                tt = None
                for kc in range(8):
                    tt = S.op("tensor", lambda e, nb=nb, kc=kc: e.matmul(
                        pmo[nb][:], lhsT=yT[:, kc, :], rhs=wo_b[:, kc, nb * 512:(nb + 1) * 512], start=(kc == 0), stop=(kc == 7)),
                        deps=tcy + [pmo_free[nb]] + w_ready["wo"][0])
                ev.append(cp(S, "vector", tmpF[:, nb * 512:(nb + 1) * 512], pmo[nb][:], deps=[tt, stg_free[0]] + fr2.get(r - 1, [])))
                pmo_free[nb] = ev[-1]
            fr1[r] = ev

        def front2(r):
            p = r % 2
            X1 = x1b[p]
            c0 = 4 * p
            t_r = rms_scale(tmpF[:], c0, fr1[r], hmb[:])
            t_a = S.op("vector", lambda e: e.scalar_tensor_tensor(out=tmpF[:], in0=tmpF[:], scalar=st_[:, c0:c0 + 1], in1=gb[:, 0, :],
                                                                  op0=ALU.mult, op1=ALU.mult), deps=[t_r, t_g])
            t_x1 = S.op("vector", lambda e: e.tensor_add(out=X1[:], in0=tmpF[:], in1=xt[:]),
                        deps=[t_a, T["t_x"], stg_free[1]] + fr2.get(("fin", r - 2), []))
            if r + 1 < 16:
                T["t_x"] = ld(S, "sync", xt[:], x_view[r + 1], deps=[t_x1])
                T["t_yl"] = ld(S, "sync", ytile[:], y_view[r + 1], deps=[T["y_done"]])
            t_r2 = rms_scale(X1[:], c0 + 1, [t_x1], hmb[:])
            t_h = S.op("vector", lambda e: e.scalar_tensor_tensor(out=hmb[:], in0=X1[:], scalar=st_[:, c0 + 1:c0 + 2], in1=gb[:, 1, :],
                                                                  op0=ALU.mult, op1=ALU.mult), deps=[t_r2])
            tcp = []
            for half in range(2):
                tt = None
                for j in range(4):
                    c = half * 4 + j
                    tt = S.op("tensor", lambda e, c=c, j=j, half=half: e.transpose(
                        out=ptr[half][:, j, :], in_=hmb[:, c * 128:(c + 1) * 128], identity=ident[:]), deps=[t_h, ptr_free[half]])
                tc_ = cp(S, "scalar", hmT[:, half * 4:half * 4 + 4, :], ptr[half][:], deps=[tt])
                ptr_free[half] = tc_
                tcp.append(tc_)
            fr2[r] = [t_x1]
            stg_free[0] = t_x1
            T[("tcp", r)] = tcp

        def mlp_in(r):
            tcp = T[("tcp", r)]
            t_act = []
            for fg in range(8):
                if r == 0 and fg % 2 == 0:
                    emit_units(1 if fg < 6 else 2)
                pb = 2 + (fg % 2)
                tt = None
                for fj in range(4):
                    f = fg * 4 + fj
                    for kc in range(8):
                        tt = S.op("tensor", lambda e, f=f, fj=fj, kc=kc, pb=pb: e.matmul(
                            pmm[pb][:, fj * 128:(fj + 1) * 128], lhsT=w1_b[:, kc, f * 128:(f + 1) * 128], rhs=hmT[:, kc, :],
                            start=(kc == 0), stop=(kc == 7)), deps=tcp + [pm_free[pb]] + w_ready["w1"][f // 8])
                hs = slice((fg % 2) * 512, (fg % 2 + 1) * 512)
                t_rl = S.op("scalar", lambda e, pb=pb, hs=hs: e.activation(out=sq[:, hs], in_=pmm[pb][:], func=AF.Relu),
                            deps=[tt, T["last"]] + t_act[-2:])
                pm_free[pb] = t_rl
                t_sq = S.op("vector", lambda e, fg=fg, hs=hs: e.tensor_tensor(out=aT[:, fg * 4:fg * 4 + 4, :], in0=sq[:, hs], in1=sq[:, hs],
                                                                               op=ALU.mult), deps=[t_rl])
                t_act.append(t_sq)
            if r == 0:
                emit_units(len(pending))
            T[("act", r)] = t_act

        def mlp_out_mm(r):
            t_act = T[("act", r)]
            tts = []
            for nb in range(2):
                tt = None
                for kc in range(32):
                    tt = S.op("tensor", lambda e, nb=nb, kc=kc: e.matmul(
                        pmm[nb][:], lhsT=aT[:, kc, :], rhs=w2_b[:, kc, nb * 512:(nb + 1) * 512], start=(kc == 0), stop=(kc == 31)),
                        deps=t_act + [pm_free[nb]] + w_ready["w2"][0])
                tts.append(tt)
            T[("mo", r)] = tts

        def final(r):
            p = r % 2
            X1 = x1b[p]
            c0 = 4 * p
            ev = []
            for nb in range(2):
                e_ = cp(S, "vector", tmp[:, nb * 512:(nb + 1) * 512], pmm[nb][:], deps=[T[("mo", r)][nb], T["last"]])
                pm_free[nb] = e_
                ev.append(e_)
            t_r3 = rms_scale(tmp[:], c0 + 2, ev + T[("act", r)], sq[:])
            t_b = S.op("vector", lambda e: e.scalar_tensor_tensor(out=tmp[:], in0=tmp[:], scalar=st_[:, c0 + 2:c0 + 3], in1=gb[:, 2, :],
                                                                  op0=ALU.mult, op1=ALU.mult), deps=[t_r3])
            t_o = S.op("vector", lambda e: e.tensor_add(out=sq[:], in0=tmp[:], in1=X1[:]), deps=[t_b])
            T["last"] = ld(S, "sync", out_d[r], sq[:], deps=[t_o])
            fr2[("fin", r)] = [t_o]

        front1(0)
        front2(0)
        for r in range(16):
            mlp_in(r)
            if r + 1 < 16:
                front1(r + 1)
            mlp_out_mm(r)
            if r + 1 < 16:
                front2(r + 1)
            final(r)
        S.finish()


def _prep_core(inp, b, h, consts):
    rev = (h == 1)
    x = np.asarray(inp["x"][b], dtype=np.float32)
    w_in = np.asarray(inp["w_in"], dtype=np.float32)
    bg = np.asarray(inp["b_gates"], dtype=np.float32)
    conv_w = np.asarray(inp["conv_w"], dtype=np.float32)
    w4 = np.asarray(inp["filt_w4"], dtype=np.float32).reshape(64, 2, 2, 512)
    w4l0 = np.ascontiguousarray(w4[:, 0])
    if rev:
        x = x[::-1]
        w_in = w_in.copy()
        w_in[:, OFF_GATE:OFF_GATE + 8], w_in[:, OFF_GATE + 8:OFF_GATE + 16] = (
            np.asarray(inp["w_in"])[:, OFF_GATE + 8:OFF_GATE + 16], np.asarray(inp["w_in"])[:, OFF_GATE:OFF_GATE + 8])
        bg = np.concatenate([bg[8:], bg[:8]])
        conv_w = conv_w[::-1]
        w4 = w4[:, ::-1]
    conv_b = np.asarray(inp["conv_b"], dtype=np.float32)
    cw_all = np.concatenate([conv_w, conv_b[None]], axis=0)
    d = dict(consts)
    d.update(
        xd=np.ascontiguousarray(x), w_in=np.ascontiguousarray(w_in),
        gains=np.ascontiguousarray(np.stack([inp["norm_mix_pre"], inp["norm_mix_post"], inp["norm_mlp_pre"],
                                             inp["norm_mlp_post"]]).astype(np.float32)),
        bg=np.ascontiguousarray(bg), cw_qk=np.ascontiguousarray(cw_all[:, 0:1024].T), cw_h=np.ascontiguousarray(cw_all[:, 1024:2560]),
        nw_m=np.asarray(inp["mlstm_norm_w"], np.float32), nw_h=np.asarray(inp["hyena_norm_w"], np.float32),
        fw1=np.asarray(inp["filt_w1"], np.float32), fw2=np.asarray(inp["filt_w2"], np.float32),
        fw3=np.asarray(inp["filt_w3"], np.float32), fw4=np.ascontiguousarray(w4.reshape(64, 2048)), w4l0=w4l0,
        fbs=np.ascontiguousarray(np.stack([inp["filt_b1"], inp["filt_b2"], inp["filt_b3"]]).astype(np.float32).T),
        ffq=np.ascontiguousarray(np.asarray(inp["filt_freq"], np.float32).T), fb=np.asarray(inp["filt_bias"], np.float32),
        w_out=np.asarray(inp["w_out"], np.float32), w1=np.asarray(inp["w_mlp_in"], np.float32),
        w2=np.asarray(inp["w_mlp_out"], np.float32))
    return d


_CACHE = {}


def kernel(**inputs):
    if "nc" not in _CACHE:
        _CACHE["consts"] = _tables()
        _CACHE["nc"] = build_nc()
    consts, nc = _CACHE["consts"], _CACHE["nc"]
    maps = [_prep_core(inputs, c // 2, c % 2, consts) for c in range(8)]
    res = run_bass_kernel_spmd(nc, maps, core_ids=list(range(8)))
    out = np.empty((4, SEQ, D), np.float32)
    for c in range(8):
        b, h = c // 2, c % 2
        o = np.asarray(res.results[c]["out_d"], dtype=np.float32)
        tdev = (32 * np.arange(128)[None, :] + np.arange(16)[:, None]).reshape(-1)
        torig = tdev if h == 0 else (SEQ - 1 - tdev)
        out[b, torig] = o.reshape(16 * 128, D)
    return out
```

```python
import contextlib
import math

import ml_dtypes
import numpy as np

import concourse.bass as bass
import concourse.mybir as mybir
from concourse.bass_utils import run_bass_kernel_spmd

F32 = mybir.dt.float32
BF16 = mybir.dt.bfloat16
AF = mybir.ActivationFunctionType
ALU = mybir.AluOpType
AX = mybir.AxisListType
BF = ml_dtypes.bfloat16

D = 1024
SEQ = 4096
NCH = 32
NH = 4
DFF = 4096
N_IN = 3600
OFF_MQ, OFF_MK, OFF_HV, OFF_HX1, OFF_HX2, OFF_MV, OFF_MO, OFF_GATE = 0, 512, 1024, 1536, 2048, 2560, 3072, 3584
EPS = 1e-6
NEG = -30000.0
NF, NF1, NF2, HK = 8192, 256, 32, 128
CB, NQ = 128, 32
TWO_PI = 2 * math.pi
MAGIC = 12582912.0
ENGS = ("sync", "scalar", "vector", "gpsimd", "tensor")
DMA_ENGS = ("sync", "gpsimd")
DEBUG_SCRATCH = False
STOP_AFTER = None
P2_ITERS = NCH_DEFAULT = 32
P2_FLUSH = 1000
P2_POST = True


class Tok:
    __slots__ = ("eng", "idx", "sem", "val", "gen")

    def __init__(self, gen, eng=None, idx=None, sem=None, val=None):
        self.gen, self.eng, self.idx, self.sem, self.val = gen, eng, idx, sem, val


def sem_names(n_dma_sems):
    return [f"c_{e}" for e in ENGS] + [f"d_{e}_{i}" for e in DMA_ENGS for i in range(n_dma_sems)]


class SemSet:
    BUDGET = 96

    def __init__(self, n_dma_sems=4):
        self.n_dma = n_dma_sems
        self.used = 0


class Sched:
    def __init__(self, nc, tag, semset):
        self.nc, self.tag, self.n_dma, self.semset = nc, tag, semset.n_dma, semset
        self.gen = 0
        self.max_seen = 0
        self._reset()

    def _reset(self):
        self.ops = {e: [] for e in ENGS}
        self.dma_i = {e: 0 for e in DMA_ENGS}

    def _live(self, deps):
        return [d for d in deps if d is not None and d.gen == self.gen]

    def op(self, eng, fn, deps=()):
        self.ops[eng].append([fn, self._live(deps), None])
        return Tok(self.gen, eng=eng, idx=len(self.ops[eng]) - 1)

    def dma(self, eng, fn, deps=()):
        i = self.dma_i[eng]
        self.dma_i[eng] += 1
        slot, rnd = i % self.n_dma, i // self.n_dma
        key = f"d_{eng}_{slot}"
        deps = self._live(deps)
        if rnd > 0:
            deps.append(Tok(self.gen, sem=key, val=16 * rnd))
        self.ops[eng].append([fn, deps, key])
        return Tok(self.gen, sem=key, val=16 * (rnd + 1))

    def flush(self):
        if not any(self.ops[e] for e in ENGS):
            return
        nc = self.nc
        waited = {e: set() for e in ENGS}
        for e in ENGS:
            for _, deps, _ in self.ops[e]:
                for d in deps:
                    if d.sem is None:
                        waited[d.eng].add(d.idx)
        value = {}
        for e in ENGS:
            n = 0
            for i in range(len(self.ops[e])):
                if i in waited[e]:
                    n += 1
                    value[(e, i)] = n
        names = sem_names(self.n_dma)
        self.semset.used += len(names)
        assert self.semset.used <= SemSet.BUDGET, f"semaphore ID budget exceeded ({self.semset.used}); IDs would be recycled stale"
        pre = f"{self.tag}g{self.gen}_"
        with contextlib.ExitStack() as st:
            sems = {k: st.enter_context(nc.semaphore(pre + k)) for k in names}
            block = st.enter_context(nc.Block())

            def make(eng):
                def body(e):
                    seen = {}
                    for i, (fn, deps, dkey) in enumerate(self.ops[eng]):
                        for d in deps:
                            k, v = (d.sem, d.val) if d.sem is not None else (f"c_{d.eng}", value[(d.eng, d.idx)])
                            if seen.get(k, 0) < v:
                                e.wait_ge(sems[k], v)
                                seen[k] = v
                        ins = fn(e)
                        if dkey is not None:
                            ins.then_inc(sems[dkey], 16)
                        elif (eng, i) in value:
                            ins.then_inc(sems[f"c_{eng}"], 1)
                    if eng in DMA_ENGS:
                        n = self.dma_i[eng]
                        for slot in range(min(n, self.n_dma)):
                            rounds = (n - slot + self.n_dma - 1) // self.n_dma
                            e.wait_ge(sems[f"d_{eng}_{slot}"], 16 * rounds)
                return body
            for eng in ENGS:
                if self.ops[eng]:
                    getattr(block, eng)(make(eng))
        mx = max([0] + list(value.values()) + [16 * ((self.dma_i[e] + self.n_dma - 1) // self.n_dma) for e in DMA_ENGS])
        self.max_seen = max(self.max_seen, mx)
        self.gen += 1
        self._reset()

    def finish(self):
        self.flush()
        print(f"[build] {self.tag}: {self.gen} block(s), max semaphore value {self.max_seen}", flush=True)


def _tables():
    m = np.arange(NF1)[:, None, None]
    r = np.arange(NF2)[None, :, None]
    k1 = np.arange(HK)[None, None, :]
    th = 2 * np.pi * ((NF2 * m + r) * (2 * k1 + 1) % (2 * NF)) / (2 * NF)
    FA = np.stack([np.cos(th), -np.sin(th)], axis=2)
    thT = np.transpose(th[:128], (2, 1, 0))
    FAi = np.stack([np.cos(thT), -np.sin(thT)], axis=2) * (2.0 / NF)
    rr = np.arange(NF2)[:, None]
    k2 = np.arange(NF2)[None, :]
    tb = 2 * np.pi * ((rr * k2) % NF2) / NF2
    Br, Bi = np.cos(tb), -np.sin(tb)
    bd = lambda a: np.kron(a, np.eye(4))
    FB = np.stack([bd(Br), bd(Bi), bd(-Bi)], axis=1)
    Cr, Ci = bd(Br.T), bd(-Bi.T)
    FBi = np.stack([np.concatenate([Cr, Ci], 1), np.concatenate([-Ci, Cr], 1)], axis=1)
    n = np.arange(NF)
    j = np.where(n < SEQ, n, NF - n).astype(np.float64)
    j[SEQ] = 0
    t = j / (SEQ - 1)
    ang = TWO_PI * j / SEQ
    f = np.linspace(1e-4, 15, 16)
    z = np.concatenate([t[:, None], np.cos(f[None] * ang[:, None]), -np.sin(f[None] * ang[:, None])], axis=1)
    sg = np.where(n < SEQ, 1.0, -1.0)
    sg[SEQ] = 0.0
    deltas = np.abs(np.linspace(math.log(1e-2) / 1.5, math.log(1e-2) / 0.3, 512))
    idx = np.arange(128)
    triF = (idx[:, None] <= idx[None, :]).astype(np.float32)
    triB = (idx[:, None] >= idx[None, :]).astype(np.float32)
    maskF = np.where(idx[:, None] <= idx[None, :], 0.0, NEG).astype(np.float32)
    maskB = np.where(idx[:, None] >= idx[None, :], 0.0, NEG).astype(np.float32)
    return dict(
        FA_lo=FA[:128].astype(BF), FA_hi=FA[128:].astype(BF), FAi=FAi.astype(BF), FB=FB.astype(BF), FBi=FBi.astype(BF),
        zT=np.ascontiguousarray(z.T).astype(np.float32), ntj=(-t.reshape(256, 32)).astype(np.float32),
        sg=sg.reshape(256, 32).astype(np.float32), deltas=deltas.astype(np.float32),
        cst=np.ascontiguousarray(np.stack([triF, triB, maskF, maskB], axis=1)),
        ident=np.eye(128).astype(BF), identf=np.eye(128, dtype=np.float32))


CONST_SPECS = dict(FA_lo=([128, 32, 2, 128], BF16), FA_hi=([128, 32, 2, 128], BF16), FAi=([128, 32, 2, 128], BF16),
                   FB=([128, 3, 128], BF16), FBi=([128, 2, 256], BF16), zT=([33, NF], F32), ntj=([256, 32], F32),
                   sg=([256, 32], F32), deltas=([512], F32), cst=([128, 4, 128], F32), ident=([128, 128], BF16),
                   identf=([128, 128], F32))
PARAM_SPECS = dict(xd=([SEQ, D], F32), w_in=([D, N_IN], F32), gains=([4, D], F32), bg=([16], F32), cw_qk=([1024, 4], F32),
                   cw_h=([4, 1536], F32), nw_m=([512], F32), nw_h=([512], F32), fw1=([33, 64], F32), fw2=([64, 64], F32),
                   fw3=([64, 64], F32), fw4=([64, 2048], F32), w4l0=([64, 2, 512], F32), fbs=([64, 3], F32),
                   ffq=([64, 3], F32), fb=([2, 512], F32), w_out=([D, D], F32), w1=([D, DFF], F32), w2=([DFF, D], F32))


def build_nc():
    nc = bass.Bass("TRN2", target_bir_lowering=False)
    I = {k: nc.dram_tensor(k, sh, dt, kind="ExternalInput").ap() for k, (sh, dt) in {**CONST_SPECS, **PARAM_SPECS}.items()}
    out_d = nc.dram_tensor("out_d", [16, 128, D], F32, kind="ExternalOutput").ap()
    qk_d = nc.dram_tensor("qk_d", [8, 128, SEQ], BF16, kind=("ExternalOutput" if DEBUG_SCRATCH else "Internal")).ap()
    v_d = nc.dram_tensor("v_d", [NCH, 128, 512], BF16, kind=("ExternalOutput" if DEBUG_SCRATCH else "Internal")).ap()
    og_d = nc.dram_tensor("og_d", [NCH, 128, 512], BF16, kind=("ExternalOutput" if DEBUG_SCRATCH else "Internal")).ap()
    gt_d = nc.dram_tensor("gt_d", [128, NCH, 16], F32, kind=("ExternalOutput" if DEBUG_SCRATCH else "Internal")).ap()
    cvh_d = nc.dram_tensor("cvh_d", [128, 32, 1536], BF16, kind=("ExternalOutput" if DEBUG_SCRATCH else "Internal")).ap()
    h3_d = nc.dram_tensor("h3_d", [64, NF], BF16, kind=("ExternalOutput" if DEBUG_SCRATCH else "Internal")).ap()
    ytok_d = nc.dram_tensor("ytok_d", [SEQ, D], BF16, kind=("ExternalOutput" if DEBUG_SCRATCH else "Internal")).ap()

    SEMS = SemSet()
    _build_phases(nc, I, out_d, SEMS, qk_d, v_d, og_d, gt_d, cvh_d, h3_d, ytok_d)
    print(f"[build] semaphore IDs used: {SEMS.used} / budget {SemSet.BUDGET}", flush=True)
    return nc


def _build_phases(nc, I, out_d, SEMS, qk_d, v_d, og_d, gt_d, cvh_d, h3_d, ytok_d):

    def ld(S, eng, out, in_, deps=()):
        return S.dma(eng, lambda e: e.dma_start(out=out, in_=in_), deps=list(deps))

    def cp(S, eng, out, in_, deps=()):
        if eng == "scalar":
            return S.op("scalar", lambda e: e.copy(out=out, in_=in_), deps=list(deps))
        return S.op(eng, lambda e: e.tensor_copy(out=out, in_=in_), deps=list(deps))

    with contextlib.ExitStack() as outer:
        hnT = outer.enter_context(nc.sbuf_tensor("p1_hnT", [128, 8, SEQ + 2], BF16))
        with contextlib.ExitStack() as st:
            sb = lambda name, shape, dt: st.enter_context(nc.sbuf_tensor("p1a_" + name, shape, dt))
            pt = lambda name, shape, dt: st.enter_context(nc.psum_tensor("p1a_" + name, shape, dt))
            S = Sched(nc, "p1a", SEMS)
            xt = [sb(f"xt{i}", [128, D], F32) for i in range(3)]
            xn = [sb(f"xn{i}", [128, D], BF16) for i in range(3)]
            sq = sb("sq", [128, D], F32)
            ss = sb("ss", [128, NCH], F32)
            gb = sb("gb", [128, D], F32)
            ident = sb("ident", [128, 128], BF16)
            wbuf = [sb(f"wbuf{i}", [128, 8, 512], BF16) for i in range(2)]
            wg = sb("wg", [128, 8, 16], BF16)
            stg = [sb(f"stg{i}", [128, 4, 512], F32) for i in range(2)]
            cw = sb("cw", [128, 8, 4], F32)
            bgb = sb("bgb", [128, 16], F32)
            row2 = [sb(f"row{i}", [128, SEQ + 2], F32) for i in range(2)]
            acc = sb("acc", [128, SEQ], F32)
            qko = [sb(f"qko{i}", [128, SEQ], BF16) for i in range(2)]
            vst = [sb(f"vst{i}", [128, 512], BF16) for i in range(2)]
            ost = [sb(f"ost{i}", [128, 512], BF16) for i in range(2)]
            gts = sb("gts", [128, NCH, 16], F32)
            pT = [pt(f"pT{i}", [128, 4, 128], BF16) for i in range(2)]
            pA = [pt(f"pA{i}", [128, 512], F32) for i in range(4)]

            t_g = ld(S, "sync", gb[:], I["gains"][0].partition_broadcast(128))
            t_id = ld(S, "sync", ident[:], I["ident"][:, :])
            t_cw = ld(S, "sync", cw[:], I["cw_qk"].rearrange("(c p) k -> p c k", p=128))
            t_bg = ld(S, "sync", bgb[:], I["bg"].partition_broadcast(128))
            t_hz = S.op("gpsimd", lambda e: e.memset(hnT[:, :, 0:1], 0.0))
            t_hz2 = S.op("gpsimd", lambda e: e.memset(hnT[:, :, SEQ + 1:SEQ + 2], 0.0))
            t_rz2 = [S.op("gpsimd", lambda e, i=i: e.memset(row2[i][:], 0.0)) for i in range(2)]

            stg_free = [None, None]
            si = [0]

            def load_group(dst, col0, ncols, guard):
                toks = []
                for half in range(2):
                    b = si[0] % 2
                    si[0] += 1
                    t_l = ld(S, "gpsimd" if b else "sync", stg[b][:, :, :ncols],
                             I["w_in"][half * 512:(half + 1) * 512, col0:col0 + ncols].rearrange("(k p) n -> p k n", p=128),
                             deps=[stg_free[b]])
                    t_c = cp(S, "scalar" if half else "vector", dst[:, half * 4:half * 4 + 4, :ncols], stg[b][:, :, :ncols],
                             deps=[t_l] + guard)
                    stg_free[b] = t_c
                    toks.append(t_c)
                return toks

            pa_free = [None] * 4
            pa_i = [0]
            w_users = [[], []]
            t_wv = load_group(wbuf[0], OFF_MV, 512, [])
            t_wo = load_group(wbuf[1], OFF_MO, 512, [])
            t_wgl = ld(S, "sync", stg[0][:, :, 0:16], I["w_in"][0:512, OFF_GATE:OFF_GATE + 16].rearrange("(k p) n -> p k n", p=128),
                       deps=[stg_free[0]])
            t_wgl2 = ld(S, "sync", stg[1][:, :, 0:16], I["w_in"][512:1024, OFF_GATE:OFF_GATE + 16].rearrange("(k p) n -> p k n", p=128),
                        deps=[stg_free[1]])
            t_wg = [cp(S, "vector", wg[:, 0:4, :], stg[0][:, :, 0:16], deps=[t_wgl]),
                    cp(S, "vector", wg[:, 4:8, :], stg[1][:, :, 0:16], deps=[t_wgl2])]
            stg_free[0], stg_free[1] = t_wg[0], t_wg[1]
            vo_free = [[None, None], [None, None]]
            g_evs = []

            def emit_vog(c, hn_c):
                for grp, (wb, t_w) in enumerate(((0, t_wv), (1, t_wo))):
                    pb = pa_i[0] % 4
                    pa_i[0] += 1
                    tt = None
                    for kc in range(8):
                        tt = S.op("tensor", lambda e, wb=wb, kc=kc, pb=pb: e.matmul(
                            pA[pb][:], lhsT=hnT[:, kc, 1 + c * 128:1 + (c + 1) * 128], rhs=wbuf[wb][:, kc, :],
                            start=(kc == 0), stop=(kc == 7)), deps=hn_c + t_w + [pa_free[pb]])
                    sbi = c % 2
                    if grp == 0:
                        ev = cp(S, "vector", vst[sbi][:], pA[pb][:], deps=[tt, vo_free[0][sbi]])
                        vo_free[0][sbi] = ld(S, "sync", v_d[c], vst[sbi][:], deps=[ev])
                    else:
                        ev = S.op("scalar", lambda e, sbi=sbi, pb=pb: e.activation(out=ost[sbi][:], in_=pA[pb][:], func=AF.Sigmoid),
                                  deps=[tt, vo_free[1][sbi]])
                        vo_free[1][sbi] = ld(S, "sync", og_d[c], ost[sbi][:], deps=[ev])
                    pa_free[pb] = ev
                    w_users[wb] = [tt]
                pb = pa_i[0] % 4
                pa_i[0] += 1
                tt = None
                for kc in range(8):
                    tt = S.op("tensor", lambda e, kc=kc, pb=pb: e.matmul(
                        pA[pb][:, 0:16], lhsT=hnT[:, kc, 1 + c * 128:1 + (c + 1) * 128], rhs=wg[:, kc, :],
                        start=(kc == 0), stop=(kc == 7)), deps=hn_c + t_wg + [pa_free[pb]])
                ev = S.op("vector", lambda e, pb=pb: e.tensor_add(out=gts[:, c, :], in0=pA[pb][:, 0:16], in1=bgb[:]), deps=[tt, t_bg])
                pa_free[pb] = ev
                g_evs.append(ev)

            NXB = 3
            x_free = [None] * NXB
            xn_free = [None] * NXB
            pT_free = [None, None]
            hn_tok = [None] * NCH
            tk_ld, tk_n = {}, {}
            pi = [0]

            def st_load(i):
                b = i % NXB
                tk_ld[i] = ld(S, "sync", xt[b][:], I["xd"][i * 128:(i + 1) * 128, :], deps=[x_free[b]])

            def st_norm(i):
                b = i % NXB
                t_sq = S.op("scalar", lambda e: e.activation(out=sq[:], in_=xt[b][:], func=AF.Square, accum_out=ss[:, i:i + 1]),
                            deps=[tk_ld[i]])
                t_ln = S.op("scalar", lambda e: e.activation(out=ss[:, i:i + 1], in_=ss[:, i:i + 1], func=AF.Ln, scale=1.0 / D, bias=EPS),
                            deps=[t_sq])
                t_ex = S.op("scalar", lambda e: e.activation(out=ss[:, i:i + 1], in_=ss[:, i:i + 1], func=AF.Exp, scale=-0.5), deps=[t_ln])
                t_n = S.op("vector", lambda e: e.scalar_tensor_tensor(out=xn[b][:], in0=xt[b][:], scalar=ss[:, i:i + 1], in1=gb[:],
                                                                      op0=ALU.mult, op1=ALU.mult), deps=[t_ex, t_g, xn_free[b]])
                x_free[b] = t_n
                tk_n[i] = t_n

            def st_tr(i):
                b = i % NXB
                cps = []
                tt = None
                for half in range(2):
                    pb = pi[0] % 2
                    pi[0] += 1
                    for j in range(4):
                        c = half * 4 + j
                        tt = S.op("tensor", lambda e, c=c, j=j, pb=pb: e.transpose(
                            out=pT[pb][:, j, :], in_=xn[b][:, c * 128:(c + 1) * 128], identity=ident[:]),
                            deps=[tk_n[i], t_id, pT_free[pb]])
                    t_cp = cp(S, "scalar" if half else "vector", hnT[:, half * 4:half * 4 + 4, 1 + i * 128:1 + (i + 1) * 128],
                              pT[pb][:], deps=[tt])
                    pT_free[pb] = t_cp
                    cps.append(t_cp)
                xn_free[b] = tt
                hn_tok[i] = cps

            for k in range(-2, NCH + 1):
                if 0 <= k + 2 < NCH:
                    st_load(k + 2)
                if 0 <= k + 1 < NCH:
                    st_norm(k + 1)
                if 0 <= k < NCH:
                    st_tr(k)
                if 0 <= k - 1 < NCH:
                    emit_vog(k - 1, hn_tok[k - 1] + [t_hz, t_hz2])
            all_hn = [t for c in hn_tok for t in c] + [t_hz, t_hz2]

            qko_free = [None, None]
            row_free = [t_rz2[0], t_rz2[1]]
            st8 = dict(acc_free=None, t_w=None)
            ev8 = {}

            def qk_mm(t):
                grp, ct = t // 4, t % 4
                wb = grp % 2
                if ct == 0:
                    st8["t_w"] = load_group(wbuf[wb], (OFF_MQ, OFF_MK)[grp], 512, w_users[wb])
                    w_users[wb] = []
                t_w = st8["t_w"]
                rb = t % 2
                row = row2[rb]
                evs = []
                tt = None
                for tb in range(8):
                    pb = pa_i[0] % 4
                    pa_i[0] += 1
                    for kc in range(8):
                        tt = S.op("tensor", lambda e, wb=wb, ct=ct, kc=kc, tb=tb, pb=pb: e.matmul(
                            pA[pb][:], lhsT=wbuf[wb][:, kc, ct * 128:(ct + 1) * 128],
                            rhs=hnT[:, kc, 1 + tb * 512:1 + (tb + 1) * 512], start=(kc == 0), stop=(kc == 7)),
                            deps=all_hn + t_w + [pa_free[pb]])
                    ev = cp(S, "scalar", row[:, 1 + tb * 512:1 + (tb + 1) * 512], pA[pb][:], deps=[tt, row_free[rb]])
                    pa_free[pb] = ev
                    evs.append(ev)
                w_users[wb].append(tt)
                ev8[t] = evs

            def qk_conv(t):
                grp = t // 4
                rb = t % 2
                row = row2[rb]
                c1 = S.op("vector", lambda e: e.tensor_scalar(
                    out=acc[:], in0=row[:, 1:SEQ + 1], scalar1=cw[:, t, 1:2], scalar2=cw[:, t, 3:4],
                    op0=ALU.mult, op1=ALU.add), deps=ev8[t] + [t_cw, st8["acc_free"]])
                c2 = S.op("vector", lambda e: e.scalar_tensor_tensor(
                    out=acc[:], in0=row[:, 0:SEQ], scalar=cw[:, t, 0:1], in1=acc[:], op0=ALU.mult, op1=ALU.add), deps=[c1])
                c3 = S.op("vector", lambda e: e.scalar_tensor_tensor(
                    out=acc[:], in0=row[:, 2:SEQ + 2], scalar=cw[:, t, 2:3], in1=acc[:], op0=ALU.mult, op1=ALU.add), deps=[c2])
                row_free[rb] = c3
                ob = t % 2
                if grp == 0:
                    t_s = S.op("scalar", lambda e: e.activation(out=qko[ob][:], in_=acc[:], func=AF.Silu), deps=[c3, qko_free[ob]])
                else:
                    t_s0 = S.op("scalar", lambda e: e.activation(out=acc[:], in_=acc[:], func=AF.Silu), deps=[c3])
                    t_s = S.op("vector", lambda e: e.tensor_scalar(out=qko[ob][:], in0=acc[:], scalar1=128 ** -0.5, scalar2=None,
                                                                   op0=ALU.mult), deps=[t_s0, qko_free[ob]])
                st8["acc_free"] = t_s
                qko_free[ob] = ld(S, "sync", qk_d[t], qko[ob][:], deps=[t_s])

            for t in range(9):
                if t < 8:
                    qk_mm(t)
                if t >= 1:
                    qk_conv(t - 1)

            ld(S, "sync", gt_d[:, :, :], gts[:], deps=g_evs)
            S.finish()
        if STOP_AFTER == 'P1a':
            return

        with contextlib.ExitStack() as st:
            sb = lambda name, shape, dt: st.enter_context(nc.sbuf_tensor("p1b_" + name, shape, dt))
            pt = lambda name, shape, dt: st.enter_context(nc.psum_tensor("p1b_" + name, shape, dt))
            S = Sched(nc, "p1b", SEMS)
            wh = sb("wh", [128, 8, 1536], BF16)
            stg = [sb(f"stgb{i}", [128, 4, 512], F32) for i in range(2)]
            cwb = sb("cwb", [128, 4, 1536], F32)
            pl = sb("pl", [128, 3, 1536], F32)
            tmpc = sb("tmpc", [128, 1536], F32)
            oacc = sb("oacc", [128, 1536], F32)
            cvb = [sb(f"cvb{i}", [128, 1536], BF16) for i in range(2)]
            pP = [[pt(f"pP{i}_{j}", [128, 512], F32) for j in range(3)] for i in range(2)]
            t_cwb = ld(S, "sync", cwb[:], I["cw_h"].partition_broadcast(128))
            stg_free = [None, None]
            t_wh = []
            k_ = 0
            for g3 in range(3):
                for half in range(2):
                    b = k_ % 2
                    k_ += 1
                    t_l = ld(S, "gpsimd" if b else "sync", stg[b][:],
                             I["w_in"][half * 512:(half + 1) * 512, OFF_HV + g3 * 512:OFF_HV + (g3 + 1) * 512].rearrange(
                                 "(k p) n -> p k n", p=128), deps=[stg_free[b]])
                    t_c = cp(S, "vector" if half else "scalar", wh[:, half * 4:half * 4 + 4, g3 * 512:(g3 + 1) * 512], stg[b][:],
                             deps=[t_l])
                    stg_free[b] = t_c
                    t_wh.append(t_c)
            pp_free = [[None] * 3, [None] * 3]
            pl_ready = {}
            pl_free = [None, None, None]
            cvb_free = [None, None]
            OWN_R = 16
            HV_FULL = (slice(0, 1024), slice(1024, 1536))
            HV_TRIM = (slice(0, 640), slice(640, 1024))
            HE = ("vector", "gpsimd")
            for r in range(-1, 34):
                if r <= 32:
                    pb = (r + 1) % 2
                    slot = (r + 1) % 3
                    evs = []
                    for g3 in range(3 if r <= OWN_R else 2):
                        tt = None
                        for kc in range(8):
                            tt = S.op("tensor", lambda e, kc=kc, g3=g3, pb=pb, r=r: e.matmul(
                                pP[pb][g3][:], lhsT=hnT[:, kc, bass.ds(1 + r, 128, step=32)],
                                rhs=wh[:, kc, g3 * 512:(g3 + 1) * 512], start=(kc == 0), stop=(kc == 7)),
                                deps=t_wh + [pp_free[pb][g3]])
                        ev = cp(S, "scalar", pl[:, slot, g3 * 512:(g3 + 1) * 512], pP[pb][g3][:], deps=[tt, pl_free[slot]])
                        pp_free[pb][g3] = ev
                        evs.append(ev)
                    pl_ready[r] = evs
                ro = r - 1
                if 0 <= ro <= 31:
                    a, b_, c_ = (ro) % 3, (ro + 1) % 3, (ro + 2) % 3
                    ob = ro % 2
                    fin = []
                    ncol = 1536 if ro < OWN_R else 1024
                    for hv, he in zip(HV_FULL if ro < OWN_R else HV_TRIM, HE):
                        dps = pl_ready[ro - 1] + pl_ready[ro] + pl_ready[ro + 1] + [t_cwb]
                        o1 = S.op(he, lambda e, hv=hv, b_=b_: e.tensor_tensor(out=oacc[:, hv], in0=pl[:, b_, hv], in1=cwb[:, 1, hv],
                                                                               op=ALU.mult), deps=dps)
                        o2 = S.op(he, lambda e, hv=hv: e.tensor_add(out=oacc[:, hv], in0=oacc[:, hv], in1=cwb[:, 3, hv]), deps=[o1])
                        o3 = S.op(he, lambda e, hv=hv, a=a: e.tensor_tensor(out=tmpc[:, hv], in0=pl[:, a, hv], in1=cwb[:, 0, hv],
                                                                             op=ALU.mult), deps=[o2])
                        o4 = S.op(he, lambda e, hv=hv: e.tensor_add(out=oacc[:, hv], in0=oacc[:, hv], in1=tmpc[:, hv]), deps=[o3])
                        o5 = S.op(he, lambda e, hv=hv, c_=c_: e.tensor_tensor(out=tmpc[:, hv], in0=pl[:, c_, hv], in1=cwb[:, 2, hv],
                                                                               op=ALU.mult), deps=[o4])
                        o6 = S.op(he, lambda e, hv=hv, ob=ob: e.tensor_add(out=cvb[ob][:, hv], in0=oacc[:, hv], in1=tmpc[:, hv]),
                                  deps=[o5, cvb_free[ob]])
                        fin.append(o6)
                    pl_free[a] = None
                    pl_free[a] = fin[0]
                    pl_free_extra = fin[1]
                    mk = S.op("vector", lambda e: e.tensor_copy(out=tmpc[0:1, 0:1], in_=tmpc[0:1, 0:1]), deps=fin)
                    pl_free[a] = mk
                    cvb_free[ob] = ld(S, "sync", cvh_d[:, ro, 0:ncol], cvb[ob][:, 0:ncol], deps=fin)
            S.finish()
    if STOP_AFTER == 'P1':
        return
    with contextlib.ExitStack() as st:
        sb = lambda name, shape, dt: st.enter_context(nc.sbuf_tensor("p2_" + name, shape, dt))
        pt = lambda name, shape, dt: st.enter_context(nc.psum_tensor("p2_" + name, shape, dt))
        S = Sched(nc, "p2", SEMS)
        cst = sb("cst", [128, 4, 128], F32)
        ones_f = sb("ones_f", [128, 128], F32)
        ident = sb("ident", [128, 128], BF16)
        identf = sb("identf", [128, 128], F32)
        nwb = sb("nwb", [128, 512], F32)
        gt = sb("gt", [128, NCH, 16], F32)
        NB = 3
        qkc = [[sb(f"qkc{d}{p}", [128, 8, 128], BF16) for p in range(NB)] for d in range(2)]
        v1c = [[sb(f"v1c{d}{p}", [128, NH, 129], BF16) for p in range(NB)] for d in range(2)]
        ogc = [sb(f"ogc{p}", [128, 512], BF16) for p in range(2)]
        lfB2 = [sb(f"lfB{q}", [128, 8, 128], F32) for q in range(2)]
        lf2 = [sb(f"lf{q}", [128, 8], F32) for q in range(2)]
        bcs2 = [sb(f"bcs{q}", [128, 8], F32) for q in range(2)]
        bias2 = [sb(f"bias{q}", [128, 8], F32) for q in range(2)]
        eb2 = [sb(f"eb{q}", [128, 8], F32) for q in range(2)]
        av2 = [sb(f"av{q}", [128, 8], F32) for q in range(2)]
        eg2 = [sb(f"eg{q}", [128, 8], F32) for q in range(2)]
        WT2 = [sb(f"WT{q}", [128, 8, 128], F32) for q in range(2)]
        SW2 = [sb(f"SW{q}", [128, 8, 128], BF16) for q in range(2)]
        kT_tok2 = [[sb(f"kT_tok{d}{q}", [128, NH, 128], BF16) for q in range(2)] for d in range(2)]
        Va2 = [sb(f"Va{q}", [128, 8, 129], BF16) for q in range(2)]
        Cst = sb("Cst", [128, 8, 129], F32)
        Cbf = sb("Cbf", [128, 8, 129], BF16)
        hacc = sb("hacc", [128, NCH, 512], F32)
        inter = [sb(f"inter{d}", [128, 129], F32) for d in range(2)]
        tot = [sb(f"tot{d}", [128, 129], F32) for d in range(2)]
        rden = [sb(f"rden{d}", [128, 1], F32) for d in range(2)]
        hsum = sb("hsum", [128, 512], F32)
        sqs = sb("sqs", [128, 512], F32)
        gst = sb("gst", [128, 4], F32)
        yt = [sb(f"yt{p}", [128, 512], BF16) for p in range(2)]
        bankA = [pt(f"bankA{d}", [128, NH, 128], F32) for d in range(2)]
        bankB = [pt(f"bankB{d}", [128, 512], F32) for d in range(2)]
        bankC = [pt(f"bankC{d}", [128, NH, 128], BF16) for d in range(2)]
        bankD = [pt(f"bankD{d}", [128, 129], F32) for d in range(2)]
        dg = sb("dg", [128, NH, 128], F32)
        t_cst = ld(S, "sync", cst[:], I["cst"][:, :, :])
        t_id = ld(S, "sync", ident[:], I["ident"][:, :])
        t_idf = ld(S, "sync", identf[:], I["identf"][:, :])
        t_nw = ld(S, "sync", nwb[:], I["nw_m"].partition_broadcast(128))
        t_gt = ld(S, "sync", gt[:], gt_d[:, :, :])
        t_ones = S.op("gpsimd", lambda e: e.memset(ones_f[:], 1.0))
        t_cz = S.op("gpsimd", lambda e: e.memset(Cst[:], 0.0))
        t_cbz = S.op("gpsimd", lambda e: e.memset(Cbf[:], 0.0))
        t_v1 = [[S.op("gpsimd", lambda e, d=d, p=p: e.memset(v1c[d][p][:], 1.0)) for p in range(NB)] for d in range(2)]
        qk_view = qk_d.rearrange("t p s -> p t s")

        buf_free = [[None] * NB, [None] * NB]
        loaded = {}

        def issue_loads(i):
            p = i % NB
            for d, c in ((0, i), (1, NCH - 1 - i)):
                t1 = ld(S, "sync", qkc[d][p][:], qk_view[:, :, c * 128:(c + 1) * 128], deps=[buf_free[d][p]])
                t2 = ld(S, "sync", v1c[d][p][:, :, 0:128], v_d[c].rearrange("p (h e) -> p h e", h=NH),
                        deps=[buf_free[d][p], t_v1[d][p]])
                loaded[(i, d)] = [t1, t2]

        state = dict(lastp=None, post_i=0, og_free=[None, None], yt_free=[None, None])
        done = {}
        ctx = {}

        def front(i, c, d):
            q = i % 2
            p = i % NB
            lf, bcs, bias, eb, av, eg = lf2[q], bcs2[q], bias2[q], eb2[q], av2[q], eg2[q]
            lfB, WT, SW, Va, kT_tok = lfB2[q], WT2[q], SW2[q], Va2[q], kT_tok2[d][q]
            Q, V = qkc[d][p], v1c[d][p]
            tl = loaded[(i, d)]
            old = done.get((i - 2, d), [])
            pc = ctx.get((i - 1, d), {})
            pW = pS = bankA[d]
            pTk = bankC[d]
            J = slice(d * 4, d * 4 + 4)
            gi = slice(d * 8, d * 8 + 4)
            gf = slice(d * 8 + 4, d * 8 + 8)
            a1 = S.op("scalar", lambda e: e.activation(out=lf[:, J], in_=gt[:, c, gf], func=AF.Exp, scale=-1.0), deps=[t_gt] + old)
            a2 = S.op("scalar", lambda e: e.activation(out=lf[:, J], in_=lf[:, J], func=AF.Ln, bias=1.0), deps=[a1])
            a3 = S.op("vector", lambda e: e.tensor_scalar(out=lf[:, J], in0=lf[:, J], scalar1=-1.0, scalar2=None, op0=ALU.mult),
                      deps=[a2])
            tb = [S.op("vector", lambda e: e.tensor_tensor(
                out=lfB[:, J, :], in0=cst[:, d, :].unsqueeze(1).to_broadcast([128, NH, 128]),
                in1=lf[:, J].unsqueeze(2).to_broadcast([128, NH, 128]), op=ALU.mult), deps=[a3, t_cst] + old)]
            yield
            tw = None
            for h in range(NH):
                S.op("tensor", lambda e, h=h: e.matmul(pW[:, h, :], lhsT=ones_f[:], rhs=lfB[:, d * 4 + h, :], start=True, stop=False),
                     deps=tb + [t_ones] + pc.get("tsw", []))
                tw = S.op("tensor", lambda e, h=h: e.matmul(pW[:, h, :], lhsT=identf[:], rhs=cst[:, 2 + d, :], start=False, stop=True),
                          deps=[t_idf])
            tk = None
            for h in range(NH):
                tk = S.op("tensor", lambda e, h=h: e.transpose(out=pTk[:, h, :], in_=Q[:, 4 + h, :], identity=ident[:]),
                          deps=tl + [t_id] + pc.get("tkc", []))
            yield
            b0 = S.op("vector", lambda e: e.tensor_tensor(out=dg[:], in0=pW[:], in1=identf[:].unsqueeze(1).to_broadcast([128, NH, 128]),
                                                          op=ALU.mult), deps=[tw, t_idf] + old + pc.get("b1", []))
            b1 = S.op("vector", lambda e: e.reduce_sum(out=bcs[:, J], in_=dg[:], axis=AX.X), deps=[b0])
            b2 = S.op("vector", lambda e: e.tensor_sub(out=bias[:, J], in0=gt[:, c, gi], in1=bcs[:, J]), deps=[b1])
            b3 = S.op("scalar", lambda e: e.activation(out=eb[:, J], in_=bcs[:, J], func=AF.Exp), deps=[b1] + old)
            tkc = S.op("scalar", lambda e: e.copy(out=kT_tok[:], in_=pTk[:]), deps=[tk] + old)
            gcol = 127 if d == 0 else 0
            tW = []
            for h in range(NH):
                j = d * 4 + h
                tW.append(S.op("scalar", lambda e, h=h, j=j: e.activation(
                    out=WT[:, j, :], in_=pW[:, h, :], func=AF.Exp, bias=bias[:, j:j + 1]), deps=[tw, b2] + old))
            t_eg = S.op("scalar", lambda e: e.activation(out=eg[:, J], in_=pW[:, :, gcol], func=AF.Exp), deps=[tw] + old)
            t_ebi = S.op("scalar", lambda e: e.activation(out=av[:, J], in_=bias[:, J], func=AF.Exp), deps=[b2] + old)
            t_av = S.op("vector", lambda e: e.tensor_mul(out=av[:, J], in0=av[:, J], in1=eg[:, J]), deps=[t_eg, t_ebi])
            yield
            ts_ = None
            for h in range(NH):
                ts_ = S.op("tensor", lambda e, h=h: e.matmul(pS[:, h, :], lhsT=Q[:, 4 + h, :], rhs=Q[:, h, :], start=True, stop=True),
                           deps=tl + [t_eg, b0])
            tsw = [S.op("vector", lambda e, h=h: e.tensor_tensor(out=SW[:, d * 4 + h, :], in0=pS[:, h, :], in1=WT[:, d * 4 + h, :],
                                                                 op=ALU.mult), deps=[ts_, tW[h]] + old) for h in range(NH)]
            u1s = [S.op("scalar", lambda e, h=h: e.activation(out=Va[:, d * 4 + h, :], in_=V[:, h, :], func=AF.Copy,
                                                              scale=av[:, d * 4 + h:d * 4 + h + 1]), deps=[t_av] + old + tl) for h in range(NH)]
            yield
            ctx[(i, d)] = dict(tsw=tsw, b1=[b1], tkc=[tkc], u1s=u1s, b3=b3, t_av=t_av, tl=tl)

        def heads(i, c, d):
            q = i % 2
            p = i % NB
            eb, eg = eb2[q], eg2[q]
            SW, Va, kT_tok = SW2[q], Va2[q], kT_tok2[d][q]
            Q, V = qkc[d][p], v1c[d][p]
            cx = ctx[(i, d)]
            tsw, u1s, tkc, b3, t_av, tl = cx["tsw"], cx["u1s"], cx["tkc"][0], cx["b3"], cx["t_av"], cx["tl"]
            prev_l = done.get((i - 1, d), [])
            pN = bankB[d][:, 0:258].rearrange("p (a b) -> p a b", a=2)
            pC = bankD[d][:]
            first = (d == 0 and c < NCH // 2) or (d == 1 and c >= NCH // 2)
            last_v = None
            tm_last = None
            u4 = None
            for h in range(NH):
                j = d * 4 + h
                S.op("tensor", lambda e, h=h, j=j: e.matmul(pN[:, 0, :], lhsT=SW[:, j, :], rhs=V[:, h, :], start=True, stop=True),
                     deps=[tsw[h], last_v] + tl + prev_l)
                tm = S.op("tensor", lambda e, h=h, j=j: e.matmul(pN[:, 1, :], lhsT=Q[:, h, :], rhs=Cbf[:, j, :], start=True, stop=True),
                          deps=[t_cbz, last_v] + prev_l)
                u2 = S.op("tensor", lambda e, h=h, j=j: e.matmul(pC, lhsT=kT_tok[:, h, :], rhs=Va[:, j, :], start=True, stop=True),
                          deps=[u1s[h], tkc, last_v] + prev_l)
                i1 = S.op("scalar", lambda e, j=j: e.activation(out=inter[d][:], in_=pN[:, 1, :], func=AF.Copy, scale=eb[:, j:j + 1]),
                          deps=[tm, b3, last_v] + prev_l)
                i2 = S.op("vector", lambda e: e.tensor_add(out=tot[d][:], in0=pN[:, 0, :], in1=inter[d][:]), deps=[i1, tm] + prev_l)
                u3 = S.op("vector", lambda e, j=j: e.scalar_tensor_tensor(out=Cst[:, j, :], in0=Cst[:, j, :], scalar=eg[:, j:j + 1],
                                                                          in1=pC, op0=ALU.mult, op1=ALU.add), deps=[u2, t_av, t_cz, tm])
                u4 = S.op("scalar", lambda e, j=j: e.copy(out=Cbf[:, j, :], in_=Cst[:, j, :]), deps=[u3, tm])
                i3a = S.op("vector", lambda e: e.scalar_tensor_tensor(out=rden[d][:], in0=tot[d][:, 128:129], scalar=-1.0,
                                                                      in1=tot[d][:, 128:129], op0=ALU.mult, op1=ALU.max), deps=[i2])
                i3 = S.op("vector", lambda e: e.tensor_scalar_max(out=rden[d][:], in0=rden[d][:], scalar1=1.0), deps=[i3a])
                i4 = S.op("vector", lambda e: e.reciprocal(out=rden[d][:], in_=rden[d][:]), deps=[i3])
                hs = slice(h * 128, (h + 1) * 128)
                if first:
                    i5 = S.op("vector", lambda e, hs=hs: e.tensor_scalar(out=hacc[:, c, hs], in0=tot[d][:, 0:128], scalar1=rden[d][:, 0:1],
                                                                         scalar2=None, op0=ALU.mult), deps=[i4])
                else:
                    i5 = S.op("vector", lambda e, hs=hs: e.scalar_tensor_tensor(out=hacc[:, c, hs], in0=tot[d][:, 0:128], scalar=rden[d][:, 0:1],
                                                                                in1=hacc[:, c, hs], op0=ALU.mult, op1=ALU.add), deps=[i4])
                last_v = i5
                tm_last = u2
                yield
            done[(i, d)] = [last_v, u4]
            buf_free[d][p] = tm_last
            if not first and P2_POST:
                post(c, last_v)

        def post(c, done):
            k = state["post_i"]
            state["post_i"] += 1
            p = k % 2
            t_og = ld(S, "sync", ogc[p][:], og_d[c], deps=[state["og_free"][p]])
            p1 = S.op("vector", lambda e: e.tensor_tensor(out=hsum[:], in0=hacc[:, c, :], in1=ogc[p][:], op=ALU.mult),
                      deps=[done, state["lastp"], t_og])
            state["og_free"][p] = p1
            pl_ = None
            for h in range(NH):
                hs = slice(h * 128, (h + 1) * 128)
                pl_ = S.op("scalar", lambda e, h=h, hs=hs: e.activation(out=sqs[:, hs], in_=hsum[:, hs], func=AF.Square,
                                                                        accum_out=gst[:, h:h + 1]), deps=[p1])
            p3 = S.op("scalar", lambda e: e.activation(out=gst[:], in_=gst[:], func=AF.Ln, scale=1.0 / 128, bias=EPS), deps=[pl_])
            p4 = S.op("scalar", lambda e: e.activation(out=gst[:], in_=gst[:], func=AF.Exp, scale=-0.5), deps=[p3])
            pq = None
            for h in range(NH):
                hs = slice(h * 128, (h + 1) * 128)
                pq = S.op("vector", lambda e, h=h, hs=hs: e.scalar_tensor_tensor(
                    out=yt[p][:, hs], in0=hsum[:, hs], scalar=gst[:, h:h + 1], in1=nwb[:, hs], op0=ALU.mult, op1=ALU.mult),
                    deps=[p4, t_nw, state["yt_free"][p]])
            state["lastp"] = pq
            state["yt_free"][p] = ld(S, "sync", ytok_d[c * 128:(c + 1) * 128, 0:512], yt[p][:], deps=[pq])

        def run_interleaved(gens):
            gens = list(gens)
            while gens:
                for g_ in list(gens):
                    try:
                        next(g_)
                    except StopIteration:
                        gens.remove(g_)

        issue_loads(0)
        issue_loads(1)
        run_interleaved([front(0, 0, 0), front(0, NCH - 1, 1)])
        for i in range(NCH):
            if i + 2 < NCH:
                issue_loads(i + 2)
            gens = [heads(i, i, 0), heads(i, NCH - 1 - i, 1)]
            if i + 1 < NCH:
                gens += [front(i + 1, i + 1, 0), front(i + 1, NCH - 2 - i, 1)]
            run_interleaved(gens)
        S.finish()
    if STOP_AFTER == 'P2':
        return
    with contextlib.ExitStack() as st:
        sb = lambda name, shape, dt: st.enter_context(nc.sbuf_tensor("p3_" + name, shape, dt))
        pt = lambda name, shape, dt: st.enter_context(nc.psum_tensor("p3_" + name, shape, dt))
        S = Sched(nc, "p3", SEMS)
        zT = sb("zT", [33, NF], F32)
        w1s = sb("w1", [33, 64], F32)
        w23 = [sb("w2", [64, 64], BF16), sb("w3", [64, 64], BF16)]
        wst = sb("wst", [64, 2, 64], F32)
        sc = sb("sc", [64, 3], F32)
        bi = sb("bi", [64, 3], F32)
        bsT = sb("bsT", [64, 3], F32)
        hA = sb("hA", [64, NF], BF16)
        hB = sb("hB", [64, NF], BF16)
        NPH = 4
        tmpf = [sb(f"tmpf{i}", [64, 512], F32) for i in range(NPH)]
        tmpu = [sb(f"tmpu{i}", [64, 512], F32) for i in range(NPH)]
        pH = [pt(f"pH{i}", [64, 512], F32) for i in range(NPH)]
        t_z = ld(S, "sync", zT[:], I["zT"][:, :])
        t_w1 = ld(S, "sync", w1s[:], I["fw1"][:, :])
        t_bs = ld(S, "sync", bsT[:], I["fbs"][:, :])
        t_fq = ld(S, "sync", sc[:], I["ffq"][:, :])
        t_l2 = ld(S, "sync", wst[:, 0, :], I["fw2"][:, :])
        t_l3 = ld(S, "sync", wst[:, 1, :], I["fw3"][:, :])
        t_c2 = cp(S, "vector", w23[0][:], wst[:, 0, :], deps=[t_l2])
        t_c3 = cp(S, "vector", w23[1][:], wst[:, 1, :], deps=[t_l3])
        t_bi = S.op("vector", lambda e: e.tensor_tensor(out=bi[:], in0=bsT[:], in1=sc[:], op=ALU.mult), deps=[t_bs, t_fq])
        frH = [None] * NPH
        toks_prev = [t_z, t_w1, t_bi, t_fq, t_c2, t_c3]
        src = None
        for layer in range(3):
            dst = hA if layer % 2 == 0 else hB
            toks = []
            for ch in range(16):
                b = ch % NPH
                cs = slice(ch * 512, (ch + 1) * 512)
                if layer == 0:
                    tt = S.op("tensor", lambda e, b=b, cs=cs: e.matmul(pH[b][:], lhsT=w1s[:], rhs=zT[:, cs], start=True, stop=True),
                              deps=toks_prev + [frH[b]])
                else:
                    tt = S.op("tensor", lambda e, b=b, cs=cs, w=w23[layer - 1], src=src: e.matmul(
                        pH[b][:], lhsT=w[:], rhs=src[:, cs], start=True, stop=True), deps=toks_prev + [frH[b]])
                r1 = S.op("vector", lambda e, b=b, layer=layer: e.tensor_scalar(
                    out=tmpf[b][:], in0=pH[b][:], scalar1=sc[:, layer:layer + 1], scalar2=bi[:, layer:layer + 1],
                    op0=ALU.mult, op1=ALU.add), deps=[tt] + (toks[-NPH:len(toks) - NPH + 1] if len(toks) >= NPH else []))
                frH[b] = r1
                r2a = S.op("vector", lambda e, b=b: e.tensor_scalar(out=tmpu[b][:], in0=tmpf[b][:], scalar1=1.0 / TWO_PI, scalar2=MAGIC,
                                                                    op0=ALU.mult, op1=ALU.add), deps=[r1] + (toks[-NPH:len(toks) - NPH + 1] if len(toks) >= NPH else []))
                r2b = S.op("vector", lambda e, b=b: e.tensor_scalar(out=tmpu[b][:], in0=tmpu[b][:], scalar1=-MAGIC, scalar2=-TWO_PI,
                                                                    op0=ALU.add, op1=ALU.mult), deps=[r2a])
                r2 = S.op("vector", lambda e, b=b: e.tensor_add(out=tmpf[b][:], in0=tmpf[b][:], in1=tmpu[b][:]), deps=[r2b])
                r3 = S.op("scalar", lambda e, b=b, cs=cs, dst=dst: e.activation(out=dst[:, cs], in_=tmpf[b][:], func=AF.Sin,
                                                                                 scale=1.0 - 2e-6), deps=[r2] + toks_prev)
                toks.append(r3)
            toks_prev = toks
            src = dst
        ld(S, "sync", h3_d[:, :], src[:], deps=toks_prev)
        S.finish()
    if STOP_AFTER == 'P3':
        return

    with contextlib.ExitStack() as st:
        sb = lambda name, shape, dt: st.enter_context(nc.sbuf_tensor("p4_" + name, shape, dt))
        pt = lambda name, shape, dt: st.enter_context(nc.psum_tensor("p4_" + name, shape, dt))
        S = Sched(nc, "p4", SEMS)
        FAl = sb("FAl", [128, 32, 2, 128], BF16)
        FAh = sb("FAh", [128, 32, 2, 128], BF16)
        FAi = sb("FAi", [128, 32, 2, 128], BF16)
        FB = sb("FB", [128, 3, 128], BF16)
        FBi = sb("FBi", [128, 2, 256], BF16)
        ident = sb("ident", [128, 128], BF16)
        h3T = sb("h3T", [64, NF], BF16)
        w4b = sb("w4b", [64, 2048], BF16)
        w4l0b = sb("w4l0b", [64, 2, 512], BF16)
        wst = sb("wst", [64, 1024], F32)
        ntj = sb("ntj", [128, 2, 32], F32)
        sgs = sb("sg", [128, 2, 32], F32)
        dlb = sb("dlb", [128, 512], F32)
        fbr = sb("fbr", [1, 2, 512], F32)
        nwh = sb("nwh", [128, 512], F32)
        xin = [sb(f"xin{i}", [128, 32, CB], BF16) for i in range(3)]
        kern = [sb(f"kern{i}", [128, 32, CB], BF16) for i in range(2)]
        A_sb = sb("A_sb", [128, 2, NQ, 32, 4], BF16)
        At = sb("At", [128, 2, NQ, 128], BF16)
        Kf1 = sb("Kf", [128, 2, NQ, 128], BF16)
        Kf = [Kf1, Kf1]
        Xs2 = [sb(f"Xs{i}", [128, 2, 512], F32) for i in range(2)]
        t1_2 = [sb(f"t1_{i}", [128, 2, 512], F32) for i in range(2)]
        t2_2 = [sb(f"t2_{i}", [128, 2, 512], F32) for i in range(2)]
        win16 = [sb(f"win{i}", [128, 4, CB], F32) for i in range(2)]
        l0t = sb("l0t", [1, CB], F32)
        zc = sb("zc", [128, 4, CB], F32)
        sqc = sb("sqc", [128, 4, CB], F32)
        gss = sb("gss", [128, 4, 2], F32)
        pool32 = [pt(f"pb{i}", [128, 512], F32) for i in range(6)]
        pTn = [pt(f"pT{i}", [128, 4, 128], BF16) for i in range(2)]
        free32 = [None] * 6
        free16 = [None] * 2
        rr = dict(b32=0, b16=0, xs=0, win=0)

        def get32():
            i = rr["b32"] % 6
            rr["b32"] += 1
            return i

        def get16():
            k = rr["b16"] % 2
            rr["b16"] += 1
            return k, pTn[k][:]
        Gt = A_sb[:].rearrange("p i q r c -> p (i q r c)").rearrange("p (i r c) -> p i r c", i=2, r=32)

        cl = [ld(S, "sync", FAl[:], I["FA_lo"][:, :, :, :]), ld(S, "gpsimd", FAh[:], I["FA_hi"][:, :, :, :]),
              ld(S, "sync", FAi[:], I["FAi"][:, :, :, :]), ld(S, "sync", FB[:], I["FB"][:, :, :]),
              ld(S, "sync", FBi[:], I["FBi"][:, :, :]), ld(S, "sync", ident[:], I["ident"][:, :]),
              ld(S, "gpsimd", h3T[:], h3_d[:, :])]
        t_ntj = ld(S, "sync", ntj[:], I["ntj"].rearrange("(h p) r -> p h r", p=128))
        t_sg = ld(S, "sync", sgs[:], I["sg"].rearrange("(h p) r -> p h r", p=128))
        t_dl = ld(S, "sync", dlb[:], I["deltas"].partition_broadcast(128))
        t_fb = ld(S, "sync", fbr[:], I["fb"].rearrange("(a o) c -> a o c", a=1))
        t_nw = ld(S, "sync", nwh[:], I["nw_h"].partition_broadcast(128))
        t_l4a = ld(S, "sync", wst[:], I["fw4"][:, 0:1024])
        t_w4a = cp(S, "vector", w4b[:, 0:1024], wst[:], deps=[t_l4a])
        t_l4b = ld(S, "sync", wst[:], I["fw4"][:, 1024:2048], deps=[t_w4a])
        t_w4 = cp(S, "vector", w4b[:, 1024:2048], wst[:], deps=[t_l4b])
        t_l40 = ld(S, "sync", wst[:], I["w4l0"].rearrange("k o c -> k (o c)"), deps=[t_w4])
        t_w40 = cp(S, "vector", w4l0b[:].rearrange("k o c -> k (o c)"), wst[:], deps=[t_l40])
        t_neg = S.op("vector", lambda e: e.tensor_scalar(out=w4b[:, 1024:2048], in0=w4b[:, 1024:2048], scalar1=-1.0, scalar2=None,
                                                         op0=ALU.mult), deps=[t_w4])
        cdeps = cl + [t_w4a, t_w4, t_w40, t_neg]
        xs_free = [[], []]
        win_free = [None] * 2
        ytok_v = ytok_d.rearrange("(m r) c -> m r c", r=32)
        cvh_v = cvh_d

        def fwd_transform(srcs, deps, on_group):
            evA = []
            for rp in range(16):
                i = get32()
                pa = pool32[i][:].rearrange("p (a b c) -> p a b c", a=2, b=2)
                tt = None
                for r2 in range(2):
                    r = rp * 2 + r2
                    for ri in range(2):
                        for si_, (FAt, dat) in enumerate(srcs):
                            tt = S.op("tensor", lambda e, pa=pa, r=r, r2=r2, ri=ri, FAt=FAt, dat=dat, si_=si_: e.matmul(
                                pa[:, r2, ri, :], lhsT=FAt[:, r, ri, :], rhs=dat[:, r, :], start=(si_ == 0),
                                stop=(si_ == len(srcs) - 1)), deps=deps + cdeps + [free32[i]])
                ev = None
                for r2 in range(2):
                    ev = cp(S, "vector" if (rp % 2 == 0 and r2 == 0) else "scalar", A_sb[:, :, :, rp * 2 + r2, :],
                            pa[:, r2, :, :].rearrange("p i (q c) -> p i q c", c=4), deps=[tt, ev] + deps)
                    evA.append(ev)
                free32[i] = ev
            evT = []
            n_ = 0
            for ri in range(2):
                for qg in range(NQ // 4):
                    k, ptile = get16()
                    tt = None
                    for j in range(4):
                        q = qg * 4 + j
                        tt = S.op("tensor", lambda e, ptile=ptile, ri=ri, q=q, j=j: e.transpose(
                            out=ptile[:, j, :], in_=A_sb[:, ri, q, :, :].rearrange("p r c -> p (r c)"),
                            identity=ident[:]), deps=evA + [free16[k]])
                    ev = cp(S, "vector" if n_ % 4 == 0 else "scalar", At[:, ri, qg * 4:qg * 4 + 4, :], ptile, deps=[tt] + deps)
                    n_ += 1
                    free16[k] = ev
                    evT.append(ev)
            outs = []
            for qg in range(NQ // 4):
                qs = slice(qg * 4, qg * 4 + 4)
                rhs_r = At[:, 0, qs, :].rearrange("p q k -> p (q k)")
                rhs_i = At[:, 1, qs, :].rearrange("p q k -> p (q k)")
                i0, i1 = get32(), get32()
                b0, b1 = pool32[i0], pool32[i1]
                S.op("tensor", lambda e, rhs_r=rhs_r, b0=b0: e.matmul(b0[:], lhsT=FB[:, 0, :], rhs=rhs_r, start=True, stop=False),
                     deps=[evT[qg], evT[NQ // 4 + qg], free32[i0]])
                tr = S.op("tensor", lambda e, rhs_i=rhs_i, b0=b0: e.matmul(b0[:], lhsT=FB[:, 2, :], rhs=rhs_i, start=False, stop=True))
                S.op("tensor", lambda e, rhs_r=rhs_r, b1=b1: e.matmul(b1[:], lhsT=FB[:, 1, :], rhs=rhs_r, start=True, stop=False),
                     deps=[free32[i1]])
                ti = S.op("tensor", lambda e, rhs_i=rhs_i, b1=b1: e.matmul(b1[:], lhsT=FB[:, 0, :], rhs=rhs_i, start=False, stop=True))
                o_, f0, f1 = on_group(qg, qs, b0, b1, tr, ti)
                free32[i0], free32[i1] = f0, f1
                outs += o_
            return outs

        def make_kernel(blk, order, deps):
            tk_all = []
            for half in range(2):
                col0 = half * 1024 + order * 512 + blk * CB
                for r4 in range(8):
                    i = get32()
                    pk4 = pool32[i][:].rearrange("p (j c) -> p j c", j=4)
                    wi = rr["win"] % 2
                    rr["win"] += 1
                    tt = None
                    tws = []
                    for jj in range(4):
                        r = r4 * 4 + jj
                        tt = S.op("tensor", lambda e, pk4=pk4, jj=jj, half=half, r=r, col0=col0: e.matmul(
                            pk4[:, jj, :], lhsT=h3T[:, bass.ds(half * 4096 + r, 128, step=32)], rhs=w4b[:, col0:col0 + CB],
                            start=True, stop=True), deps=cdeps + deps + [free32[i]])
                        tws.append(S.op("scalar", lambda e, half=half, r=r, wi=wi, jj=jj: e.activation(
                            out=win16[wi][:, jj, :], in_=dlb[:, blk * CB:(blk + 1) * CB], func=AF.Exp, scale=ntj[:, half, r:r + 1]),
                            deps=[t_dl, t_ntj, win_free[wi]]))
                    tk = S.op("vector", lambda e, pk4=pk4, half=half, r4=r4, wi=wi: e.tensor_tensor(
                        out=kern[half][:, r4 * 4:r4 * 4 + 4, :], in0=pk4, in1=win16[wi][:], op=ALU.mult), deps=[tt] + tws + deps)
                    free32[i] = tk
                    win_free[wi] = tk
                    tk_all.append(tk)
            tz0 = S.op("vector", lambda e: e.memset(kern[1][0:1, 0, :], 0.0), deps=tk_all)
            tk_all.append(tz0)
            i = get32()
            p0 = pool32[i][0:1, 0:CB]
            t0 = S.op("tensor", lambda e: e.matmul(p0, lhsT=h3T[:, 0:1], rhs=w4l0b[:, order, blk * CB:(blk + 1) * CB],
                                                   start=True, stop=True), deps=cdeps + [free32[i]])
            t0a = S.op("vector", lambda e: e.tensor_add(out=l0t[:], in0=p0, in1=fbr[0:1, order, blk * CB:(blk + 1) * CB]),
                       deps=[t0, t_fb] + tk_all[-1:])
            free32[i] = t0a
            t0b = S.op("vector", lambda e: e.tensor_copy(out=kern[0][0:1, 0, :], in_=l0t[:]), deps=[t0a] + tk_all)

            def on_group(qg, qs, b0, b1, tr, ti):
                a = cp(S, "scalar", Kf[order][:, 0, qs, :].rearrange("p q k -> p (q k)"), b0[:], deps=[tr] + deps)
                b_ = cp(S, "vector", Kf[order][:, 1, qs, :].rearrange("p q k -> p (q k)"), b1[:], deps=[ti] + deps)
                return [a, b_], a, b_
            return fwd_transform([(FAl, kern[0]), (FAh, kern[1])], tk_all + [t0b], on_group)

        def conv(x_tile, Kft, deps, on_out, n_rg=8):
            def on_group(qg, qs, b0, b1, tr, ti):
                xi_ = rr["xs"] % 2
                rr["xs"] += 1
                Xs, t1, t2 = Xs2[xi_], t1_2[xi_], t2_2[xi_]
                xr = cp(S, "scalar", Xs[:, 0, :], b0[:], deps=[tr] + xs_free[xi_])
                xi = cp(S, "scalar", Xs[:, 1, :], b1[:], deps=[ti])
                kr = Kft[:, 0, qs, :].rearrange("p q k -> p (q k)")
                ki = Kft[:, 1, qs, :].rearrange("p q k -> p (q k)")
                yr = At[:, 0, qs, :].rearrange("p q k -> p (q k)")
                yi = At[:, 1, qs, :].rearrange("p q k -> p (q k)")
                a1 = S.op("vector", lambda e: e.tensor_mul(out=t1[:, 0, :], in0=Xs[:, 0, :], in1=kr), deps=[xr] + deps)
                a2 = S.op("vector", lambda e: e.tensor_mul(out=t1[:, 1, :], in0=Xs[:, 1, :], in1=ki), deps=[xi])
                a3 = S.op("vector", lambda e: e.tensor_sub(out=yr, in0=t1[:, 0, :], in1=t1[:, 1, :]), deps=[a1, a2, tr, ti])
                b1_ = S.op("gpsimd", lambda e: e.tensor_mul(out=t2[:, 0, :], in0=Xs[:, 0, :], in1=ki), deps=[xr] + deps)
                b2_ = S.op("gpsimd", lambda e: e.tensor_mul(out=t2[:, 1, :], in0=Xs[:, 1, :], in1=kr), deps=[xi])
                b3_ = S.op("gpsimd", lambda e: e.tensor_add(out=yi, in0=t2[:, 0, :], in1=t2[:, 1, :]), deps=[b1_, b2_, tr, ti])
                xs_free[xi_] = [a3, b3_]
                return [a3, b3_], xr, xi
            evY = fwd_transform([(FAl, x_tile)], deps, on_group)
            evG = []
            for qp in range(NQ // 2):
                i = get32()
                pg = pool32[i][:].rearrange("p (j n) -> p j n", j=2)
                tt = None
                for j in range(2):
                    q = qp * 2 + j
                    S.op("tensor", lambda e, pg=pg, q=q, j=j: e.matmul(pg[:, j, :], lhsT=At[:, 0, q, :], rhs=FBi[:, 0, :], start=True, stop=False),
                         deps=evY[2 * (q // 4):2 * (q // 4) + 2] + [free32[i]])
                    tt = S.op("tensor", lambda e, pg=pg, q=q, j=j: e.matmul(pg[:, j, :], lhsT=At[:, 1, q, :], rhs=FBi[:, 1, :], start=False, stop=True))
                nr = n_rg * 4
                srcv = pg.rearrange("p j (i r c) -> p j i r c", i=2, r=32)[:, :, :, 0:nr, :]
                dstv = Gt[:, :, 0:nr, qp * 8:qp * 8 + 8].rearrange("p i r (j c) -> p j i r c", j=2)
                ev = cp(S, "vector" if qp % 4 == 0 else "scalar", dstv, srcv, deps=[tt] + deps)
                free32[i] = ev
                evG.append(ev)
            outs = []
            for rg in range(n_rg):
                i = get32()
                po = pool32[i][:].rearrange("p (j n) -> p j n", j=4)
                tt = None
                for j in range(4):
                    r = rg * 4 + j
                    S.op("tensor", lambda e, po=po, r=r, j=j: e.matmul(po[:, j, :], lhsT=FAi[:, r, 0, :], rhs=Gt[:, 0, r, :], start=True, stop=False),
                         deps=evG + [free32[i]])
                    tt = S.op("tensor", lambda e, po=po, r=r, j=j: e.matmul(po[:, j, :], lhsT=FAi[:, r, 1, :], rhs=Gt[:, 1, r, :], start=False, stop=True))
                ev, ftok = on_out(rg, tt, po)
                free32[i] = ftok
                outs += ev
            return outs

        blk_done = []
        for blk in range(4):
            tl = [ld(S, "sync", xin[g][:], cvh_v[:, :, g * 512 + blk * CB:g * 512 + (blk + 1) * CB], deps=blk_done) for g in range(2)]
            tl.append(ld(S, "sync", xin[2][:, 0:16, :], cvh_v[:, 0:16, 1024 + blk * CB:1024 + (blk + 1) * CB], deps=blk_done))
            tK0 = make_kernel(blk, 0, blk_done)
            z1 = xin[0]
            yh = xin[1]

            def out1(rg, tt, po):
                o = S.op("vector", lambda e: e.tensor_tensor(out=z1[:, rg * 4:rg * 4 + 4, :], in0=po, in1=xin[1][:, rg * 4:rg * 4 + 4, :],
                                                             op=ALU.mult), deps=[tt] + tK0 + tl)
                return [o], o
            tz1 = conv(xin[0], Kf[0], tl + tK0, out1)
            tK1 = make_kernel(blk, 1, tz1)

            def out2(rg, tt, po):
                o1 = S.op("vector", lambda e: e.tensor_tensor(out=zc[:], in0=po, in1=xin[2][:, rg * 4:rg * 4 + 4, :], op=ALU.mult),
                          deps=[tt] + out2.prev)
                o2 = S.op("scalar", lambda e: e.activation(out=sqc[:], in_=zc[:], func=AF.Square), deps=[o1])
                o3 = S.op("vector", lambda e: e.reduce_sum(out=gss[:], in_=sqc[:].rearrange("p r (g c) -> p r g c", g=2), axis=AX.X), deps=[o2])
                o4 = S.op("scalar", lambda e: e.activation(out=gss[:], in_=gss[:], func=AF.Ln, scale=1.0 / 64, bias=EPS), deps=[o3])
                o5 = S.op("scalar", lambda e: e.activation(out=gss[:], in_=gss[:], func=AF.Exp, scale=-0.5), deps=[o4])
                o6 = S.op("vector", lambda e: e.tensor_tensor(
                    out=zc[:].rearrange("p r (g c) -> p r g c", g=2), in0=zc[:].rearrange("p r (g c) -> p r g c", g=2),
                    in1=gss[:].unsqueeze(3).to_broadcast([128, 4, 2, 64]), op=ALU.mult), deps=[o5])
                o7 = S.op("vector", lambda e, blk=blk: e.tensor_tensor(
                    out=yh[:, rg * 4:rg * 4 + 4, :], in0=zc[:], in1=nwh[:, blk * CB:(blk + 1) * CB].unsqueeze(1).to_broadcast([128, 4, CB]),
                    op=ALU.mult), deps=[o6, t_nw])
                out2.prev = [o7]
                return [o7], o1
            out2.prev = []
            ty = conv(z1, Kf[1], tz1 + tK1, out2, n_rg=4)
            t_st = ld(S, "sync", ytok_v[:, 0:16, 512 + blk * CB:512 + (blk + 1) * CB], yh[:, 0:16, :], deps=ty)
            blk_done = [t_st] + ty
        S.finish()
    if STOP_AFTER == 'P4':
        return

    with contextlib.ExitStack() as st:
        sb = lambda name, shape, dt: st.enter_context(nc.sbuf_tensor("p5_" + name, shape, dt))
        pt = lambda name, shape, dt: st.enter_context(nc.psum_tensor("p5_" + name, shape, dt))
        S = Sched(nc, "p5", SEMS)
        wo_b = sb("wo_b", [128, 8, D], BF16)
        w1_b = sb("w1_b", [128, 8, DFF], BF16)
        w2_b = sb("w2_b", [128, 32, D], BF16)
        stg = [sb(f"stg{i}", [128, 1024], F32) for i in range(2)]
        gb = sb("gb", [128, 3, D], F32)
        ident = sb("ident", [128, 128], BF16)
        ytile = sb("ytile", [128, D], BF16)
        yT = sb("yT", [128, 8, 128], BF16)
        xt = sb("xt", [128, D], F32)
        x1 = sb("x1", [128, D], F32)
        tmp = sb("tmp", [128, D], F32)
        sq = sb("sq", [128, D], F32)
        hmb = sb("hmb", [128, D], BF16)
        hmT = sb("hmT", [128, 8, 128], BF16)
        aT = sb("aT", [128, 32, 128], BF16)
        st_ = sb("stat", [128, 8], F32)
        pmm = [pt(f"pmm{i}", [128, 512], F32) for i in range(4)]
        ptr = [pt(f"ptr{i}", [128, 4, 128], BF16) for i in range(2)]
        t_g = ld(S, "sync", gb[:], I["gains"][1:4].partition_broadcast(128))
        t_id = ld(S, "sync", ident[:], I["ident"][:, :])
        y_view = ytok_d.rearrange("(m r) c -> r m c", r=32)
        x_view = I["xd"].rearrange("(m r) d -> r m d", r=32)
        t_x = ld(S, "sync", xt[:], x_view[0])
        t_yl = ld(S, "sync", ytile[:], y_view[0])
        stg_free = [None, None]
        si = [0]

        w_ready = {"wo": [[]], "w1": [[], [], [], []], "w2": [[]]}

        def emit_unit(name, blk_i, dst, src, kcs, c0):
            for kc in kcs:
                b = si[0] % 2
                si[0] += 1
                t_l = ld(S, "gpsimd" if b else "sync", stg[b][:], src[kc * 128:(kc + 1) * 128, c0:c0 + 1024], deps=[stg_free[b]])
                t_c = cp(S, "vector" if b else "scalar", dst[:, kc, c0:c0 + 1024], stg[b][:], deps=[t_l])
                stg_free[b] = t_c
                w_ready[name][blk_i].append(t_c)
        units = [lambda: emit_unit("wo", 0, wo_b, I["w_out"], range(8), 0)]
        units += [lambda cb=cb: emit_unit("w1", cb, w1_b, I["w1"], range(8), cb * 1024) for cb in range(4)]
        units += [lambda q=q: emit_unit("w2", 0, w2_b, I["w2"], range(q * 8, q * 8 + 8), 0) for q in range(4)]
        pending = list(units)

        def emit_units(n):
            for _ in range(n):
                if pending:
                    pending.pop(0)()
        emit_units(2)

        pmo = [pt(f"pmo{i}", [128, 512], F32) for i in range(2)]
        tmpF = stg[0]
        x1b = [x1, stg[1]]

        def rms_scale(src_ap, col, deps, junk):
            t1_ = S.op("scalar", lambda e: e.activation(out=junk, in_=src_ap, func=AF.Square, accum_out=st_[:, col:col + 1]), deps=deps)
            t2_ = S.op("scalar", lambda e: e.activation(out=st_[:, col:col + 1], in_=st_[:, col:col + 1], func=AF.Ln, scale=1.0 / D, bias=EPS),
                       deps=[t1_])
            return S.op("scalar", lambda e: e.activation(out=st_[:, col:col + 1], in_=st_[:, col:col + 1], func=AF.Exp, scale=-0.5), deps=[t2_])

        pm_free = [None] * 4
        pmo_free = [None, None]
        ptr_free = [None, None]
        T = dict(t_x=t_x, t_yl=t_yl, y_done=None, last=None, mo_done=None)
        fr1, fr2 = {}, {}

        def front1(r):
            tcy = []
            tt = None
            for half in range(2):
                for j in range(4):
                    c = half * 4 + j
                    tt = S.op("tensor", lambda e, c=c, j=j, half=half: e.transpose(
                        out=ptr[half][:, j, :], in_=ytile[:, c * 128:(c + 1) * 128], identity=ident[:]), deps=[T["t_yl"], t_id, ptr_free[half]])
                tc_ = cp(S, "scalar", yT[:, half * 4:half * 4 + 4, :], ptr[half][:], deps=[tt])
                ptr_free[half] = tc_
                tcy.append(tc_)
            T["y_done"] = tt
            ev = []
            for nb in range(2):
                tt = None
                for kc in range(8):
                    tt = S.op("tensor", lambda e, nb=nb, kc=kc: e.matmul(
                        pmo[nb][:], lhsT=yT[:, kc, :], rhs=wo_b[:, kc, nb * 512:(nb + 1) * 512], start=(kc == 0), stop=(kc == 7)),
                        deps=tcy + [pmo_free[nb]] + w_ready["wo"][0])
                ev.append(cp(S, "vector", tmpF[:, nb * 512:(nb + 1) * 512], pmo[nb][:], deps=[tt, stg_free[0]] + fr2.get(r - 1, [])))
                pmo_free[nb] = ev[-1]
            fr1[r] = ev

        def front2(r):
            p = r % 2
            X1 = x1b[p]
            c0 = 4 * p
            t_r = rms_scale(tmpF[:], c0, fr1[r], hmb[:])
            t_a = S.op("vector", lambda e: e.scalar_tensor_tensor(out=tmpF[:], in0=tmpF[:], scalar=st_[:, c0:c0 + 1], in1=gb[:, 0, :],
                                                                  op0=ALU.mult, op1=ALU.mult), deps=[t_r, t_g])
            t_x1 = S.op("vector", lambda e: e.tensor_add(out=X1[:], in0=tmpF[:], in1=xt[:]),
                        deps=[t_a, T["t_x"], stg_free[1]] + fr2.get(("fin", r - 2), []))
            if r + 1 < 16:
                T["t_x"] = ld(S, "sync", xt[:], x_view[r + 1], deps=[t_x1])
                T["t_yl"] = ld(S, "sync", ytile[:], y_view[r + 1], deps=[T["y_done"]])
            t_r2 = rms_scale(X1[:], c0 + 1, [t_x1], hmb[:])
            t_h = S.op("vector", lambda e: e.scalar_tensor_tensor(out=hmb[:], in0=X1[:], scalar=st_[:, c0 + 1:c0 + 2], in1=gb[:, 1, :],
                                                                  op0=ALU.mult, op1=ALU.mult), deps=[t_r2])
            tcp = []
            for half in range(2):
                tt = None
                for j in range(4):
                    c = half * 4 + j
                    tt = S.op("tensor", lambda e, c=c, j=j, half=half: e.transpose(
                        out=ptr[half][:, j, :], in_=hmb[:, c * 128:(c + 1) * 128], identity=ident[:]), deps=[t_h, ptr_free[half]])
                tc_ = cp(S, "scalar", hmT[:, half * 4:half * 4 + 4, :], ptr[half][:], deps=[tt])
                ptr_free[half] = tc_
                tcp.append(tc_)
            fr2[r] = [t_x1]
            stg_free[0] = t_x1
            T[("tcp", r)] = tcp

        def mlp_in(r):
            tcp = T[("tcp", r)]
            t_act = []
            for fg in range(8):
                if r == 0 and fg % 2 == 0:
                    emit_units(1 if fg < 6 else 2)
                pb = 2 + (fg % 2)
                tt = None
                for fj in range(4):
                    f = fg * 4 + fj
                    for kc in range(8):
                        tt = S.op("tensor", lambda e, f=f, fj=fj, kc=kc, pb=pb: e.matmul(
                            pmm[pb][:, fj * 128:(fj + 1) * 128], lhsT=w1_b[:, kc, f * 128:(f + 1) * 128], rhs=hmT[:, kc, :],
                            start=(kc == 0), stop=(kc == 7)), deps=tcp + [pm_free[pb]] + w_ready["w1"][f // 8])
                hs = slice((fg % 2) * 512, (fg % 2 + 1) * 512)
                t_rl = S.op("scalar", lambda e, pb=pb, hs=hs: e.activation(out=sq[:, hs], in_=pmm[pb][:], func=AF.Relu),
                            deps=[tt, T["last"]] + t_act[-2:])
                pm_free[pb] = t_rl
                t_sq = S.op("vector", lambda e, fg=fg, hs=hs: e.tensor_tensor(out=aT[:, fg * 4:fg * 4 + 4, :], in0=sq[:, hs], in1=sq[:, hs],
                                                                               op=ALU.mult), deps=[t_rl])
                t_act.append(t_sq)
            if r == 0:
                emit_units(len(pending))
            T[("act", r)] = t_act

        def mlp_out_mm(r):
            t_act = T[("act", r)]
            tts = []
            for nb in range(2):
                tt = None
                for kc in range(32):
                    tt = S.op("tensor", lambda e, nb=nb, kc=kc: e.matmul(
                        pmm[nb][:], lhsT=aT[:, kc, :], rhs=w2_b[:, kc, nb * 512:(nb + 1) * 512], start=(kc == 0), stop=(kc == 31)),
                        deps=t_act + [pm_free[nb]] + w_ready["w2"][0])
                tts.append(tt)
            T[("mo", r)] = tts

        def final(r):
            p = r % 2
            X1 = x1b[p]
            c0 = 4 * p
            ev = []
            for nb in range(2):
                e_ = cp(S, "vector", tmp[:, nb * 512:(nb + 1) * 512], pmm[nb][:], deps=[T[("mo", r)][nb], T["last"]])
                pm_free[nb] = e_
                ev.append(e_)
            t_r3 = rms_scale(tmp[:], c0 + 2, ev + T[("act", r)], sq[:])
            t_b = S.op("vector", lambda e: e.scalar_tensor_tensor(out=tmp[:], in0=tmp[:], scalar=st_[:, c0 + 2:c0 + 3], in1=gb[:, 2, :],
                                                                  op0=ALU.mult, op1=ALU.mult), deps=[t_r3])
            t_o = S.op("vector", lambda e: e.tensor_add(out=sq[:], in0=tmp[:], in1=X1[:]), deps=[t_b])
            T["last"] = ld(S, "sync", out_d[r], sq[:], deps=[t_o])
            fr2[("fin", r)] = [t_o]

        front1(0)
        front2(0)
        for r in range(16):
            mlp_in(r)
            if r + 1 < 16:
                front1(r + 1)
            mlp_out_mm(r)
            if r + 1 < 16:
                front2(r + 1)
            final(r)
        S.finish()


def _prep_core(inp, b, h, consts):
    rev = (h == 1)
    x = np.asarray(inp["x"][b], dtype=np.float32)
    w_in = np.asarray(inp["w_in"], dtype=np.float32)
    bg = np.asarray(inp["b_gates"], dtype=np.float32)
    conv_w = np.asarray(inp["conv_w"], dtype=np.float32)
    w4 = np.asarray(inp["filt_w4"], dtype=np.float32).reshape(64, 2, 2, 512)
    w4l0 = np.ascontiguousarray(w4[:, 0])
    if rev:
        x = x[::-1]
        w_in = w_in.copy()
        w_in[:, OFF_GATE:OFF_GATE + 8], w_in[:, OFF_GATE + 8:OFF_GATE + 16] = (
            np.asarray(inp["w_in"])[:, OFF_GATE + 8:OFF_GATE + 16], np.asarray(inp["w_in"])[:, OFF_GATE:OFF_GATE + 8])
        bg = np.concatenate([bg[8:], bg[:8]])
        conv_w = conv_w[::-1]
        w4 = w4[:, ::-1]
    conv_b = np.asarray(inp["conv_b"], dtype=np.float32)
    cw_all = np.concatenate([conv_w, conv_b[None]], axis=0)
    d = dict(consts)
    d.update(
        xd=np.ascontiguousarray(x), w_in=np.ascontiguousarray(w_in),
        gains=np.ascontiguousarray(np.stack([inp["norm_mix_pre"], inp["norm_mix_post"], inp["norm_mlp_pre"],
                                             inp["norm_mlp_post"]]).astype(np.float32)),
        bg=np.ascontiguousarray(bg), cw_qk=np.ascontiguousarray(cw_all[:, 0:1024].T), cw_h=np.ascontiguousarray(cw_all[:, 1024:2560]),
        nw_m=np.asarray(inp["mlstm_norm_w"], np.float32), nw_h=np.asarray(inp["hyena_norm_w"], np.float32),
        fw1=np.asarray(inp["filt_w1"], np.float32), fw2=np.asarray(inp["filt_w2"], np.float32),
        fw3=np.asarray(inp["filt_w3"], np.float32), fw4=np.ascontiguousarray(w4.reshape(64, 2048)), w4l0=w4l0,
        fbs=np.ascontiguousarray(np.stack([inp["filt_b1"], inp["filt_b2"], inp["filt_b3"]]).astype(np.float32).T),
        ffq=np.ascontiguousarray(np.asarray(inp["filt_freq"], np.float32).T), fb=np.asarray(inp["filt_bias"], np.float32),
        w_out=np.asarray(inp["w_out"], np.float32), w1=np.asarray(inp["w_mlp_in"], np.float32),
        w2=np.asarray(inp["w_mlp_out"], np.float32))
    return d


_CACHE = {}


def kernel(**inputs):
    if "nc" not in _CACHE:
        _CACHE["consts"] = _tables()
        _CACHE["nc"] = build_nc()
    consts, nc = _CACHE["consts"], _CACHE["nc"]
    maps = [_prep_core(inputs, c // 2, c % 2, consts) for c in range(8)]
    res = run_bass_kernel_spmd(nc, maps, core_ids=list(range(8)))
    out = np.empty((4, SEQ, D), np.float32)
    for c in range(8):
        b, h = c // 2, c % 2
        o = np.asarray(res.results[c]["out_d"], dtype=np.float32)
        tdev = (32 * np.arange(128)[None, :] + np.arange(16)[:, None]).reshape(-1)
        torig = tdev if h == 0 else (SEQ - 1 - tdev)
        out[b, torig] = o.reshape(16 * 128, D)
    return out
```
